# Optimizing a Trainium2 kernel written in Bass

```python
import math
import jax, jax.numpy as jnp
from jax import lax
import numpy as np

D_MODEL = 2048
BATCH = 4
SEQ = 2048
DEPTH = 1
DEC_BATCH = 128
DEC_SEQ = 4
PAST_LEN = 2048
PAGE_SIZE = 128

SB_HEADS = 8
SB_HEAD_DIM = 128
SB_WIDTH = SB_HEADS * SB_HEAD_DIM
SB_QBLOCK = 128
SB_SPAN_MIN = 4.0
SB_SPAN_MAX = 4096.0
GDN_HEADS = 8
GDN_KEY_DIM = 128
GDN_VAL_DIM = 128
GDN_QK_WIDTH = GDN_HEADS * GDN_KEY_DIM
GDN_V_WIDTH = GDN_HEADS * GDN_VAL_DIM
GDN_CONV = 4
GDN_CONV_CH = 2 * GDN_QK_WIDTH + GDN_V_WIDTH
GDN_CHUNK = 64
MIX_WIDTH = SB_WIDTH + GDN_V_WIDTH
N_MEM = 256
X_HEADS = 4
X_HEAD_DIM = 128
X_WIDTH = X_HEADS * X_HEAD_DIM
FFN_HIDDEN = -(-8 * D_MODEL // (3 * 256)) * 256
IN_SPLITS = (SB_WIDTH, SB_WIDTH, SB_WIDTH, GDN_QK_WIDTH, GDN_QK_WIDTH, GDN_V_WIDTH, GDN_HEADS, GDN_HEADS, GDN_V_WIDTH)
IN_WIDTH = 3 * SB_WIDTH + 2 * GDN_QK_WIDTH + 2 * GDN_V_WIDTH + 2 * GDN_HEADS
EPS = 1e-6

kernel_name = 'hybrid_stickbreak_gdeltanet_decode_step'


def split_cols(t, sizes):
    idx, acc = [], 0
    for s in sizes[:-1]:
        acc += s
        idx.append(acc)
    return jnp.split(t, idx, axis=-1)


def rmsnorm(x, g):
    xf = x.astype(jnp.float32)
    y = xf * lax.rsqrt(jnp.mean(xf * xf, axis=-1, keepdims=True) + EPS)
    return (y * g.astype(jnp.float32)).astype(x.dtype)


def l2norm(x):
    xf = x.astype(jnp.float32)
    return (xf * lax.rsqrt(jnp.sum(xf * xf, axis=-1, keepdims=True) + EPS)).astype(x.dtype)


def stick_breaking_attention(q, k, v, q_pos, k_pos, bias):
    b, sq, h, d = q.shape
    blk = math.gcd(sq, SB_QBLOCK)
    nb = sq // blk
    scale = d ** -0.5
    qb = q.reshape(b, nb, blk, h, d).transpose(1, 0, 2, 3, 4)
    pb = q_pos.reshape(nb, blk)
    bh = bias.astype(jnp.float32)[None, :, None, None]

    def block(args):
        qi, pi = args
        z = jnp.einsum('bqhd,bkhd->bhqk', qi, k, preferred_element_type=jnp.float32) * scale + bh
        causal = (k_pos[None, :] < pi[:, None])[None, None]
        log_stay = jnp.where(causal, jax.nn.log_sigmoid(-z), 0.0)
        survive = lax.cumsum(log_stay, axis=3, reverse=True) - log_stay
        w = jnp.exp(jnp.where(causal, jax.nn.log_sigmoid(z) + survive, -jnp.inf))
        return jnp.einsum('bhqk,bkhd->bqhd', w.astype(v.dtype), v)

    o = lax.map(block, (qb, pb))
    return o.transpose(1, 0, 2, 3, 4).reshape(b, sq, h, d)


def gated_delta_rule(q, k, v, g, beta, s0):
    b, s, h, dk = q.shape
    dv = v.shape[-1]
    c = math.gcd(s, GDN_CHUNK)
    n = s // c
    f32 = jnp.float32

    def to_chunks(t):
        t = t.astype(f32).reshape(b, n, c, h, *t.shape[3:])
        return jnp.moveaxis(t, 3, 2).swapaxes(0, 1)

    qc, kc, vc, gc, bc = (to_chunks(t) for t in (q, k, v, g, beta))
    cum = jnp.cumsum(gc, axis=-1)
    ii = jnp.arange(c)
    incl = ii[:, None] >= ii[None, :]
    strict = ii[:, None] > ii[None, :]
    decay = jnp.exp(jnp.where(incl, cum[..., :, None] - cum[..., None, :], -jnp.inf))
    kk = jnp.einsum('nbhid,nbhjd->nbhij', kc, kc)
    lower = jnp.where(strict, bc[..., :, None] * kk * decay, 0.0)
    rhs = jnp.concatenate([vc * bc[..., None], kc * (bc * jnp.exp(cum))[..., None]], axis=-1)
    sol = lax.linalg.triangular_solve(lower + jnp.eye(c, dtype=f32), rhs, left_side=True, lower=True, unit_diagonal=True)
    u, w = sol[..., :dv], sol[..., dv:]
    qk = jnp.einsum('nbhid,nbhjd->nbhij', qc, kc) * decay
    q_dec = qc * jnp.exp(cum)[..., None]
    k_end = kc * jnp.exp(cum[..., -1:] - cum)[..., None]
    g_end = jnp.exp(cum[..., -1])[..., None, None]

    def step(state, inp):
        u_i, w_i, qk_i, q_i, k_i, g_i = inp
        v_new = u_i - jnp.einsum('bhcd,bhde->bhce', w_i, state)
        o_i = jnp.einsum('bhcd,bhde->bhce', q_i, state) + jnp.einsum('bhij,bhje->bhie', qk_i, v_new)
        state = g_i * state + jnp.einsum('bhcd,bhce->bhde', k_i, v_new)
        return state, o_i

    s_fin, o = lax.scan(step, s0.astype(f32), (u, w, qk, q_dec, k_end, g_end))
    o = jnp.moveaxis(o.swapaxes(0, 1), 2, 3).reshape(b, s, h, dv)
    return o, s_fin


def memory_kv(mem, p):
    b, m, _ = mem.shape
    hm = rmsnorm(mem, p['norm_mem_g'])
    mk = rmsnorm((hm @ p['w_mk']).reshape(b, m, X_HEADS, X_HEAD_DIM), p['x_k_norm_g'])
    mv = (hm @ p['w_mv']).reshape(b, m, X_HEADS, X_HEAD_DIM)
    return mk, mv


def decoder_layer(x, mem_k, mem_v, sb_k_past, sb_v_past, gdn_s0, conv0, pos0, p):
    b, s, _ = x.shape
    h = rmsnorm(x, p['norm_mix_g'])
    sq, sk, sv, gq, gk, gv, ga, gb, gz = split_cols(h @ p['w_in'], IN_SPLITS)
    q = rmsnorm(sq.reshape(b, s, SB_HEADS, SB_HEAD_DIM), p['sb_q_norm_g'])
    k_new = rmsnorm(sk.reshape(b, s, SB_HEADS, SB_HEAD_DIM), p['sb_k_norm_g'])
    v_new = sv.reshape(b, s, SB_HEADS, SB_HEAD_DIM)
    k_all = jnp.concatenate([sb_k_past.astype(k_new.dtype), k_new], axis=1)
    v_all = jnp.concatenate([sb_v_past.astype(v_new.dtype), v_new], axis=1)
    q_pos = pos0 + jnp.arange(s, dtype=jnp.int32)
    k_pos = jnp.arange(k_all.shape[1], dtype=jnp.int32)
    o_sb = stick_breaking_attention(q, k_all, v_all, q_pos, k_pos, p['sb_logit_bias']).reshape(b, s, SB_WIDTH)
    raw = jnp.concatenate([gq, gk, gv], axis=-1)
    padded = jnp.concatenate([conv0.astype(raw.dtype), raw], axis=1)
    conv = sum(padded[:, j:j + s] * p['gdn_conv_w'][j] for j in range(GDN_CONV))
    conv_new = padded[:, s:]
    cq, ck, cv = split_cols(jax.nn.silu(conv), (GDN_QK_WIDTH, GDN_QK_WIDTH, GDN_V_WIDTH))
    gdn_q = l2norm(cq.reshape(b, s, GDN_HEADS, GDN_KEY_DIM)) * (GDN_KEY_DIM ** -0.5)
    gdn_k = l2norm(ck.reshape(b, s, GDN_HEADS, GDN_KEY_DIM))
    gdn_v = cv.reshape(b, s, GDN_HEADS, GDN_VAL_DIM)
    beta = jax.nn.sigmoid(gb.astype(jnp.float32))
    log_decay = -jnp.exp(p['gdn_a_log'].astype(jnp.float32)) * jax.nn.softplus(ga.astype(jnp.float32) + p['gdn_dt_bias'].astype(jnp.float32))
    o_gdn, gdn_s = gated_delta_rule(gdn_q, gdn_k, gdn_v, log_decay, beta, gdn_s0)
    o_gdn = rmsnorm(o_gdn.astype(x.dtype), p['gdn_out_norm_g']) * jax.nn.silu(gz.reshape(b, s, GDN_HEADS, GDN_VAL_DIM))
    x = x + jnp.concatenate([o_sb, o_gdn.reshape(b, s, GDN_V_WIDTH)], axis=-1) @ p['w_out']
    hx = rmsnorm(x, p['norm_x_g'])
    xq = rmsnorm((hx @ p['w_xq']).reshape(b, s, X_HEADS, X_HEAD_DIM), p['x_q_norm_g'])
    logits = jnp.einsum('bshd,bmhd->bhsm', xq, mem_k, preferred_element_type=jnp.float32) * (X_HEAD_DIM ** -0.5)
    attn = jax.nn.softmax(logits, axis=-1).astype(mem_v.dtype)
    xo = jnp.einsum('bhsm,bmhd->bshd', attn, mem_v).reshape(b, s, X_WIDTH)
    x = x + xo @ p['w_xo']
    hf = rmsnorm(x, p['norm_ffn_g'])
    gate, up = jnp.split(hf @ p['w_gate_up'], 2, axis=-1)
    x = x + (jax.nn.silu(gate) * up) @ p['w_down']
    return x, k_new, v_new, gdn_s.astype(gdn_s0.dtype), conv_new


def setup_inputs(seed: int = 0) -> dict:
    key = jax.random.key(seed)
    k = jax.random.split(key, 32)
    f32 = jnp.float32

    def normal(i, shape, scale):
        return jax.random.normal(k[i], shape, f32) * scale

    def gain(i, n):
        return 1.0 + 0.02 * jax.random.normal(k[i], (DEPTH, n), f32)

    n_pages = PAST_LEN // PAGE_SIZE
    n_used = DEC_BATCH * n_pages
    n_phys = n_used + max(1, n_used // 4)
    page_table = jax.random.permutation(k[5], n_phys)[:n_used].reshape(DEC_BATCH, n_pages).astype(jnp.int32)
    dt = jnp.exp(jax.random.uniform(k[14], (DEPTH, GDN_HEADS), f32, math.log(1e-3), math.log(1e-1)))
    dt_bias = dt + jnp.log(-jnp.expm1(-dt))
    a_log = jnp.log(jax.random.uniform(k[15], (DEPTH, GDN_HEADS), f32, 1.0, 16.0))
    span = jnp.exp(jnp.linspace(math.log(SB_SPAN_MIN), math.log(SB_SPAN_MAX), SB_HEADS, dtype=f32))
    sb_bias = -jnp.log(span)[None, :] + 0.05 * jax.random.normal(k[30], (DEPTH, SB_HEADS), f32)
    return {
        'x_prompt': normal(0, (BATCH, SEQ, D_MODEL), 1.0),
        'x_sample': normal(1, (DEC_BATCH, DEC_SEQ, D_MODEL), 1.0),
        'mem_prompt': normal(2, (BATCH, N_MEM, D_MODEL), 1.0),
        'cache_sb_k': normal(3, (DEPTH, n_phys, PAGE_SIZE, SB_HEADS, SB_HEAD_DIM), 1.0),
        'cache_sb_v': normal(4, (DEPTH, n_phys, PAGE_SIZE, SB_HEADS, SB_HEAD_DIM), 1.0),
        'page_table': page_table,
        'state_gdn': normal(6, (DEPTH, DEC_BATCH, GDN_HEADS, GDN_KEY_DIM, GDN_VAL_DIM), 0.1),
        'state_gdn_conv': normal(7, (DEPTH, DEC_BATCH, GDN_CONV - 1, GDN_CONV_CH), 1.0),
        'cache_mem_k': normal(8, (DEPTH, DEC_BATCH, N_MEM, X_HEADS, X_HEAD_DIM), 1.0),
        'cache_mem_v': normal(9, (DEPTH, DEC_BATCH, N_MEM, X_HEADS, X_HEAD_DIM), 1.0),
        'norm_mix_g': gain(10, D_MODEL),
        'w_in': normal(11, (DEPTH, D_MODEL, IN_WIDTH), D_MODEL ** -0.5),
        'sb_q_norm_g': gain(12, SB_HEAD_DIM),
        'sb_k_norm_g': gain(13, SB_HEAD_DIM),
        'sb_logit_bias': sb_bias,
        'gdn_conv_w': normal(16, (DEPTH, GDN_CONV, GDN_CONV_CH), GDN_CONV ** -0.5),
        'gdn_a_log': a_log,
        'gdn_dt_bias': dt_bias,
        'gdn_out_norm_g': gain(17, GDN_VAL_DIM),
        'w_out': normal(18, (DEPTH, MIX_WIDTH, D_MODEL), MIX_WIDTH ** -0.5),
        'norm_x_g': gain(19, D_MODEL),
        'norm_mem_g': gain(20, D_MODEL),
        'w_xq': normal(21, (DEPTH, D_MODEL, X_WIDTH), D_MODEL ** -0.5),
        'w_mk': normal(22, (DEPTH, D_MODEL, X_WIDTH), D_MODEL ** -0.5),
        'w_mv': normal(23, (DEPTH, D_MODEL, X_WIDTH), D_MODEL ** -0.5),
        'x_q_norm_g': gain(24, X_HEAD_DIM),
        'x_k_norm_g': gain(25, X_HEAD_DIM),
        'w_xo': normal(26, (DEPTH, X_WIDTH, D_MODEL), X_WIDTH ** -0.5),
        'norm_ffn_g': gain(27, D_MODEL),
        'w_gate_up': normal(28, (DEPTH, D_MODEL, 2 * FFN_HIDDEN), D_MODEL ** -0.5),
        'w_down': normal(29, (DEPTH, FFN_HIDDEN, D_MODEL), FFN_HIDDEN ** -0.5),
    }


def reference(x_prompt, x_sample, mem_prompt, cache_sb_k, cache_sb_v, page_table,
              state_gdn, state_gdn_conv, cache_mem_k, cache_mem_v,
              norm_mix_g, w_in, sb_q_norm_g, sb_k_norm_g, sb_logit_bias, gdn_conv_w, gdn_a_log,
              gdn_dt_bias, gdn_out_norm_g, w_out, norm_x_g, norm_mem_g, w_xq, w_mk, w_mv, x_q_norm_g,
              x_k_norm_g, w_xo, norm_ffn_g, w_gate_up, w_down):
    bp = x_prompt.shape[0]
    bs = x_sample.shape[0]
    past_len = page_table.shape[1] * PAGE_SIZE
    dt = x_prompt.dtype
    yp, ys = x_prompt, x_sample
    sbk_p, sbv_p, sbk_s, sbv_s = [], [], [], []
    gs_p, gc_p, gs_s, gc_s, mk_p, mv_p = [], [], [], [], [], []
    for l in range(DEPTH):
        p = {
            'norm_mix_g': norm_mix_g[l], 'w_in': w_in[l], 'sb_q_norm_g': sb_q_norm_g[l],
            'sb_k_norm_g': sb_k_norm_g[l], 'sb_logit_bias': sb_logit_bias[l], 'gdn_conv_w': gdn_conv_w[l],
            'gdn_a_log': gdn_a_log[l], 'gdn_dt_bias': gdn_dt_bias[l], 'gdn_out_norm_g': gdn_out_norm_g[l],
            'w_out': w_out[l], 'norm_x_g': norm_x_g[l], 'norm_mem_g': norm_mem_g[l], 'w_xq': w_xq[l],
            'w_mk': w_mk[l], 'w_mv': w_mv[l], 'x_q_norm_g': x_q_norm_g[l], 'x_k_norm_g': x_k_norm_g[l],
            'w_xo': w_xo[l], 'norm_ffn_g': norm_ffn_g[l], 'w_gate_up': w_gate_up[l], 'w_down': w_down[l],
        }
        mem_k, mem_v = memory_kv(mem_prompt, p)
        empty = jnp.zeros((bp, 0, SB_HEADS, SB_HEAD_DIM), dt)
        s0 = jnp.zeros((bp, GDN_HEADS, GDN_KEY_DIM, GDN_VAL_DIM), state_gdn.dtype)
        c0 = jnp.zeros((bp, GDN_CONV - 1, GDN_CONV_CH), state_gdn_conv.dtype)
        yp, kp, vp, sp_, cp = decoder_layer(yp, mem_k, mem_v, empty, empty, s0, c0, 0, p)
        sbk_p.append(kp); sbv_p.append(vp); gs_p.append(sp_); gc_p.append(cp); mk_p.append(mem_k); mv_p.append(mem_v)
        past_k = cache_sb_k[l][page_table].reshape(bs, past_len, SB_HEADS, SB_HEAD_DIM)
        past_v = cache_sb_v[l][page_table].reshape(bs, past_len, SB_HEADS, SB_HEAD_DIM)
        ys, ks_, vs_, ss_, cs_ = decoder_layer(ys, cache_mem_k[l], cache_mem_v[l], past_k, past_v,
                                               state_gdn[l], state_gdn_conv[l], past_len, p)
        sbk_s.append(ks_); sbv_s.append(vs_); gs_s.append(ss_); gc_s.append(cs_)
    return (yp, ys, jnp.stack(sbk_p), jnp.stack(sbv_p), jnp.stack(sbk_s), jnp.stack(sbv_s),
            jnp.stack(gs_p), jnp.stack(gc_p), jnp.stack(gs_s), jnp.stack(gc_s),
            jnp.stack(mk_p), jnp.stack(mv_p))
```

```python
import contextlib
import os
import numpy as np
import concourse.bass as bass
import concourse.mybir as mybir
from concourse.bass_utils import run_bass_kernel_spmd

F32 = mybir.dt.float32
BF16 = mybir.dt.bfloat16
I32 = mybir.dt.int32
F32R = mybir.dt.float32r
AF = mybir.ActivationFunctionType
ALU = mybir.AluOpType

ENGS = ("pe", "act", "dve", "pool", "sp")
NDSEM = 12
EPS = 1e-6
NEG = -30000.0


class Instr:
    __slots__ = ("eng", "fn", "dma", "deps", "signal", "count", "dsem", "dcount", "barrier")

    def __init__(self, eng, fn, dma):
        self.eng = eng
        self.fn = fn
        self.dma = dma
        self.deps = []
        self.signal = False
        self.count = 0
        self.dsem = None
        self.dcount = 0
        self.barrier = None


class Prog:
    def __init__(self, nc, stack):
        self.nc = nc
        self.streams = {e: [] for e in ENGS}
        self.last_writer = {}
        self.readers = {}
        self.ndma = {e: 0 for e in ENGS}
        self.dma_hist = {e: [] for e in ENGS}
        self.ccount = {e: 0 for e in ENGS}
        self.waited = {e: {} for e in ENGS}
        self.sems = {e: stack.enter_context(nc.semaphore("s_" + e)) for e in ENGS}
        self.dsems = {}
        for e in ("sp", "pool", "act"):
            for i in range(NDSEM):
                self.dsems[(e, i)] = stack.enter_context(nc.semaphore("d_%s_%d" % (e, i)))
        self.pending_barrier = None

    def op(self, eng, fn, reads=(), writes=(), dma=False):
        ins = Instr(eng, fn, dma)
        deps = {}
        px = [k for k in reads if isinstance(k, str) and k[:2] == "ps" and k[2:].isdigit()]
        if px:
            reads = [k for k in reads if k not in px]
            writes = list(writes) + px
        for k in reads:
            w = self.last_writer.get(k)
            if w is not None:
                deps[id(w)] = w
        for k in writes:
            w = self.last_writer.get(k)
            if w is not None:
                deps[id(w)] = w
            rd = self.readers.get(k)
            if rd:
                for r in rd.values():
                    deps[id(r)] = r
        ins.deps = list(deps.values())
        rkey = ("d", eng, self.ndma[eng] % NDSEM) if dma else ("c", eng)
        for k in reads:
            self.readers.setdefault(k, {})[rkey] = ins
        for k in writes:
            self.last_writer[k] = ins
            self.readers[k] = {}
        if dma:
            i = self.ndma[eng]
            self.ndma[eng] += 1
            ins.dsem = (eng, i % NDSEM)
            ins.dcount = 16 * (i // NDSEM + 1)
            hist = self.dma_hist[eng]
            if i >= NDSEM:
                ins.deps.append(hist[i - NDSEM])
            hist.append(ins)
        self.streams[eng].append(ins)
        return ins

    def emit(self, final=False):
        nc = self.nc
        streams = self.streams
        for e in ENGS:
            for ins in streams[e]:
                for d in ins.deps:
                    if d.dma:
                        continue
                    if d.eng == ins.eng and not ins.dma and d.eng == "pe":
                        continue
                    d.signal = True
            for ins in reversed(streams[e]):
                if not ins.dma:
                    ins.signal = True
                    break
        for e in ENGS:
            c = self.ccount[e]
            for ins in streams[e]:
                if not ins.dma and ins.signal:
                    c += 1
                    ins.count = c
            self.ccount[e] = c
        bar = []
        for e in ENGS:
            if self.ccount[e]:
                bar.append((("c", e), self.sems[e], self.ccount[e]))
            hist = self.dma_hist[e]
            for i in range(max(0, len(hist) - NDSEM), len(hist)):
                d = hist[i]
                bar.append((("d",) + d.dsem, self.dsems[d.dsem], d.dcount))
        prev_bar = self.pending_barrier

        def run(engobj, e):
            waited = self.waited[e]

            def wait(key, sem, val):
                if waited.get(key, 0) >= val:
                    return
                waited[key] = val
                engobj.wait_ge(sem, val)

            if prev_bar is not None:
                for key, sem, val in prev_bar:
                    if key == ("c", e):
                        continue
                    wait(key, sem, val)
            for ins in streams[e]:
                for d in ins.deps:
                    if d.dma:
                        wait(("d",) + d.dsem, self.dsems[d.dsem], d.dcount)
                    else:
                        if d.eng == e and not ins.dma and e == "pe":
                            continue
                        wait(("c", d.eng), self.sems[d.eng], d.count)
                r = ins.fn(engobj)
                if ins.dma:
                    r.then_inc(self.dsems[ins.dsem], 16)
                elif ins.signal:
                    r.then_inc(self.sems[e], 1)
            if final and e == "sp":
                for key, sem, val in bar:
                    wait(key, sem, val)

        with nc.Block() as block:
            @block.tensor
            def _(eng):
                run(eng, "pe")

            @block.scalar
            def _(eng):
                run(eng, "act")

            @block.vector
            def _(eng):
                run(eng, "dve")

            @block.gpsimd
            def _(eng):
                run(eng, "pool")

            @block.sync
            def _(eng):
                run(eng, "sp")

        self.pending_barrier = bar
        self.streams = {e: [] for e in ENGS}
        self.last_writer = {}
        self.readers = {}


C_SQ, C_SK, C_SV, C_GQ, C_GK, C_GV, C_GA, C_GB, C_GZ = 0, 1024, 2048, 3072, 4096, 5120, 6144, 6152, 6160
NT = 2112
OWN0, SMP0 = 1024, 2048
NTO = 1088

W_NAMES = [
    ("norm_mix_g", [1, 2048]), ("w_in", [2048, 7184]), ("sb_q_norm_g", [1, 128]), ("sb_k_norm_g", [1, 128]),
    ("sb_logit_bias", [1, 8]), ("gdn_conv_w", [4, 3072]), ("gdn_a_log", [1, 8]), ("gdn_dt_bias", [1, 8]),
    ("gdn_out_norm_g", [1, 128]), ("w_out", [2048, 2048]), ("norm_x_g", [1, 2048]), ("norm_mem_g", [1, 2048]),
    ("w_xq", [2048, 512]), ("w_mk", [2048, 512]), ("w_mv", [2048, 512]), ("x_q_norm_g", [1, 128]),
    ("x_k_norm_g", [1, 128]), ("w_xo", [512, 2048]), ("norm_ffn_g", [1, 2048]), ("w_gate_up", [2048, 11264]),
    ("w_down", [5632, 2048]),
]
OUT_SPECS = [
    ("yp", [1024, 2048]), ("ys", [64, 2048]), ("skp", [1024, 1024]), ("svp", [1024, 1024]),
    ("sks", [64, 1024]), ("svs", [64, 1024]), ("gsp", [8, 128, 128]), ("gcp", [3, 3072]),
    ("gss", [16, 8, 128, 128]), ("gcs", [48, 3072]), ("mkp", [256, 512]), ("mvp", [256, 512]),
]


def build(n_phys, stage=99):
    nc = bass.Bass("TRN2", target_bir_lowering=False)

    def din(n, s, dt=F32):
        return nc.dram_tensor(n, list(s), dt, kind="ExternalInput").ap()

    def dout(n, s, dt=F32):
        return nc.dram_tensor(n, list(s), dt, kind="ExternalOutput").ap()

    D = {}
    D["xp"] = din("xp", [2048, 2048])
    D["xs"] = din("xs", [64, 2048])
    D["mem"] = din("mem", [256, 2048])
    D["ck"] = din("ck", [n_phys * 128, 1024])
    D["cv"] = din("cv", [n_phys * 128, 1024])
    D["pt"] = din("pt", [1, 256], I32)
    D["sg"] = din("sg", [16, 8, 128, 128])
    D["sc"] = din("sc", [48, 3072])
    D["cmk"] = din("cmk", [16, 256, 512])
    D["cmv"] = din("cmv", [16, 256, 512])
    D["flags"] = din("flags", [1, 4])
    for n, s in W_NAMES:
        D[n] = din(n, s)
    for n, s in OUT_SPECS:
        D[n] = dout(n, s)
    D["scr_gab"] = nc.dram_tensor("scr_gab", [64, 16], F32, kind="Internal").ap()

    with contextlib.ExitStack() as perm:
        P = Prog(nc, perm)

        def sb(name, shape, dt, st=perm):
            return st.enter_context(nc.sbuf_tensor(name, list(shape), dt))

        def act(out, in_, func, r, w, bias=0.0, scale=1.0, accum=None):
            if accum is None:
                P.op("act", lambda e: e.activation(out=out, in_=in_, func=func, bias=bias, scale=scale), r, w)
            else:
                P.op("act", lambda e: e.activation(out=out, in_=in_, func=func, bias=bias, scale=scale, accum_out=accum), r, w)

        def cp(eng, out, in_, r, w):
            if eng == "act":
                P.op("act", lambda e: e.activation(out=out, in_=in_, func=AF.Copy), r, w)
            else:
                P.op(eng, lambda e: e.tensor_copy(out=out, in_=in_), r, w)

        def tt(eng, out, a, b, op, r, w):
            P.op(eng, lambda e: e.tensor_tensor(out=out, in0=a, in1=b, op=op), r, w)

        def ts(eng, out, a, s1, s2, op0, op1, r, w):
            if s2 is None:
                P.op(eng, lambda e: e.tensor_scalar(out=out, in0=a, scalar1=s1, scalar2=None, op0=op0), r, w)
            else:
                P.op(eng, lambda e: e.tensor_scalar(out=out, in0=a, scalar1=s1, scalar2=s2, op0=op0, op1=op1), r, w)

        def stt(out, a, s, b, op0, op1, r, w):
            P.op("dve", lambda e: e.scalar_tensor_tensor(out=out, in0=a, scalar=s, in1=b, op0=op0, op1=op1), r, w)

        def mm(out, lhsT, rhs, start, stop, r, w):
            P.op("pe", lambda e: e.matmul(out, lhsT=lhsT, rhs=rhs, start=start, stop=stop), r, w)

        def tr(out, in_, ident, r, w):
            P.op("pe", lambda e: e.transpose(out=out, in_=in_, identity=ident), r, w)

        def dma(eng, out, in_, r, w):
            P.op(eng, lambda e: e.dma_start(out=out, in_=in_), r, w, dma=True)

        def memset(eng, ap, val, w):
            P.op(eng, lambda e: e.memset(ap, val), (), w)

        def asel(out, in_, pattern, cmp, base, cm, r, w, fill=0.0):
            P.op("pool", lambda e: e.affine_select(out=out, in_=in_, pattern=pattern, compare_op=cmp, fill=fill,
                                                   base=base, channel_multiplier=cm), r, w)

        PS = [perm.enter_context(nc.psum_tensor("ps%d" % i, [128, 512], F32)) for i in range(8)]
        PSK = ["ps%d" % i for i in range(8)]

        def psb(i):
            return PS[i][:].bitcast(BF16)

        ident_f = sb("ident_f", [128, 128], F32)
        ident_b = sb("ident_b", [128, 128], BF16)
        ones_f = sb("ones_f", [128, 128], F32)
        ones_b = sb("ones_b", [128, 128], BF16)
        tri_gt = sb("tri_gt", [128, 128], F32)
        tri_le = sb("tri_le", [128, 128], F32)
        memset("pool", ones_f[:], 1.0, ["ones_f"])
        memset("pool", ones_b[:], 1.0, ["ones_b"])
        memset("pool", ident_f[:], 0.0, ["ident_f"])
        asel(ident_f[:], ident_f[:], [[-1, 128]], ALU.not_equal, 0, 1, ["ident_f"], ["ident_f"], fill=1.0)
        cp("dve", ident_b[:], ident_f[:], ["ident_f"], ["ident_b"])
        asel(tri_gt[:], ones_f[:], [[-1, 128]], ALU.is_gt, 0, 1, ["ones_f"], ["tri_gt"])
        asel(tri_le[:], ones_f[:], [[1, 128]], ALU.is_ge, 0, -1, ["ones_f"], ["tri_le"])
        tri_gt_r = sb("tri_gt_r", [128, 128], F32)
        ones_r = sb("ones_r", [128, 128], F32)
        cp("dve", tri_gt_r[:].bitcast(F32R), tri_gt[:], ["tri_gt"], ["tri_gt_r"])
        cp("dve", ones_r[:].bitcast(F32R), ones_f[:], ["ones_f"], ["ones_r"])

        flg = sb("flg", [128, 4], F32)
        dma("sp", flg[:], D["flags"][0:1, :].partition_broadcast(128), [], ["flg"])
        sbb = sb("sbb", [128, 8], F32)
        dma("sp", sbb[:], D["sb_logit_bias"][0:1, :].partition_broadcast(128), [], ["sbb"])
        sbb_pre = sb("sbb_pre", [128, 8], F32)
        ts("dve", sbb_pre[:], sbb[:], flg[:, 1:2], None, ALU.add, None, ["sbb", "flg"], ["sbb_pre"])
        alog = sb("alog", [128, 8], F32)
        dma("sp", alog[:], D["gdn_a_log"][0:1, :].partition_broadcast(128), [], ["alog"])
        dtb = sb("dtb", [128, 8], F32)
        dma("sp", dtb[:], D["gdn_dt_bias"][0:1, :].partition_broadcast(128), [], ["dtb"])
        nea = sb("nea", [128, 8], F32)
        act(nea[:], alog[:], AF.Exp, ["alog"], ["nea"])
        ts("dve", nea[:], nea[:], -1.0, None, ALU.mult, None, ["nea"], ["nea"])
        hg_row = sb("hg_row", [128, 128], F32)
        memset("pool", hg_row[:], 0.0, ["hg_row"])
        for i, n in enumerate(["sb_q_norm_g", "sb_k_norm_g", "x_q_norm_g", "x_k_norm_g", "gdn_out_norm_g"]):
            dma("sp", hg_row[i:i + 1, :], D[n][0:1, :], [], ["hg_row"])
        hg = sb("hg", [128, 8], F32)
        tr(PS[7][:, 0:128], hg_row[:, :], ident_f[:, :], ["hg_row", "ident_f"], ["ps7"])
        cp("dve", hg[:], PS[7][:, 0:8], ["ps7"], ["hg"])
        gn = sb("gn", [128, 4, 16, 1], F32)
        gn_row = sb("gn_row", [128, 128], F32)
        memset("pool", gn_row[:], 0.0, ["gn_row"])
        for i, n in enumerate(["norm_mix_g", "norm_x_g", "norm_ffn_g", "norm_mem_g"]):
            dma("sp", gn_row[16 * i:16 * i + 16, :], D[n].rearrange("o (k p) -> (o k) p", p=128), [], ["gn_row"])
        tr(PS[7][:, 128:256], gn_row[:, :], ident_f[:, :], ["gn_row", "ident_f"], ["ps7"])
        cp("dve", gn[:].rearrange("p a k o -> p (a k o)"), PS[7][:, 128:192], ["ps7"], ["gn"])

        BIG = sb("BIG", [128, 16 * NT], BF16)
        XT = BIG.reshape([128, 16, NT])
        CT = sb("CT", [128, 16, NTO], BF16)
        MKT = sb("MKT", [128, 4, 256], BF16)
        MV = sb("MV", [128, 2, 512], BF16)
        QTS = sb("QTS", [128, 8, 64], BF16)
        KTS = sb("KTS", [128, 8, 64], BF16)
        CW = sb("CW", [128, 24, 4], F32)
        SCT = sb("SCT", [128, 24, 48], F32)

        wst = [sb("wst%d" % i, [128, 512], F32) for i in range(4)]
        wcnt = [0]
        cvt_rot = [0]

        def load_w(dst, dkey, src, k0, nk, col0, ncols, gains=None, dcol0=0):
            srcv = src.rearrange("(k p) n -> p k n", p=128)
            per = max(1, 512 // ncols)
            kk = k0
            while kk < k0 + nk:
                n = min(per, k0 + nk - kk)
                i = wcnt[0] % 4
                wcnt[0] += 1
                st = wst[i]
                stv = st[:, 0:n * ncols].rearrange("p (k n) -> p k n", k=n)
                dma("sp", stv, srcv[:, kk:kk + n, col0:col0 + ncols], [], ["wst%d" % i])
                eng = ("dve", "pool", "act")[cvt_rot[0] % 3]
                cvt_rot[0] += 1
                o = dst[:, kk:kk + n, dcol0:dcol0 + ncols]
                if gains is None:
                    cp(eng, o, stv, ["wst%d" % i], [dkey])
                else:
                    if eng == "act":
                        eng = "dve"
                    g = gains[:, kk:kk + n, :].broadcast_to([128, n, ncols])
                    tt(eng, o, stv, g, ALU.mult, ["wst%d" % i, "gn"], [dkey])
                kk += n

        def rms_to_fm(st, src_rows, n, dstT, c0, dkey, xin, xkey, tag):
            dma("sp", xin[0:n, :], src_rows, [], [xkey])
            junk = st["junk"]
            ss = st["ss"]
            act(junk[0:n, :], xin[0:n, :], AF.Square, [xkey], ["junk", "ss"], accum=ss[0:n, :])
            act(ss[0:n, :], ss[0:n, :], AF.Ln, ["ss"], ["ss"], bias=EPS, scale=1.0 / 2048)
            act(ss[0:n, :], ss[0:n, :], AF.Exp, ["ss"], ["ss"], scale=-0.5)
            xb = st["xb"]
            ts("dve", xb[0:n, :], xin[0:n, :], ss[0:n, 0:1], None, ALU.mult, None, [xkey, "ss"], ["xb"])
            for half in range(2):
                bank = 6 + half
                pv = psb(bank)
                for k in range(8):
                    kk = half * 8 + k
                    tr(pv[:, k * 128:k * 128 + n], xb[0:n, kk * 128:(kk + 1) * 128], ident_b[0:n, 0:n],
                       ["xb", "ident_b"], [PSK[bank]])
                src = pv[:, 0:1024].rearrange("p (a b) -> p a b", a=8)[:, :, 0:n]
                cp("act" if half == 0 else "dve", dstT[:, half * 8:half * 8 + 8, c0:c0 + n], src, [PSK[bank]],
                   [(dkey, tag, half)])

        def fm_norm(psk, ps_ap, T, sc, out, wkeys, st, rextra=(), mean=True):
            sq = st["sq"]
            rr = st["rr"]
            act(sq[:, 0:T], ps_ap, AF.Square, [psk], ["sq"])
            mm(PS[5][:, 0:T], ones_b[:], sq[:, 0:T], True, True, ["sq", "ones_b"], ["ps5"])
            act(rr[:, 0:T], PS[5][:, 0:T], AF.Ln, ["ps5"], ["rr"], bias=EPS, scale=(1.0 / 128 if mean else 1.0))
            act(rr[:, 0:T], rr[:, 0:T], AF.Exp, ["rr"], ["rr"], scale=-0.5)
            stt(out, ps_ap, sc, rr[:, 0:T], ALU.mult, ALU.mult, [psk, "rr"] + list(rextra), wkeys)

        with contextlib.ExitStack() as ph:
            rowbuf = sb("rowbuf", [128, 3072], F32, ph)
            memset("pool", rowbuf[:], 0.0, ["rowbuf"])
            dma("sp", rowbuf[0:4, :], D["gdn_conv_w"][:, :], [], ["rowbuf"])
            for c4 in range(6):
                for j in range(4):
                    c = c4 * 4 + j
                    tr(PS[4][:, j * 128:(j + 1) * 128], rowbuf[:, c * 128:(c + 1) * 128], ident_f[:], ["rowbuf", "ident_f"], ["ps4"])
                cp("dve", CW[:, c4 * 4:c4 * 4 + 4, :], PS[4][:, :].rearrange("p (j x) -> p j x", j=4)[:, :, 0:4], ["ps4"], ["CW"])
            dma("sp", rowbuf[0:48, :], D["sc"][:, :], ["rowbuf"], ["rowbuf"])
            for c4 in range(6):
                for j in range(4):
                    c = c4 * 4 + j
                    tr(PS[4][:, j * 128:(j + 1) * 128], rowbuf[:, c * 128:(c + 1) * 128], ident_f[:], ["rowbuf", "ident_f"], ["ps4"])
                cp("dve", SCT[:, c4 * 4:c4 * 4 + 4, :], PS[4][:, :].rearrange("p (j x) -> p j x", j=4)[:, :, 0:48], ["ps4"], ["SCT"])
            P.emit()

        with contextlib.ExitStack() as ph:
            st1 = {
                "junk": sb("junk", [128, 2048], BF16, ph),
                "ss": sb("ss", [128, 1], F32, ph),
                "xb": sb("xb", [128, 2048], BF16, ph),
                "sq": sb("sq1", [128, 512], BF16, ph),
                "rr": sb("rr1", [128, 512], F32, ph),
            }
            xin = [sb("xin%d" % i, [128, 2048], F32, ph) for i in range(2)]
            KD = int(os.environ.get("KDBG", "9"))
            for t in range(17 if KD >= 3 else (1 if KD == 2 else 0)):
                if t < 16:
                    rows, n, c0 = D["xp"][t * 128:(t + 1) * 128, :], 128, t * 128
                else:
                    rows, n, c0 = D["xs"][:, :], 64, SMP0
                rms_to_fm(st1, rows, n, XT, c0, "XT", xin[t % 2], "xin%d" % (t % 2), t)
            MT = sb("MT", [128, 16, 256], BF16, ph)
            for t in range(2 if KD >= 4 else 0):
                rms_to_fm(st1, D["mem"][t * 128:(t + 1) * 128, :], 128, MT, t * 128, "MT", xin[t % 2], "xin%d" % (t % 2), t)
            wmk = sb("wmk", [128, 16, 512], BF16, ph)
            wmv = sb("wmv", [128, 16, 512], BF16, ph)
            load_w(wmk, "wmk", D["w_mk"], 0, 16, 0, 512, gn[:, 3])
            load_w(wmv, "wmv", D["w_mv"], 0, 16, 0, 512, gn[:, 3])
            MTK = [("MT", t, h2) for t in range(2) for h2 in range(2)]
            mk32 = sb("mk32", [128, 256], F32, ph)
            mko = sb("mko", [128, 2, 512], F32, ph)
            mvo = sb("mvo", [128, 2, 512], F32, ph)
            for h in range(4 if KD >= 5 else 0):
                for k in range(16):
                    mm(PS[0][:, 0:256], wmk[:, k, h * 128:(h + 1) * 128], MT[:, k, :], k == 0, k == 15,
                       ["wmk"] + MTK, ["ps0"])
                fm_norm("ps0", PS[0][:, 0:256], 256, hg[:, 3:4], mk32[:, :], ["mk32"], st1, ["hg"])
                cp("act", MKT[:, h, :], mk32[:, :], ["mk32"], [("MKT", h)])
                for t in range(2):
                    tr(PS[1][:, t * 128:(t + 1) * 128], mk32[:, t * 128:(t + 1) * 128], ident_f[:], ["mk32", "ident_f"], ["ps1"])
                cp("dve", mko[:, :, h * 128:(h + 1) * 128], PS[1][:, 0:256].rearrange("p (t d) -> p t d", t=2), ["ps1"], ["mko"])
            if KD >= 5:
                dma("sp", D["mkp"].rearrange("(t p) c -> p t c", p=128), mko[:], ["mko"], [])
            for t in range(2 if KD >= 6 else 0):
                for k in range(16):
                    mm(PS[2 + t][:, :], MT[:, k, t * 128:(t + 1) * 128], wmv[:, k, :], k == 0, k == 15, ["wmv"] + MTK, [PSK[2 + t]])
                cp("dve", mvo[:, t, :], PS[2 + t][:, :], [PSK[2 + t]], ["mvo"])
                cp("act", MV[:, t, :], mvo[:, t, :], ["mvo"], [("MV", t)])
            if KD >= 6:
                dma("sp", D["mvp"].rearrange("(t p) c -> p t c", p=128), mvo[:], ["mvo"], [])
            P.emit()

        SCALE = 128.0 ** -0.5
        with contextlib.ExitStack() as ph:
            st2 = {"sq": sb("sq2", [128, 512], BF16, ph), "rr": sb("rr2", [128, 512], F32, ph)}
            WB = [sb("wb%d" % i, [128, 16, 128], BF16, ph) for i in range(5)]
            wbi = [0]

            def wb_next():
                i = wbi[0] % 5
                wbi[0] += 1
                return WB[i], "wb%d" % i

            pa = [0]

            def pa_next():
                i = pa[0] % 2
                pa[0] += 1
                return i

            def proj_fm(bank, w, wkey, xc0, T):
                for k in range(16):
                    mm(PS[bank][:, 0:T], w[:, k, :], XT[:, k, xc0:xc0 + T], k == 0, k == 15, [wkey], [PSK[bank]])

            with contextlib.ExitStack() as ph2a:
                ph2 = ph
                ph = ph2a
                MSK = sb("MSK", [128, 4, 512], BF16, ph)
                memset("pool", MSK[:], 1.0, ["MSK"])
                for kd in range(4):
                    asel(MSK[:, kd, :], MSK[:, kd, :], [[1, 512]], ALU.is_ge, -128 * kd - 1, -1, ["MSK"], ["MSK"])
                QT = sb("QT", [128, NTO], BF16, ph)
                KT = sb("KT", [128, NT], BF16, ph)
                VH = sb("VH", [128, 17, 128], BF16, ph)
                kn32 = sb("kn32", [128, 512], F32, ph)
                kout = sb("kout", [128, 9, 128], F32, ph)
                vout = sb("vout", [128, 9, 128], F32, ph)
                EXb = [sb("EX%d" % i, [128, 512], F32, ph) for i in range(2)]
                T1b = [sb("T1%d" % i, [128, 512], F32, ph) for i in range(2)]
                Wbb = [sb("Wb%d" % i, [128, 512], BF16, ph) for i in range(2)]
                Rrs = [sb("Rr%d" % i, [128, 512], F32, ph) for i in range(2)]
                n_sb = 8 if stage >= 2 else 0
                for h in range(n_sb):
                    wq, wqk = wb_next()
                    wk, wkk = wb_next()
                    wv, wvk = wb_next()
                    load_w(wq, wqk, D["w_in"], 0, 16, C_SQ + h * 128, 128, gn[:, 0])
                    load_w(wk, wkk, D["w_in"], 0, 16, C_SK + h * 128, 128, gn[:, 0])
                    load_w(wv, wvk, D["w_in"], 0, 16, C_SV + h * 128, 128, gn[:, 0])
                    for xc0, T, qc0 in ((OWN0, 512, 0), (OWN0 + 512, 512, 512), (SMP0, 64, 1024)):
                        bk = pa_next()
                        proj_fm(bk, wq, wqk, xc0, T)
                        fm_norm(PSK[bk], PS[bk][:, 0:T], T, hg[:, 0:1], QT[:, qc0:qc0 + T], [("QT", qc0)], st2, ["hg"])
                    cp("pool", QTS[:, h, :], QT[:, 1024:1088], [("QT", 1024)], [("QTS", h)])
                    for gi, (xc0, T) in enumerate(((0, 512), (512, 512), (OWN0, 512), (OWN0 + 512, 512), (SMP0, 64))):
                        bk = pa_next()
                        proj_fm(bk, wk, wkk, xc0, T)
                        fm_norm(PSK[bk], PS[bk][:, 0:T], T, hg[:, 1:2], kn32[:, 0:T], ["kn32"], st2, ["hg"])
                        cp("act", KT[:, xc0:xc0 + T], kn32[:, 0:T], ["kn32"], [("KT", gi)])
                        if gi >= 2:
                            nt = T // 128 if T >= 128 else 1
                            w_ = 128 if T >= 128 else T
                            for j in range(nt):
                                tr(PS[7][0:w_, j * 128:(j + 1) * 128], kn32[:, j * 128:j * 128 + w_], ident_f[:], ["kn32", "ident_f"], ["ps7"])
                            t0 = (gi - 2) * 4
                            cp("dve", kout[0:w_, t0:t0 + nt, :], PS[7][0:w_, 0:nt * 128].rearrange("p (t d) -> p t d", t=nt),
                               ["ps7"], ["kout"])
                    cp("pool", KTS[:, h, :], KT[:, SMP0:SMP0 + 64], [("KT", 4)], [("KTS", h)])
                    dma("sp", D["skp"][:, h * 128:(h + 1) * 128].rearrange("(t p) d -> p t d", p=128), kout[:, 0:8, :], ["kout"], [])
                    dma("sp", D["sks"][:, h * 128:(h + 1) * 128], kout[0:64, 8, :], ["kout"], [])
                    for g4 in range(5):
                        bk = pa_next()
                        tiles = list(range(g4 * 4, min(g4 * 4 + 4, 17)))
                        for ti, t in enumerate(tiles):
                            n = 128 if t < 16 else 64
                            for k in range(16):
                                mm(PS[bk][0:n, ti * 128:(ti + 1) * 128], XT[:, k, t * 128:t * 128 + n], wv[:, k, :], k == 0, k == 15,
                                   [wvk], [PSK[bk]])
                        nt = len(tiles)
                        n = 128 if g4 < 4 else 64
                        src = PS[bk][0:n, 0:nt * 128].rearrange("p (t d) -> p t d", t=nt)
                        if g4 >= 2:
                            vo = vout[0:n, g4 * 4 - 8:g4 * 4 - 8 + nt, :]
                            cp("dve", vo, src, [PSK[bk]], ["vout"])
                            cp("act", VH[0:n, g4 * 4:g4 * 4 + nt, :], vo, ["vout"], [("VH", g4)])
                        else:
                            cp("act", VH[0:n, g4 * 4:g4 * 4 + nt, :], src, [PSK[bk]], [("VH", g4)])
                    dma("sp", D["svp"][:, h * 128:(h + 1) * 128].rearrange("(t p) d -> p t d", p=128), vout[:, 0:8, :], ["vout"], [])
                    dma("sp", D["svs"][:, h * 128:(h + 1) * 128], vout[0:64, 8, :], ["vout"], [("dram", "svs")])
                    VHK = [("VH", g) for g in range(5)]
                    KTK = [("KT", g) for g in range(5)]
                    tl = [list(range(8 + 4 * q_ + 3, 7, -1)) + list(range(7, -1, -1)) for q_ in range(2)]
                    banks = ((2, 4, 5, 6), (3, 7, 1, 0))
                    for idx in range(max(len(tl[0]), len(tl[1]))):
                        for stq in range(2):
                            tiles = tl[stq]
                            if idx >= len(tiles):
                                continue
                            j = tiles[idx]
                            first, last = idx == 0, idx == len(tiles) - 1
                            zb, cb_, tb_, ob_ = banks[stq]
                            EX, exk = EXb[stq], "EX%d" % stq
                            T1, t1k = T1b[stq], "T1%d" % stq
                            Wb, wbk = Wbb[stq], "Wb%d" % stq
                            Rq, rk_ = Rrs[stq], "Rr%d" % stq
                            mm(PS[zb][:, :], KT[:, j * 128:(j + 1) * 128], QT[:, stq * 512:(stq + 1) * 512], True, True,
                               KTK + [("QT", stq * 512)], [PSK[zb]])
                            bias = sbb_pre[:, h:h + 1] if j < 8 else sbb[:, h:h + 1]
                            bk_ = "sbb_pre" if j < 8 else "sbb"
                            act(EX[:].bitcast(F32R), PS[zb][:, :], AF.Exp, [PSK[zb], bk_], [exk], bias=bias, scale=SCALE)
                            act(EX[:].bitcast(F32R), EX[:], AF.Ln, [exk], [exk], bias=1.0)
                            kd = (j - 8) - 4 * stq
                            diag = j >= 8 and kd >= 0
                            if diag:
                                tt("dve", EX[:].bitcast(F32R), EX[:], MSK[:, kd, :], ALU.mult, [exk, "MSK"], [exk])
                            mm(PS[cb_][:, :], tri_gt_r[:].bitcast(F32R), EX[:].bitcast(F32R), True, True, [exk, "tri_gt_r"], [PSK[cb_]])
                            if not last:
                                mm(PS[tb_][:, :], ones_r[:].bitcast(F32R), EX[:].bitcast(F32R), True, True, [exk, "ones_r"], [PSK[tb_]])
                            stt(T1[:], PS[zb][:, :], SCALE, EX[:], ALU.mult, ALU.subtract, [PSK[zb], exk], [t1k])
                            tt("dve", T1[:], T1[:], PS[cb_][:, :], ALU.subtract, [t1k, PSK[cb_]], [t1k])
                            if not first:
                                tt("pool", T1[:], T1[:], Rq[:], ALU.subtract, [t1k, rk_], [t1k])
                            if not last:
                                if first:
                                    cp("dve", Rq[:], PS[tb_][:, :], [PSK[tb_]], [rk_])
                                else:
                                    tt("dve", Rq[:], Rq[:], PS[tb_][:, :], ALU.add, [rk_, PSK[tb_]], [rk_])
                            act(Wb[:], T1[:], AF.Exp, [t1k, bk_], [wbk], bias=bias)
                            if diag:
                                tt("pool", Wb[:], Wb[:], MSK[:, kd, :], ALU.mult, [wbk, "MSK"], [wbk])
                            mm(PS[ob_][:, :], VH[:, j, :], Wb[:], first, last, [wbk] + VHK, [PSK[ob_]])
                            if last:
                                cp("act", CT[:, h, stq * 512:(stq + 1) * 512], PS[ob_][:, :], [PSK[ob_]], [("CT", h, stq)])
                P.emit()
                ph = ph2

            with contextlib.ExitStack() as ph2b:
                ph = ph2b
                n_gdn = 8 if stage >= 3 else 0

                def T_(name, shape, dt):
                    return sb(name, shape, dt, ph2b)

                tri_le3 = T_("tri_le3", [128, 1, 128], F32)
                cp("pool", tri_le3[:, 0, :], tri_le[:], ["tri_le"], ["tri_le3"])
                identf3 = T_("identf3", [128, 1, 128], F32)
                cp("pool", identf3[:, 0, :], ident_f[:], ["ident_f"], ["identf3"])
                MI = T_("MI", [128, 4, 128], BF16)
                MS = T_("MS", [128, 4, 128], BF16)
                MI4 = T_("MI4", [128, 16, 4], BF16)
                MS4 = T_("MS4", [128, 16, 4], BF16)
                for m_, nm_, pat_, base_ in ((MI, "MI", [[0, 4], [1, 128]], 0), (MS, "MS", [[0, 4], [1, 128]], -1),
                                             (MI4, "MI4", [[0, 16], [1, 4]], 0), (MS4, "MS4", [[0, 16], [1, 4]], -1)):
                    memset("pool", m_[:], 1.0, [nm_])
                    asel(m_[:], m_[:], pat_, ALU.is_ge, base_, -1, [nm_], [nm_])
                wgab = T_("wgab", [128, 16, 16], BF16)
                GD = T_("GD", [128, 17, 8], F32)
                BT = T_("BT", [128, 17, 8], F32)
                NBm = T_("NBm", [128, 17, 8], F32)
                gtm = T_("gtm", [128, 16], F32)
                GDS = T_("GDS", [128, 16, 24], F32)
                if n_gdn:
                    load_w(wgab, "wgab", D["w_in"], 0, 16, C_GA, 16, gn[:, 0])
                    for t in range(17):
                        n = 128 if t < 16 else 64
                        for k in range(16):
                            mm(PS[0][0:n, 0:16], XT[:, k, t * 128:t * 128 + n], wgab[:, k, :], k == 0, k == 15, ["wgab"], ["ps0"])
                        cp("dve", gtm[0:n, :], PS[0][0:n, 0:16], ["ps0"], ["gtm"])
                        tt("dve", gtm[0:n, 0:8], gtm[0:n, 0:8], dtb[0:n, :], ALU.add, ["gtm", "dtb"], ["gtm"])
                        act(gtm[0:n, 0:8], gtm[0:n, 0:8], AF.Exp, ["gtm"], ["gtm"])
                        act(gtm[0:n, 0:8], gtm[0:n, 0:8], AF.Ln, ["gtm"], ["gtm"], bias=1.0)
                        tt("dve", GD[0:n, t, :], gtm[0:n, 0:8], nea[0:n, :], ALU.mult, ["gtm", "nea"], [("GD", t)])
                        act(gtm[0:n, 8:16], gtm[0:n, 8:16], AF.Exp, ["gtm"], ["gtm"], scale=-1.0)
                        ts("dve", gtm[0:n, 8:16], gtm[0:n, 8:16], 1.0, None, ALU.add, None, ["gtm"], ["gtm"])
                        P.op("dve", (lambda o, i: (lambda e: e.reciprocal(out=o, in_=i)))(BT[0:n, t, :], gtm[0:n, 8:16]), ["gtm"], [("BT", t)])
                        ts("dve", NBm[0:n, t, :], BT[0:n, t, :], -1.0, None, ALU.mult, None, [("BT", t)], [("NBm", t)])
                    dma("sp", D["scr_gab"][:, 0:8], GD[0:64, 16, :], [("GD", 16)], [("dram", "scr")])
                    dma("sp", D["scr_gab"][:, 8:16], BT[0:64, 16, :], [("BT", 16)], [("dram", "scr2")])
                    dma("sp", GDS[0:4, :, 0:16], D["scr_gab"].rearrange("(i t) c -> t i c", t=4), [("dram", "scr"), ("dram", "scr2")], ["GDS"])
                    ts("dve", GDS[0:4, :, 16:24], GDS[0:4, :, 8:16], -1.0, None, ALU.mult, None, ["GDS"], ["GDS"])
                GDK = [("GD", t) for t in range(17)]
                BTK = [("BT", t) for t in range(17)]
                NBK = [("NBm", t) for t in range(17)]
                G3 = T_("G3", [128, 512], F32)
                CB = T_("CB", [128, 512], F32)
                ECB = T_("ECB", [128, 512], F32)
                EI = T_("EI", [128, 512], BF16)
                ES = T_("ES", [128, 512], BF16)
                CC = T_("CC", [128, 16, 1], F32)
                DEX = T_("DEX", [128, 16, 1], F32)
                GCc = T_("GCc", [128, 16, 1], F32)
                BCc = T_("BCc", [128, 16, 1], F32)
                NBc = T_("NBc", [128, 16, 1], F32)
                MA = [T_("MA%d" % i, [128, 512], F32) for i in range(2)]
                MTt = [T_("MT%d" % i, [128, 512], F32) for i in range(2)]
                X32 = T_("X32", [128, 512], F32)
                XB = T_("XB", [128, 512], BF16)
                QKD = T_("QKD", [128, 512], BF16)
                KCT = T_("KCT", [128, 512], BF16)
                QDT = T_("QDT", [128, 512], BF16)
                KTM = T_("KTM", [128, 8, 128], BF16)
                VTM = T_("VTM", [128, 8, 128], BF16)
                KEND = T_("KEND", [128, 8, 128], BF16)
                RT = T_("RT", [128, 128], BF16)
                VN = T_("VN", [128, 128], BF16)
                S32 = T_("S32", [128, 128], F32)
                Sb_ = T_("Sb_", [128, 128], BF16)
                S32A = T_("S32A", [128, 16, 128], F32)
                RAW = [T_("RAW%d" % i, [128, 515], F32) for i in range(3)]
                RAWS = [T_("RAWS%d" % i, [128, 16, 7], F32) for i in range(3)]
                ACC = T_("ACC", [128, 512], F32)
                CQ = T_("CQ", [128, 512], F32)
                QTg = T_("QTg", [128, 512], BF16)
                KTg = T_("KTg", [128, 512], BF16)
                CVb = T_("CVb", [128, 512], BF16)
                ZS = T_("ZS", [128, 512], BF16)
                gco = T_("gco", [128, 128], F32)
                t48 = T_("t48", [128, 48], F32)

                def gdn_group(C, G, own, h, gsrc, bsrc, nbsrc, gkeys, mi, ms, S_of, ctc0, c0=0, chain=True):
                    T = C * G
                    nsq = {128: 6, 4: 1}[C]

                    def v3(t):
                        return t[0:C, 0:T].rearrange("p (g c) -> p g c", g=G)

                    def p3(bank):
                        return PS[bank][0:C, 0:T].rearrange("p (g c) -> p g c", g=G)

                    cp("pool", GCc[0:C, 0:G, :], gsrc, gkeys, ["GCc"])
                    cp("pool", BCc[0:C, 0:G, :], bsrc, gkeys, ["BCc"])
                    cp("pool", NBc[0:C, 0:G, :], nbsrc, gkeys, ["NBc"])
                    tt("pool", v3(G3), GCc[0:C, 0:G, :].broadcast_to([C, G, C]), tri_le3[0:C, :, 0:C].broadcast_to([C, G, C]), ALU.mult,
                       ["GCc", "tri_le3"], ["G3"])
                    mm(PS[2][:, 0:T], ones_f[0:C, :], G3[0:C, 0:T], True, True, ["G3", "ones_f"], ["ps2"])
                    mm(PS[7][0:C, 0:G], tri_le[0:C, 0:C], GCc[0:C, 0:G, 0], True, True, ["GCc", "tri_le"], ["ps7"])
                    cp("dve", CC[0:C, 0:G, 0], PS[7][0:C, 0:G], ["ps7"], ["CC"])
                    cp("dve", CB[:, 0:T], PS[2][:, 0:T], ["ps2"], ["CB"])
                    act(ECB[:, 0:T], CB[:, 0:T], AF.Exp, ["CB"], ["ECB"])
                    cb3 = CB[0:C, 0:T].rearrange("p (g c) -> p g c", g=G)
                    tt("dve", DEX[0:C, 0:G, 0], cb3[:, :, C - 1], CC[0:C, 0:G, 0], ALU.subtract, ["CB", "CC"], ["DEX"])
                    act(DEX[0:C, 0:G, :], DEX[0:C, 0:G, :], AF.Exp, ["DEX"], ["DEX"])
                    tt("dve", v3(CB), v3(CB), CC[0:C, 0:G, :].broadcast_to([C, G, C]), ALU.subtract, ["CB", "CC"], ["CB"])
                    ts("pool", v3(CB), v3(CB), 0.0, None, ALU.min, None, ["CB"], ["CB"])
                    act(v3(CB), v3(CB), AF.Exp, ["CB"], ["CB"])
                    tt("pool", v3(EI), v3(CB), mi, ALU.mult, ["CB", "MI", "MI4"], ["EI"])
                    tt("pool", v3(ES), v3(CB), ms, ALU.mult, ["CB", "MS", "MS4"], ["ES"])
                    for g in range(G):
                        mm(PS[3][0:C, g * C:(g + 1) * C], KTg[:, c0 + g * C:c0 + (g + 1) * C], KTg[:, c0 + g * C:c0 + (g + 1) * C], True, True, ["KTg"], ["ps3"])
                    tt("dve", v3(G3), p3(3), v3(ES), ALU.mult, ["ps3", "ES"], ["G3"])
                    tt("dve", v3(MTt[0]).bitcast(F32R), v3(G3), NBc[0:C, 0:G, :].broadcast_to([C, G, C]), ALU.mult,
                       ["G3", "NBc"], ["MT0"])
                    for g in range(G):
                        mm(PS[4][0:C, g * C:(g + 1) * C], KTg[:, c0 + g * C:c0 + (g + 1) * C], QTg[:, c0 + g * C:c0 + (g + 1) * C], True, True, ["KTg", "QTg"], ["ps4"])
                    tt("dve", v3(QKD), p3(4), v3(EI), ALU.mult, ["ps4", "EI"], ["QKD"])
                    tt("pool", KCT[:, 0:T], KTg[:, c0:c0 + T], ECB[:, 0:T], ALU.mult, ["KTg", "ECB"], ["KCT"])
                    tt("pool", QDT[:, 0:T], QTg[:, c0:c0 + T], ECB[:, 0:T], ALU.mult, ["QTg", "ECB"], ["QDT"])
                    for g0 in range(0, G, 8):
                        ng = min(8, G - g0)
                        for g in range(g0, g0 + ng):
                            tr(psb(6)[0:C, (g - g0) * 128:(g - g0 + 1) * 128], KTg[:, c0 + g * C:c0 + (g + 1) * C], ident_b[:, :], ["KTg", "ident_b"], ["ps6"])
                        cp("act", KTM[0:C, g0:g0 + ng, :], psb(6)[0:C, 0:ng * 128].rearrange("p (g d) -> p g d", g=ng), ["ps6"], ["KTM"])
                        for g in range(g0, g0 + ng):
                            tr(psb(7)[0:C, (g - g0) * 128:(g - g0 + 1) * 128], CVb[:, c0 + g * C:c0 + (g + 1) * C], ident_b[:, :], ["CVb", "ident_b"], ["ps7"])
                        cp("dve", VTM[0:C, g0:g0 + ng, :], psb(7)[0:C, 0:ng * 128].rearrange("p (g d) -> p g d", g=ng), ["ps7"], ["VTM"])
                    tt("pool", KEND[0:C, 0:G, :], KTM[0:C, 0:G, :], DEX[0:C, 0:G, :].broadcast_to([C, G, 128]), ALU.mult, ["KTM", "DEX"], ["KEND"])
                    RD = F32R if C == 128 else F32

                    def rv(ap):
                        return ap.bitcast(RD) if RD is F32R else ap

                    def wv(ap):
                        return ap.bitcast(F32R)

                    for g in range(G):
                        tr(PS[3][0:C, g * C:(g + 1) * C], MTt[0][0:C, g * C:(g + 1) * C], ident_f[0:C, 0:C], ["MT0", "ident_f"], ["ps3"])
                    cp("act", wv(MA[0][0:C, 0:T]), PS[3][0:C, 0:T], ["ps3"], ["MA0"])
                    tt("dve", wv(v3(X32)), v3(MTt[0]), identf3[0:C, :, 0:C].broadcast_to([C, G, C]), ALU.add, ["MT0", "identf3"], ["X32"])
                    cur = 0
                    for r in range(nsq):
                        M, Mk = MA[cur], "MA%d" % cur
                        Mt, Mtk = MTt[cur], "MT%d" % cur
                        Mn, Mnk = MA[1 - cur], "MA%d" % (1 - cur)
                        Mtn, Mtnk = MTt[1 - cur], "MT%d" % (1 - cur)
                        for g in range(G):
                            sl = slice(g * C, (g + 1) * C)
                            mm(PS[3][0:C, sl], rv(Mt[0:C, sl]), rv(M[0:C, sl]), True, True, [Mk, Mtk], ["ps3"])
                        if r < nsq - 1:
                            for g in range(G):
                                sl = slice(g * C, (g + 1) * C)
                                mm(PS[4][0:C, sl], rv(M[0:C, sl]), rv(Mt[0:C, sl]), True, True, [Mk, Mtk], ["ps4"])
                        cp("act", wv(Mn[0:C, 0:T]), PS[3][0:C, 0:T], ["ps3"], [Mnk])
                        if r < nsq - 1:
                            cp("dve", wv(Mtn[0:C, 0:T]), PS[4][0:C, 0:T], ["ps4"], [Mtnk])
                        for g in range(G):
                            sl = slice(g * C, (g + 1) * C)
                            mm(PS[5][0:C, sl], rv(Mn[0:C, sl]), rv(X32[0:C, sl]), True, True, [Mnk, "X32"], ["ps5"])
                        tt("dve", wv(X32[0:C, 0:T]), X32[0:C, 0:T], PS[5][0:C, 0:T], ALU.add, ["X32", "ps5"], ["X32"])
                        cur = 1 - cur
                    cp("act", XB[0:C, 0:T], X32[0:C, 0:T], ["X32"], ["XB"])
                    for g in range(G):
                        S32g, Sbg, skey, sbkey = S_of(g)
                        sl = slice(g * C, (g + 1) * C)
                        if not chain:
                            cp("act", Sbg, S32g, [skey], [sbkey])
                        kb = pa_next()
                        mm(PS[kb][0:C, 0:128], KCT[:, sl], Sbg, True, True, ["KCT", sbkey], [PSK[kb]])
                        tt("dve", RT[0:C, :], VTM[0:C, g, :], PS[kb][0:C, 0:128], ALU.subtract, ["VTM", PSK[kb]], ["RT"])
                        kb2 = pa_next()
                        mm(PS[kb2][0:C, 0:128], XB[0:C, sl], RT[0:C, :], True, True, ["XB", "RT"], [PSK[kb2]])
                        ts("dve", VN[0:C, :], PS[kb2][0:C, 0:128], BCc[0:C, g, :], None, ALU.mult, None, [PSK[kb2], "BCc"], ["VN"])
                        if own:
                            mm(PS[6][:, sl], Sbg, QDT[:, sl], True, False, [sbkey, "QDT"], ["ps6"])
                            mm(PS[6][:, sl], VN[0:C, :], QKD[0:C, sl], False, True, ["VN", "QKD"], ["ps6"])
                        mm(PS[7][:, 0:128], KEND[0:C, g, :], VN[0:C, :], True, True, ["KEND", "VN"], ["ps7"])
                        stt(S32g, S32g, ECB[:, g * C + C - 1:g * C + C], PS[7][:, 0:128], ALU.mult, ALU.add, [skey, "ECB", "ps7"], [skey])
                        if chain:
                            cp("act", Sbg, S32g, [skey], [sbkey])
                    if own:
                        fm_norm("ps6", PS[6][:, 0:T], T, hg[:, 4:5], ACC[:, 0:T], ["ACC"], st2, ["hg"])
                        tt("pool", CT[:, 8 + h, ctc0:ctc0 + T], ACC[:, 0:T], ZS[:, c0:c0 + T], ALU.mult, ["ACC", "ZS"], [("CT", 8 + h, ctc0)])

                def conv_silu(flat_in, L, cidx, out_ap_fn, key_in):
                    ts("dve", ACC[:, 0:L], flat_in[:, 3:3 + L], CW[:, cidx, 3:4], None, ALU.mult, None, [key_in, "CW"], ["ACC"])
                    for j in (2, 1, 0):
                        stt(ACC[:, 0:L], flat_in[:, j:j + L], CW[:, cidx, j:j + 1], ACC[:, 0:L], ALU.mult, ALU.add, [key_in, "CW", "ACC"], ["ACC"])

                for h in range(n_gdn):
                    ws = []
                    for c0_ in (C_GQ, C_GK, C_GV, C_GZ):
                        w_, wk_ = wb_next()
                        load_w(w_, wk_, D["w_in"], 0, 16, c0_ + h * 128, 128, gn[:, 0])
                        ws.append((w_, wk_))
                    memset("pool", S32[:], 0.0, ["S"])
                    memset("pool", Sb_[:], 0.0, [("S", "b")])
                    for i in range(3):
                        memset("pool", RAW[i][:, 0:3], 0.0, ["RAW%d" % i])
                    for grp in range(4):
                        own = grp >= 2
                        xc0 = grp * 512
                        if grp == 2:
                            for i in range(3):
                                ts("dve", RAW[i][:, 0:3], RAW[i][:, 0:3], flg[:, 0:1], None, ALU.mult, None, ["RAW%d" % i, "flg"], ["RAW%d" % i])
                            ts("dve", S32[:], S32[:], flg[:, 0:1], None, ALU.mult, None, ["S", "flg"], ["S"])
                            cp("act", Sb_[:], S32[:], ["S"], [("S", "b")])
                        for comp in range(3):
                            w_, wk_ = ws[comp]
                            rk = "RAW%d" % comp
                            bk = pa_next()
                            proj_fm(bk, w_, wk_, xc0, 512)
                            cp("act", RAW[comp][:, 3:515], PS[bk][:, 0:512], [PSK[bk]], [rk])
                            cidx = comp * 8 + h
                            if grp == 3:
                                tr(PS[7][0:3, 0:128], RAW[comp][:, 512:515], ident_f[:], [rk, "ident_f"], ["ps7"])
                                cp("dve", gco[0:3, :], PS[7][0:3, 0:128], ["ps7"], ["gco"])
                                dma("sp", D["gcp"][:, cidx * 128:(cidx + 1) * 128], gco[0:3, :], ["gco"], [])
                            conv_silu(RAW[comp], 512, cidx, None, rk)
                            cp("pool", RAW[comp][:, 0:3], RAW[comp][:, 512:515], [rk], [rk])
                            if comp == 0:
                                act(CQ[:, :], ACC[:, :], AF.Silu, ["ACC"], ["CQ"])
                                fm_norm("CQ", CQ[:, :], 512, 128.0 ** -0.5, QTg[:, :], ["QTg"], st2, mean=False)
                            elif comp == 1:
                                act(CQ[:, :], ACC[:, :], AF.Silu, ["ACC"], ["CQ"])
                                fm_norm("CQ", CQ[:, :], 512, 1.0, KTg[:, :], ["KTg"], st2, mean=False)
                            else:
                                act(CVb[:, :], ACC[:, :], AF.Silu, ["ACC"], ["CVb"])
                        if own:
                            bk = pa_next()
                            proj_fm(bk, ws[3][0], ws[3][1], xc0, 512)
                            act(ZS[:, :], PS[bk][:, 0:512], AF.Silu, [PSK[bk]], ["ZS"])
                        t0 = grp * 4
                        gdn_group(128, 4, own, h, GD[:, t0:t0 + 4, h:h + 1], BT[:, t0:t0 + 4, h:h + 1], NBm[:, t0:t0 + 4, h:h + 1],
                                  GDK + BTK + NBK, MI[:, :, :], MS[:, :, :], lambda g: (S32[:], Sb_[:], "S", ("S", "b")), (grp - 2) * 512)
                    dma("sp", D["gsp"][h], S32[:], ["S"], [])
                    dma("sp", S32A[:], D["sg"][:, h].rearrange("i d e -> d i e"), [], [("SA", g) for g in range(16)])
                    for comp in range(3):
                        w_, wk_ = ws[comp]
                        rk = "RAWS%d" % comp
                        cidx = comp * 8 + h
                        bk = pa_next()
                        proj_fm(bk, w_, wk_, SMP0, 64)
                        cp("act", RAWS[comp][:, :, 3:7], PS[bk][:, 0:64].rearrange("p (i t) -> p i t", t=4), [PSK[bk]], [rk])
                        cp("dve", RAWS[comp][:, :, 0:3], SCT[:, cidx, :].rearrange("p (i r) -> p i r", r=3), ["SCT"], [rk])
                        cp("pool", t48[:, :].rearrange("p (i r) -> p i r", r=3), RAWS[comp][:, :, 4:7], [rk], ["t48"])
                        tr(PS[7][0:48, 0:128], t48[:, :], ident_f[:], ["t48", "ident_f"], ["ps7"])
                        cp("dve", gco[0:48, :], PS[7][0:48, 0:128], ["ps7"], ["gco"])
                        dma("sp", D["gcs"][:, cidx * 128:(cidx + 1) * 128], gco[0:48, :], ["gco"], [])
                        flat = RAWS[comp][:, :, :].rearrange("p i s -> p (i s)")
                        conv_silu(flat, 109, cidx, None, rk)
                        accv = ACC[:, 0:112].rearrange("p (i s) -> p i s", s=7)[:, :, 0:4]
                        if comp == 0:
                            act(CQ[:, 0:64].rearrange("p (i t) -> p i t", t=4), accv, AF.Silu, ["ACC"], ["CQ"])
                            fm_norm("CQ", CQ[:, 0:64], 64, 128.0 ** -0.5, QTg[:, 0:64], ["QTg"], st2, mean=False)
                        elif comp == 1:
                            act(CQ[:, 0:64].rearrange("p (i t) -> p i t", t=4), accv, AF.Silu, ["ACC"], ["CQ"])
                            fm_norm("CQ", CQ[:, 0:64], 64, 1.0, KTg[:, 0:64], ["KTg"], st2, mean=False)
                        else:
                            act(CVb[:, 0:64].rearrange("p (i t) -> p i t", t=4), accv, AF.Silu, ["ACC"], ["CVb"])
                    bk = pa_next()
                    proj_fm(bk, ws[3][0], ws[3][1], SMP0, 64)
                    act(ZS[:, 0:64], PS[bk][:, 0:64], AF.Silu, [PSK[bk]], ["ZS"])
                    for i0 in (0, 8):
                        gdn_group(4, 8, True, h, GDS[0:4, i0:i0 + 8, h:h + 1], GDS[0:4, i0:i0 + 8, 8 + h:9 + h],
                                  GDS[0:4, i0:i0 + 8, 16 + h:17 + h], ["GDS"], MI4[0:4, 0:8, :], MS4[0:4, 0:8, :],
                                  (lambda i0_: (lambda g: (S32A[:, i0_ + g, :], Sb_[:], ("SA", i0_ + g), ("S", "b"))))(i0), 1024 + i0 * 4,
                                  c0=i0 * 4, chain=False)
                    dma("sp", D["gss"][:, h].rearrange("i d e -> d i e"), S32A[:], [("SA", g) for g in range(16)], [])
                P.emit()

        with contextlib.ExitStack() as ph2c:
            n_ssb = 16 if stage >= 4 else 0

            def U_(name, shape, dt):
                return sb(name, shape, dt, ph2c)

            ptb = U_("ptb", [128, 256], I32)
            IDX = U_("IDX", [128, 256], I32)
            pid = U_("pid", [128, 1], F32)
            BH512 = U_("BH512", [128, 16, 8, 4], F32)
            MN = U_("MN", [128, 8, 4], F32)
            KPb = [U_("KPb%d" % i, [128, 1024], BF16) for i in range(2)]
            KTall = U_("KTall", [128, 16, 8, 128], BF16)
            VBall = U_("VBall", [128, 16, 1024], BF16)
            VNb = U_("VNb", [128, 1024], BF16)
            ZP = U_("ZP", [128, 512], F32)
            SPt = U_("SPt", [128, 512], F32)
            TOa = U_("TOa", [128, 17, 32], F32)
            TOb = U_("TOb", [128, 17, 32], F32)
            Wsb = U_("Wsb", [128, 512], BF16)
            OACC = U_("OACC", [128, 32], F32)
            if n_ssb:
                dma("sp", ptb[:], D["pt"][0:1, :].partition_broadcast(128), [], ["ptb"])
                P.op("pool", lambda e: e.iota(pid[:], pattern=[[0, 1]], base=0, channel_multiplier=1, allow_small_or_imprecise_dtypes=True), [], ["pid"])
                ptf = SPt[:, 0:256]
                cp("dve", ptf, ptb[:], ["ptb"], ["SPt"])
                ts("dve", ptf, ptf, 128.0, pid[:, 0:1], ALU.mult, ALU.add, ["SPt", "pid"], ["SPt"])
                cp("dve", IDX[:], ptf, ["SPt"], ["IDX"])
                for j in range(16):
                    cp("dve", BH512[:, j, :, :], sbb[:, :].rearrange("p (h o) -> p h o", o=1).broadcast_to([128, 8, 4]), ["sbb"], ["BH512"])
                memset("pool", MN[:], 1.0, ["MN"])
                asel(MN[:], MN[:], [[0, 8], [1, 4]], ALU.is_ge, -1, -1, ["MN"], ["MN"])
                memset("pool", TOa[:], 0.0, ["TOa"])
                memset("pool", TOb[:], 0.0, ["TOb"])
            BHf = BH512[:].rearrange("p j h t -> p (j h t)")
            MN2 = MN[:].rearrange("p h t -> p (h t)")

            for i in range(n_ssb):
                for j in range(16):
                    b = j % 2
                    col = i * 16 + j
                    P.op("pool", (lambda o, ix: (lambda e: e.indirect_dma_start(out=o, out_offset=None, in_=D["ck"],
                         in_offset=bass.IndirectOffsetOnAxis(ap=ix, axis=0))))(KPb[b][:, :], IDX[:, col:col + 1]), ["IDX"], ["KPb%d" % b], dma=True)
                    P.op("pool", (lambda o, ix: (lambda e: e.indirect_dma_start(out=o, out_offset=None, in_=D["cv"],
                         in_offset=bass.IndirectOffsetOnAxis(ap=ix, axis=0))))(VBall[:, j, :], IDX[:, col:col + 1]), ["IDX"], [("VB", j)], dma=True)
                    bank = j % 2
                    pv = psb(bank)
                    for h in range(8):
                        tr(pv[:, h * 128:(h + 1) * 128], KPb[b][:, h * 128:(h + 1) * 128], ident_b[:], ["KPb%d" % b, "ident_b"], [PSK[bank]])
                    cp("act" if j % 2 == 0 else "dve", KTall[:, j, :, :], pv[:, 0:1024].rearrange("p (h s) -> p h s", h=8), [PSK[bank]], [("KT", j)])
                dma("pool", VNb[0:4, :], D["svs"][4 * i:4 * i + 4, :], [], ["VNb"])
                for h in range(8):
                    mm(PS[2][0:4, h * 4:(h + 1) * 4], KTS[:, h, 4 * i:4 * i + 4], QTS[:, h, 4 * i:4 * i + 4], True, True, [], ["ps2"])
                stt(ZP[0:4, 0:32], PS[2][0:4, 0:32], SCALE, BHf[0:4, 0:32], ALU.mult, ALU.add, ["ps2", "BH512"], ["ZP"])
                act(SPt[0:4, 0:32], ZP[0:4, 0:32], AF.Exp, ["ZP"], ["SPt"])
                act(SPt[0:4, 0:32], SPt[0:4, 0:32], AF.Ln, ["SPt"], ["SPt"], bias=1.0)
                tt("dve", SPt[0:4, 0:32], SPt[0:4, 0:32], MN2[0:4, :], ALU.mult, ["SPt", "MN"], ["SPt"])
                mm(PS[3][0:4, 0:32], tri_gt[0:4, 0:4], SPt[0:4, 0:32], True, True, ["SPt", "tri_gt"], ["ps3"])
                mm(PS[4][:, 0:32], ones_f[0:4, :], SPt[0:4, 0:32], True, True, ["SPt", "ones_f"], ["ps4"])
                tt("dve", ZP[0:4, 0:32], ZP[0:4, 0:32], SPt[0:4, 0:32], ALU.subtract, ["ZP", "SPt"], ["ZP"])
                tt("dve", ZP[0:4, 0:32], ZP[0:4, 0:32], PS[3][0:4, 0:32], ALU.subtract, ["ZP", "ps3"], ["ZP"])
                cp("dve", TOa[:, 16, :], PS[4][:, 0:32], ["ps4"], ["TOa"])
                act(Wsb[0:4, 0:32], ZP[0:4, 0:32], AF.Exp, ["ZP"], ["Wsb"])
                tt("dve", Wsb[0:4, 0:32], Wsb[0:4, 0:32], MN2[0:4, :], ALU.mult, ["Wsb", "MN"], ["Wsb"])
                for h in range(8):
                    mm(PS[5][:, h * 4:(h + 1) * 4], VNb[0:4, h * 128:(h + 1) * 128], Wsb[0:4, h * 4:(h + 1) * 4], True, True, ["Wsb", "VNb"], ["ps5"])
                cp("dve", OACC[:], PS[5][:, 0:32], ["ps5"], ["OACC"])
                KTK_ = [("KT", j) for j in range(16)]
                for j in range(16):
                    for h in range(8):
                        c = (j * 8 + h) * 4
                        mm(PS[2][:, c:c + 4], KTall[:, j, h, :], QTS[:, h, 4 * i:4 * i + 4], True, True, KTK_, ["ps2"])
                stt(ZP[:, :], PS[2][:, :], SCALE, BHf[:, :], ALU.mult, ALU.add, ["ps2", "BH512"], ["ZP"])
                act(SPt[:, :], ZP[:, :], AF.Exp, ["ZP"], ["SPt"])
                act(SPt[:, :], SPt[:, :], AF.Ln, ["SPt"], ["SPt"], bias=1.0)
                mm(PS[3][:, :], tri_gt[:], SPt[:, :], True, True, ["SPt", "tri_gt"], ["ps3"])
                mm(PS[4][:, :], ones_f[:], SPt[:, :], True, True, ["SPt", "ones_f"], ["ps4"])
                tt("dve", ZP[:, :], ZP[:, :], SPt[:, :], ALU.subtract, ["ZP", "SPt"], ["ZP"])
                tt("dve", ZP[:, :], ZP[:, :], PS[3][:, :], ALU.subtract, ["ZP", "ps3"], ["ZP"])
                cp("dve", TOa[:, 0:16, :], PS[4][:, :].rearrange("p (j c) -> p j c", j=16), ["ps4"], ["TOa"])
                src_, sk_, dst_, dk_ = TOa, "TOa", TOb, "TOb"
                for sh in (1, 2, 4, 8, 16):
                    n_ = 17 - sh
                    tt("pool", dst_[:, 0:n_, :], src_[:, 0:n_, :], src_[:, sh:17, :], ALU.add, [sk_], [dk_])
                    cp("pool", dst_[:, n_:17, :], src_[:, n_:17, :], [sk_], [dk_])
                    src_, sk_, dst_, dk_ = dst_, dk_, src_, sk_
                tt("dve", ZP[:, :].rearrange("p (j c) -> p j c", j=16), ZP[:, :].rearrange("p (j c) -> p j c", j=16), src_[:, 1:17, :],
                   ALU.subtract, ["ZP", sk_], ["ZP"])
                act(Wsb[:, :], ZP[:, :], AF.Exp, ["ZP"], ["Wsb"])
                VBK_ = [("VB", j) for j in range(16)]
                for h in range(8):
                    for j in range(16):
                        c = (j * 8 + h) * 4
                        mm(PS[5][:, h * 4:(h + 1) * 4], VBall[:, j, h * 128:(h + 1) * 128], Wsb[:, c:c + 4], j == 0, j == 15, ["Wsb"] + VBK_, ["ps5"])
                tt("dve", OACC[:], OACC[:], PS[5][:, 0:32], ALU.add, ["OACC", "ps5"], ["OACC"])
                cp("act", CT[:, 0:8, 1024 + 4 * i:1024 + 4 * i + 4], OACC[:].rearrange("p (h t) -> p h t", t=4), ["OACC"], [("CTs", i)])
            P.emit()

        if stage >= 5:
            X1 = BIG.bitcast(F32).reshape([128, 8, 2112])
            with contextlib.ExitStack() as ph3:
                X1s = sb("X1s", [128, 2048], F32, ph3)

                def x1_tile(t):
                    if t < 8:
                        return X1[:, t, 0:2048], 128, ("X1", t)
                    return X1s[0:64, :], 64, ("X1", 8)

                def ct_cols(t):
                    return (t * 128, 128) if t < 8 else (1024, 64)

                with contextlib.ExitStack() as pa3:
                    WO = [sb("WO%d" % i, [128, 16, 512], BF16, pa3) for i in range(2)]
                    xres = [sb("xres%d" % i, [128, 512], F32, pa3) for i in range(2)]
                    cnt = 0
                    for n in range(4):
                        wo, wok = WO[n % 2], "WO%d" % (n % 2)
                        load_w(wo, wok, D["w_out"], 0, 16, n * 512, 512, None)
                        for t in range(9):
                            xa, nn, xk = x1_tile(t)
                            c0, _ = ct_cols(t)
                            xr, xrk = xres[cnt % 2], "xres%d" % (cnt % 2)
                            bk = cnt % 2
                            cnt += 1
                            src = D["xp"][1024 + t * 128:1024 + (t + 1) * 128, n * 512:(n + 1) * 512] if t < 8 else D["xs"][:, n * 512:(n + 1) * 512]
                            dma("sp", xr[0:nn, :], src, [], [xrk])
                            for k in range(16):
                                mm(PS[bk][0:nn, :], CT[:, k, c0:c0 + nn], wo[:, k, :], k == 0, k == 15, [wok], [PSK[bk]])
                            tt("dve", xa[:, n * 512:(n + 1) * 512], PS[bk][0:nn, :], xr[0:nn, :], ALU.add, [PSK[bk], xrk], [xk + (n,)])
                    P.emit()

                def norm_to_ct(st):
                    for t in range(9):
                        xa, nn, xk = x1_tile(t)
                        c0, _ = ct_cols(t)
                        junk, ss, xb = st["junk"], st["ss"], st["xb"]
                        act(xb[0:nn, :], xa, AF.Square, [xk], ["xb", "ss"], accum=ss[0:nn, :])
                        act(ss[0:nn, :], ss[0:nn, :], AF.Ln, ["ss"], ["ss"], bias=EPS, scale=1.0 / 2048)
                        act(ss[0:nn, :], ss[0:nn, :], AF.Exp, ["ss"], ["ss"], scale=-0.5)
                        ts("dve", xb[0:nn, :], xa, ss[0:nn, 0:1], None, ALU.mult, None, [xk, "ss"], ["xb"])
                        for half in range(2):
                            bank = 6 + half
                            pv = psb(bank)
                            for k in range(8):
                                kk = half * 8 + k
                                tr(pv[:, k * 128:k * 128 + nn], xb[0:nn, kk * 128:(kk + 1) * 128], ident_b[0:nn, 0:nn], ["xb", "ident_b"], [PSK[bank]])
                            srcv = pv[:, 0:1024].rearrange("p (a b) -> p a b", a=8)[:, :, 0:nn]
                            cp("act" if half == 0 else "dve", CT[:, half * 8:half * 8 + 8, c0:c0 + nn], srcv, [PSK[bank]], [("CT", t, half)])

                CTK = [("CT", t, hf_) for t in range(9) for hf_ in range(2)]
                with contextlib.ExitStack() as pb3:
                    st3 = {"junk": None, "ss": sb("ss3", [128, 1], F32, pb3), "xb": sb("xb3", [128, 2048], BF16, pb3),
                           "sq": sb("sq3", [128, 512], BF16, pb3), "rr": sb("rr3", [128, 512], F32, pb3)}
                    XQ = sb("XQ", [128, 4, NTO], BF16, pb3)
                    XO = sb("XO", [128, 4, NTO], BF16, pb3)
                    wxq = [sb("wxq%d" % i, [128, 16, 128], BF16, pb3) for i in range(2)]
                    WXO = sb("WXO", [128, 4, 2048], BF16, pb3)
                    EM = [sb("EM%d" % i, [128, 512], BF16, pb3) for i in range(2)]
                    rden = sb("rden", [128, 512], F32, pb3)
                    CK = [sb("CK%d" % i, [128, 2, 512], F32, pb3) for i in range(2)]
                    CV = [sb("CVm%d" % i, [128, 2, 512], F32, pb3) for i in range(2)]
                    KTm = sb("KTm", [128, 4, 256], BF16, pb3)
                    Vbm = sb("Vbm", [128, 2, 512], BF16, pb3)
                    Es = sb("Es", [128, 32], BF16, pb3)
                    norm_to_ct(st3)
                    for n_ in range(4):
                        load_w(WXO, "WXO", D["w_xo"], 0, 4, n_ * 512, 512, None, dcol0=n_ * 512)
                    for h in range(4):
                        w_, wk_ = wxq[h % 2], "wxq%d" % (h % 2)
                        load_w(w_, wk_, D["w_xq"], 0, 16, h * 128, 128, gn[:, 1])
                        for c0, T in ((0, 512), (512, 512), (1024, 64)):
                            bk = pa_next()
                            for k in range(16):
                                mm(PS[bk][:, 0:T], w_[:, k, :], CT[:, k, c0:c0 + T], k == 0, k == 15, [wk_] + CTK, [PSK[bk]])
                            fm_norm(PSK[bk], PS[bk][:, 0:T], T, hg[:, 2:3], XQ[:, h, c0:c0 + T], [("XQ", h, c0)], st3, ["hg"])
                        for g in range(2):
                            c0 = g * 512
                            for mt in range(2):
                                mm(PS[2 + mt][:, :], MKT[:, h, mt * 128:(mt + 1) * 128], XQ[:, h, c0:c0 + 512], True, True, [("XQ", h, c0)], [PSK[2 + mt]])
                                act(EM[mt][:, :], PS[2 + mt][:, :], AF.Exp, [PSK[2 + mt]], ["EM%d" % mt], scale=SCALE)
                            for mt in range(2):
                                mm(PS[4][:, :], ones_b[:], EM[mt][:, :], mt == 0, mt == 1, ["EM%d" % mt, "ones_b"], ["ps4"])
                            for mt in range(2):
                                mm(PS[5][:, :], MV[:, mt, h * 128:(h + 1) * 128], EM[mt][:, :], mt == 0, mt == 1, ["EM%d" % mt], ["ps5"])
                            P.op("dve", (lambda o, i_: (lambda e: e.reciprocal(out=o, in_=i_)))(rden[:, :], PS[4][:, :]), ["ps4"], ["rden"])
                            tt("dve", XO[:, h, c0:c0 + 512], PS[5][:, :], rden[:, :], ALU.mult, ["ps5", "rden"], [("XO", h, c0)])
                    XQK = [("XQ", h, 1024) for h in range(4)]
                    for i in range(16):
                        ck_, ckk = CK[i % 2], "CK%d" % (i % 2)
                        cv_, cvk = CV[i % 2], "CVm%d" % (i % 2)
                        dma("sp", ck_[:], D["cmk"][i].rearrange("(t p) c -> p t c", p=128), [], [ckk])
                        dma("sp", cv_[:], D["cmv"][i].rearrange("(t p) c -> p t c", p=128), [], [cvk])
                        for mt in range(2):
                            for h in range(4):
                                tr(PS[mt][:, h * 128:(h + 1) * 128], ck_[:, mt, h * 128:(h + 1) * 128], ident_f[:], [ckk, "ident_f"], [PSK[mt]])
                            cp("act" if mt == 0 else "dve", KTm[:, :, mt * 128:(mt + 1) * 128], PS[mt][:, :].rearrange("p (h m) -> p h m", h=4),
                               [PSK[mt]], [("KTm", mt)])
                        cp("pool", Vbm[:], cv_[:], [cvk], ["Vbm"])
                        for mt in range(2):
                            for h in range(4):
                                mm(PS[2][:, mt * 16 + h * 4:mt * 16 + h * 4 + 4], KTm[:, h, mt * 128:(mt + 1) * 128], XQ[:, h, 1024 + 4 * i:1024 + 4 * i + 4],
                                   True, True, [("KTm", 0), ("KTm", 1)] + XQK, ["ps2"])
                        act(Es[:, :], PS[2][:, 0:32], AF.Exp, ["ps2"], ["Es"], scale=SCALE)
                        for mt in range(2):
                            mm(PS[4][:, 0:16], ones_b[:], Es[:, mt * 16:(mt + 1) * 16], mt == 0, mt == 1, ["Es", "ones_b"], ["ps4"])
                        for h in range(4):
                            for mt in range(2):
                                mm(PS[5][:, h * 4:(h + 1) * 4], Vbm[:, mt, h * 128:(h + 1) * 128], Es[:, mt * 16 + h * 4:mt * 16 + h * 4 + 4],
                                   mt == 0, mt == 1, ["Es", "Vbm"], ["ps5"])
                        P.op("dve", (lambda o, i_: (lambda e: e.reciprocal(out=o, in_=i_)))(rden[:, 0:16], PS[4][:, 0:16]), ["ps4"], ["rden"])
                        tt("dve", XO[:, :, 1024 + 4 * i:1024 + 4 * i + 4], PS[5][:, 0:16].rearrange("p (h t) -> p h t", t=4),
                           rden[:, 0:16].rearrange("p (h t) -> p h t", t=4), ALU.mult, ["ps5", "rden"], [("XOs", i)])
                    XOK = [("XO", h, c0) for h in range(4) for c0 in (0, 512)] + [("XOs", i) for i in range(16)]
                    for t in range(9):
                        xa, nn, xk = x1_tile(t)
                        c0, _ = ct_cols(t)
                        for n in range(4):
                            bk = pa_next()
                            for k in range(4):
                                mm(PS[bk][0:nn, :], XO[:, k, c0:c0 + nn], WXO[:, k, n * 512:(n + 1) * 512], k == 0, k == 3, ["WXO"] + XOK, [PSK[bk]])
                            tt("dve", xa[:, n * 512:(n + 1) * 512], xa[:, n * 512:(n + 1) * 512], PS[bk][0:nn, :], ALU.add, [PSK[bk], xk], [xk])
                    P.emit()

                with contextlib.ExitStack() as pc3:
                    st4 = {"junk": None, "ss": sb("ss4", [128, 1], F32, pc3), "xb": sb("xb4", [128, 2048], BF16, pc3)}
                    norm_to_ct(st4)
                    P.emit()
                with contextlib.ExitStack() as pc3:
                    HT = sb("HT", [128, 44, 576], BF16, pc3)
                    FH = 5632
                    for half in range(2):
                        segs = ((0, 512, 0), (1024, 64, 512)) if half == 0 else ((512, 512, 0),)
                        with contextlib.ExitStack() as pg3:
                            WG = [sb("WG%d_%d" % (i, half), [128, 16, 128], BF16, pg3) for i in range(4)]
                            SG = [sb("SG%d_%d" % (i, half), [128, 576], F32, pg3) for i in range(2)]
                            for hc in range(44):
                                wg, wgk = WG[(2 * hc) % 4], "WG%d" % ((2 * hc) % 4)
                                wu, wuk = WG[(2 * hc + 1) % 4], "WG%d" % ((2 * hc + 1) % 4)
                                load_w(wg, wgk, D["w_gate_up"], 0, 16, hc * 128, 128, gn[:, 2])
                                load_w(wu, wuk, D["w_gate_up"], 0, 16, FH + hc * 128, 128, gn[:, 2])
                                sg, sgk = SG[hc % 2], "SG%d" % (hc % 2)
                                pb_ = (hc % 2) * 4
                                for si_, (c0, T, o0) in enumerate(segs):
                                    bg, bu = pb_ + 2 * si_, pb_ + 2 * si_ + 1
                                    for k in range(16):
                                        mm(PS[bg][:, 0:T], wg[:, k, :], CT[:, k, c0:c0 + T], k == 0, k == 15, [wgk], [PSK[bg]])
                                    for k in range(16):
                                        mm(PS[bu][:, 0:T], wu[:, k, :], CT[:, k, c0:c0 + T], k == 0, k == 15, [wuk], [PSK[bu]])
                                    act(sg[:, o0:o0 + T], PS[bg][:, 0:T], AF.Silu, [PSK[bg]], [(sgk, o0)])
                                    tt("dve", HT[:, hc, o0:o0 + T], PS[bu][:, 0:T], sg[:, o0:o0 + T], ALU.mult, [PSK[bu], (sgk, o0)], [("HT", hc, o0)])
                            P.emit()
                        with contextlib.ExitStack() as pd3:
                            WD = [sb("WD%d_%d" % (i, half), [128, 44, 128], BF16, pd3) for i in range(2)]
                            YT = [sb("YT%d_%d" % (i, half), [128, 576], F32, pd3) for i in range(2)]
                            for ocb in range(16):
                                wd, wdk = WD[ocb % 2], "WD%d" % (ocb % 2)
                                load_w(wd, wdk, D["w_down"], 0, 44, ocb * 128, 128, None)
                                yt, ytk = YT[ocb % 2], "YT%d" % (ocb % 2)
                                b0 = (ocb % 2) * 2
                                for si_, (c0, T, o0) in enumerate(segs):
                                    bk = b0 + si_
                                    for k in range(44):
                                        mm(PS[bk][:, 0:T], wd[:, k, :], HT[:, k, o0:o0 + T], k == 0, k == 43, [wdk], [PSK[bk]])
                                    cp("act", yt[:, o0:o0 + T], PS[bk][:, 0:T], [PSK[bk]], [(ytk, o0)])
                                tb = 4 + (ocb % 2)
                                for j in range(4):
                                    tr(PS[tb][:, j * 128:(j + 1) * 128], yt[:, j * 128:(j + 1) * 128], ident_f[:], [(ytk, 0), "ident_f"], [PSK[tb]])
                                for j in range(4):
                                    t = half * 4 + j
                                    xs_ = X1[:, t, ocb * 128:(ocb + 1) * 128]
                                    tt("dve", xs_, xs_, PS[tb][:, j * 128:(j + 1) * 128], ALU.add, [PSK[tb], ("X1", t)], [("X1", t)])
                                if half == 0:
                                    tr(PS[6 + (ocb % 2)][0:64, 0:128], yt[:, 512:576], ident_f[:], [(ytk, 512), "ident_f"], [PSK[6 + (ocb % 2)]])
                                    xs_ = X1s[0:64, ocb * 128:(ocb + 1) * 128]
                                    tt("dve", xs_, xs_, PS[6 + (ocb % 2)][0:64, 0:128], ALU.add, [PSK[6 + (ocb % 2)], ("X1", 8)], [("X1", 8)])
                            for j in range(4):
                                t = half * 4 + j
                                dma("sp", D["yp"][t * 128:(t + 1) * 128, :], X1[:, t, 0:2048], [("X1", t)], [])
                            P.emit()
                    dma("sp", D["ys"][:, :], X1s[0:64, :], [("X1", 8)], [])
                    P.emit()

        P.emit(final=True)
    return nc


def _prep_inputs(inp):
    n_phys = inp["cache_sb_k"].shape[1]
    ck = np.ascontiguousarray(inp["cache_sb_k"][0]).reshape(n_phys * 128, 1024)
    cv = np.ascontiguousarray(inp["cache_sb_v"][0]).reshape(n_phys * 128, 1024)
    maps = []
    for c in range(8):
        b, hf = c // 2, c % 2
        xp = np.concatenate([inp["x_prompt"][b, 0:1024], inp["x_prompt"][b, hf * 1024:hf * 1024 + 1024]], axis=0)
        m = {
            "xp": np.ascontiguousarray(xp, dtype=np.float32),
            "xs": np.ascontiguousarray(inp["x_sample"][16 * c:16 * c + 16]).reshape(64, 2048),
            "mem": np.ascontiguousarray(inp["mem_prompt"][b]),
            "ck": ck, "cv": cv,
            "pt": np.ascontiguousarray(inp["page_table"][16 * c:16 * c + 16]).reshape(1, 256).astype(np.int32),
            "sg": np.ascontiguousarray(inp["state_gdn"][0, 16 * c:16 * c + 16]),
            "sc": np.ascontiguousarray(inp["state_gdn_conv"][0, 16 * c:16 * c + 16]).reshape(48, 3072),
            "cmk": np.ascontiguousarray(inp["cache_mem_k"][0, 16 * c:16 * c + 16]).reshape(16, 256, 512),
            "cmv": np.ascontiguousarray(inp["cache_mem_v"][0, 16 * c:16 * c + 16]).reshape(16, 256, 512),
            "flags": np.array([[float(hf), 0.0 if hf else NEG, 0.0, 0.0]], np.float32),
        }
        for n, s in W_NAMES:
            m[n] = np.ascontiguousarray(inp[n][0]).reshape(s)
        maps.append(m)
    return n_phys, maps


def _assemble(res):
    R = res.results
    f = np.float32
    yp = np.zeros((4, 2048, 2048), f); skp = np.zeros((1, 4, 2048, 8, 128), f); svp = np.zeros((1, 4, 2048, 8, 128), f)
    ys = np.zeros((128, 4, 2048), f); sks = np.zeros((1, 128, 4, 8, 128), f); svs = np.zeros((1, 128, 4, 8, 128), f)
    gsp = np.zeros((1, 4, 8, 128, 128), f); gcp = np.zeros((1, 4, 3, 3072), f)
    gss = np.zeros((1, 128, 8, 128, 128), f); gcs = np.zeros((1, 128, 3, 3072), f)
    mkp = np.zeros((1, 4, 256, 4, 128), f); mvp = np.zeros((1, 4, 256, 4, 128), f)
    for c in range(8):
        b, hf = c // 2, c % 2
        r = R[c]
        sl = slice(hf * 1024, hf * 1024 + 1024)
        yp[b, sl] = r["yp"]
        skp[0, b, sl] = r["skp"].reshape(1024, 8, 128)
        svp[0, b, sl] = r["svp"].reshape(1024, 8, 128)
        ss = slice(16 * c, 16 * c + 16)
        ys[ss] = r["ys"].reshape(16, 4, 2048)
        sks[0, ss] = r["sks"].reshape(16, 4, 8, 128)
        svs[0, ss] = r["svs"].reshape(16, 4, 8, 128)
        gss[0, ss] = r["gss"]
        gcs[0, ss] = r["gcs"].reshape(16, 3, 3072)
        if hf == 1:
            gsp[0, b] = r["gsp"]
            gcp[0, b] = r["gcp"]
        else:
            mkp[0, b] = r["mkp"].reshape(256, 4, 128)
            mvp[0, b] = r["mvp"].reshape(256, 4, 128)
    return (yp, ys, skp, svp, sks, svs, gsp, gcp, gss, gcs, mkp, mvp)


def kernel(**inputs):
    inp = {k: np.asarray(v) for k, v in inputs.items()}
    n_phys, maps = _prep_inputs(inp)
    nc = build(n_phys)
    res = run_bass_kernel_spmd(nc, maps, core_ids=list(range(8)))
    return _assemble(res)
```

```python
import contextlib
import os
import numpy as np
import concourse.bass as bass
import concourse.mybir as mybir
from concourse.bass_utils import run_bass_kernel_spmd

F32 = mybir.dt.float32
BF16 = mybir.dt.bfloat16
I32 = mybir.dt.int32
F32R = mybir.dt.float32r
AF = mybir.ActivationFunctionType
ALU = mybir.AluOpType

ENGS = ("pe", "act", "dve", "pool", "sp")
NDSEM = 12
EPS = 1e-6
NEG = -30000.0


class Instr:
    __slots__ = ("eng", "fn", "dma", "deps", "signal", "count", "dsem", "dcount", "barrier")

    def __init__(self, eng, fn, dma):
        self.eng = eng
        self.fn = fn
        self.dma = dma
        self.deps = []
        self.signal = False
        self.count = 0
        self.dsem = None
        self.dcount = 0
        self.barrier = None


class Prog:
    def __init__(self, nc, stack):
        self.nc = nc
        self.streams = {e: [] for e in ENGS}
        self.last_writer = {}
        self.readers = {}
        self.ndma = {e: 0 for e in ENGS}
        self.dma_hist = {e: [] for e in ENGS}
        self.ccount = {e: 0 for e in ENGS}
        self.waited = {e: {} for e in ENGS}
        self.sems = {e: stack.enter_context(nc.semaphore("s_" + e)) for e in ENGS}
        self.dsems = {}
        for e in ("sp", "pool", "act"):
            for i in range(NDSEM):
                self.dsems[(e, i)] = stack.enter_context(nc.semaphore("d_%s_%d" % (e, i)))
        self.pending_barrier = None

    def op(self, eng, fn, reads=(), writes=(), dma=False):
        ins = Instr(eng, fn, dma)
        deps = {}
        px = [k for k in reads if isinstance(k, str) and k[:2] == "ps" and k[2:].isdigit()]
        if px:
            reads = [k for k in reads if k not in px]
            writes = list(writes) + px
        for k in reads:
            w = self.last_writer.get(k)
            if w is not None:
                deps[id(w)] = w
        for k in writes:
            w = self.last_writer.get(k)
            if w is not None:
                deps[id(w)] = w
            rd = self.readers.get(k)
            if rd:
                for r in rd.values():
                    deps[id(r)] = r
        ins.deps = list(deps.values())
        rkey = ("d", eng, self.ndma[eng] % NDSEM) if dma else ("c", eng)
        for k in reads:
            self.readers.setdefault(k, {})[rkey] = ins
        for k in writes:
            self.last_writer[k] = ins
            self.readers[k] = {}
        if dma:
            i = self.ndma[eng]
            self.ndma[eng] += 1
            ins.dsem = (eng, i % NDSEM)
            ins.dcount = 16 * (i // NDSEM + 1)
            hist = self.dma_hist[eng]
            if i >= NDSEM:
                ins.deps.append(hist[i - NDSEM])
            hist.append(ins)
        self.streams[eng].append(ins)
        return ins

    def emit(self, final=False):
        nc = self.nc
        streams = self.streams
        for e in ENGS:
            for ins in streams[e]:
                for d in ins.deps:
                    if d.dma:
                        continue
                    if d.eng == ins.eng and not ins.dma and d.eng == "pe":
                        continue
                    d.signal = True
            for ins in reversed(streams[e]):
                if not ins.dma:
                    ins.signal = True
                    break
        for e in ENGS:
            c = self.ccount[e]
            for ins in streams[e]:
                if not ins.dma and ins.signal:
                    c += 1
                    ins.count = c
            self.ccount[e] = c
        bar = []
        for e in ENGS:
            if self.ccount[e]:
                bar.append((("c", e), self.sems[e], self.ccount[e]))
            hist = self.dma_hist[e]
            for i in range(max(0, len(hist) - NDSEM), len(hist)):
                d = hist[i]
                bar.append((("d",) + d.dsem, self.dsems[d.dsem], d.dcount))
        prev_bar = self.pending_barrier

        def run(engobj, e):
            waited = self.waited[e]

            def wait(key, sem, val):
                if waited.get(key, 0) >= val:
                    return
                waited[key] = val
                engobj.wait_ge(sem, val)

            if prev_bar is not None:
                for key, sem, val in prev_bar:
                    if key == ("c", e):
                        continue
                    wait(key, sem, val)
            for ins in streams[e]:
                for d in ins.deps:
                    if d.dma:
                        wait(("d",) + d.dsem, self.dsems[d.dsem], d.dcount)
                    else:
                        if d.eng == e and not ins.dma and e == "pe":
                            continue
                        wait(("c", d.eng), self.sems[d.eng], d.count)
                r = ins.fn(engobj)
                if ins.dma:
                    r.then_inc(self.dsems[ins.dsem], 16)
                elif ins.signal:
                    r.then_inc(self.sems[e], 1)
            if final and e == "sp":
                for key, sem, val in bar:
                    wait(key, sem, val)

        with nc.Block() as block:
            @block.tensor
            def _(eng):
                run(eng, "pe")

            @block.scalar
            def _(eng):
                run(eng, "act")

            @block.vector
            def _(eng):
                run(eng, "dve")

            @block.gpsimd
            def _(eng):
                run(eng, "pool")

            @block.sync
            def _(eng):
                run(eng, "sp")

        self.pending_barrier = bar
        self.streams = {e: [] for e in ENGS}
        self.last_writer = {}
        self.readers = {}


C_SQ, C_SK, C_SV, C_GQ, C_GK, C_GV, C_GA, C_GB, C_GZ = 0, 1024, 2048, 3072, 4096, 5120, 6144, 6152, 6160
NT = 2112
OWN0, SMP0 = 1024, 2048
NTO = 1088

W_NAMES = [
    ("norm_mix_g", [1, 2048]), ("w_in", [2048, 7184]), ("sb_q_norm_g", [1, 128]), ("sb_k_norm_g", [1, 128]),
    ("sb_logit_bias", [1, 8]), ("gdn_conv_w", [4, 3072]), ("gdn_a_log", [1, 8]), ("gdn_dt_bias", [1, 8]),
    ("gdn_out_norm_g", [1, 128]), ("w_out", [2048, 2048]), ("norm_x_g", [1, 2048]), ("norm_mem_g", [1, 2048]),
    ("w_xq", [2048, 512]), ("w_mk", [2048, 512]), ("w_mv", [2048, 512]), ("x_q_norm_g", [1, 128]),
    ("x_k_norm_g", [1, 128]), ("w_xo", [512, 2048]), ("norm_ffn_g", [1, 2048]), ("w_gate_up", [2048, 11264]),
    ("w_down", [5632, 2048]),
]
OUT_SPECS = [
    ("yp", [1024, 2048]), ("ys", [64, 2048]), ("skp", [1024, 1024]), ("svp", [1024, 1024]),
    ("sks", [64, 1024]), ("svs", [64, 1024]), ("gsp", [8, 128, 128]), ("gcp", [3, 3072]),
    ("gss", [16, 8, 128, 128]), ("gcs", [48, 3072]), ("mkp", [256, 512]), ("mvp", [256, 512]),
]


def build(n_phys, stage=99):
    nc = bass.Bass("TRN2", target_bir_lowering=False)

    def din(n, s, dt=F32):
        return nc.dram_tensor(n, list(s), dt, kind="ExternalInput").ap()

    def dout(n, s, dt=F32):
        return nc.dram_tensor(n, list(s), dt, kind="ExternalOutput").ap()

    D = {}
    D["xp"] = din("xp", [2048, 2048])
    D["xs"] = din("xs", [64, 2048])
    D["mem"] = din("mem", [256, 2048])
    D["ck"] = din("ck", [n_phys * 128, 1024])
    D["cv"] = din("cv", [n_phys * 128, 1024])
    D["pt"] = din("pt", [1, 256], I32)
    D["sg"] = din("sg", [16, 8, 128, 128])
    D["sc"] = din("sc", [48, 3072])
    D["cmk"] = din("cmk", [16, 256, 512])
    D["cmv"] = din("cmv", [16, 256, 512])
    D["flags"] = din("flags", [1, 4])
    for n, s in W_NAMES:
        D[n] = din(n, s)
    for n, s in OUT_SPECS:
        D[n] = dout(n, s)
    D["scr_gab"] = nc.dram_tensor("scr_gab", [64, 16], F32, kind="Internal").ap()

    with contextlib.ExitStack() as perm:
        P = Prog(nc, perm)

        def sb(name, shape, dt, st=perm):
            return st.enter_context(nc.sbuf_tensor(name, list(shape), dt))

        def act(out, in_, func, r, w, bias=0.0, scale=1.0, accum=None):
            if accum is None:
                P.op("act", lambda e: e.activation(out=out, in_=in_, func=func, bias=bias, scale=scale), r, w)
            else:
                P.op("act", lambda e: e.activation(out=out, in_=in_, func=func, bias=bias, scale=scale, accum_out=accum), r, w)

        def cp(eng, out, in_, r, w):
            if eng == "act":
                P.op("act", lambda e: e.activation(out=out, in_=in_, func=AF.Copy), r, w)
            else:
                P.op(eng, lambda e: e.tensor_copy(out=out, in_=in_), r, w)

        def tt(eng, out, a, b, op, r, w):
            P.op(eng, lambda e: e.tensor_tensor(out=out, in0=a, in1=b, op=op), r, w)

        def ts(eng, out, a, s1, s2, op0, op1, r, w):
            if s2 is None:
                P.op(eng, lambda e: e.tensor_scalar(out=out, in0=a, scalar1=s1, scalar2=None, op0=op0), r, w)
            else:
                P.op(eng, lambda e: e.tensor_scalar(out=out, in0=a, scalar1=s1, scalar2=s2, op0=op0, op1=op1), r, w)

        def stt(out, a, s, b, op0, op1, r, w):
            P.op("dve", lambda e: e.scalar_tensor_tensor(out=out, in0=a, scalar=s, in1=b, op0=op0, op1=op1), r, w)

        def mm(out, lhsT, rhs, start, stop, r, w):
            P.op("pe", lambda e: e.matmul(out, lhsT=lhsT, rhs=rhs, start=start, stop=stop), r, w)

        def tr(out, in_, ident, r, w):
            P.op("pe", lambda e: e.transpose(out=out, in_=in_, identity=ident), r, w)

        def dma(eng, out, in_, r, w):
            P.op(eng, lambda e: e.dma_start(out=out, in_=in_), r, w, dma=True)

        def memset(eng, ap, val, w):
            P.op(eng, lambda e: e.memset(ap, val), (), w)

        def asel(out, in_, pattern, cmp, base, cm, r, w, fill=0.0):
            P.op("pool", lambda e: e.affine_select(out=out, in_=in_, pattern=pattern, compare_op=cmp, fill=fill,
                                                   base=base, channel_multiplier=cm), r, w)

        PS = [perm.enter_context(nc.psum_tensor("ps%d" % i, [128, 512], F32)) for i in range(8)]
        PSK = ["ps%d" % i for i in range(8)]

        def psb(i):
            return PS[i][:].bitcast(BF16)

        ident_f = sb("ident_f", [128, 128], F32)
        ident_b = sb("ident_b", [128, 128], BF16)
        ones_f = sb("ones_f", [128, 128], F32)
        ones_b = sb("ones_b", [128, 128], BF16)
        tri_gt = sb("tri_gt", [128, 128], F32)
        tri_le = sb("tri_le", [128, 128], F32)
        memset("pool", ones_f[:], 1.0, ["ones_f"])
        memset("pool", ones_b[:], 1.0, ["ones_b"])
        memset("pool", ident_f[:], 0.0, ["ident_f"])
        asel(ident_f[:], ident_f[:], [[-1, 128]], ALU.not_equal, 0, 1, ["ident_f"], ["ident_f"], fill=1.0)
        cp("dve", ident_b[:], ident_f[:], ["ident_f"], ["ident_b"])
        asel(tri_gt[:], ones_f[:], [[-1, 128]], ALU.is_gt, 0, 1, ["ones_f"], ["tri_gt"])
        asel(tri_le[:], ones_f[:], [[1, 128]], ALU.is_ge, 0, -1, ["ones_f"], ["tri_le"])
        tri_gt_r = sb("tri_gt_r", [128, 128], F32)
        ones_r = sb("ones_r", [128, 128], F32)
        cp("dve", tri_gt_r[:].bitcast(F32R), tri_gt[:], ["tri_gt"], ["tri_gt_r"])
        cp("dve", ones_r[:].bitcast(F32R), ones_f[:], ["ones_f"], ["ones_r"])

        flg = sb("flg", [128, 4], F32)
        dma("sp", flg[:], D["flags"][0:1, :].partition_broadcast(128), [], ["flg"])
        sbb = sb("sbb", [128, 8], F32)
        dma("sp", sbb[:], D["sb_logit_bias"][0:1, :].partition_broadcast(128), [], ["sbb"])
        sbb_pre = sb("sbb_pre", [128, 8], F32)
        ts("dve", sbb_pre[:], sbb[:], flg[:, 1:2], None, ALU.add, None, ["sbb", "flg"], ["sbb_pre"])
        alog = sb("alog", [128, 8], F32)
        dma("sp", alog[:], D["gdn_a_log"][0:1, :].partition_broadcast(128), [], ["alog"])
        dtb = sb("dtb", [128, 8], F32)
        dma("sp", dtb[:], D["gdn_dt_bias"][0:1, :].partition_broadcast(128), [], ["dtb"])
        nea = sb("nea", [128, 8], F32)
        act(nea[:], alog[:], AF.Exp, ["alog"], ["nea"])
        ts("dve", nea[:], nea[:], -1.0, None, ALU.mult, None, ["nea"], ["nea"])
        hg_row = sb("hg_row", [128, 128], F32)
        memset("pool", hg_row[:], 0.0, ["hg_row"])
        for i, n in enumerate(["sb_q_norm_g", "sb_k_norm_g", "x_q_norm_g", "x_k_norm_g", "gdn_out_norm_g"]):
            dma("sp", hg_row[i:i + 1, :], D[n][0:1, :], [], ["hg_row"])
        hg = sb("hg", [128, 8], F32)
        tr(PS[7][:, 0:128], hg_row[:, :], ident_f[:, :], ["hg_row", "ident_f"], ["ps7"])
        cp("dve", hg[:], PS[7][:, 0:8], ["ps7"], ["hg"])
        gn = sb("gn", [128, 4, 16, 1], F32)
        gn_row = sb("gn_row", [128, 128], F32)
        memset("pool", gn_row[:], 0.0, ["gn_row"])
        for i, n in enumerate(["norm_mix_g", "norm_x_g", "norm_ffn_g", "norm_mem_g"]):
            dma("sp", gn_row[16 * i:16 * i + 16, :], D[n].rearrange("o (k p) -> (o k) p", p=128), [], ["gn_row"])
        tr(PS[7][:, 128:256], gn_row[:, :], ident_f[:, :], ["gn_row", "ident_f"], ["ps7"])
        cp("dve", gn[:].rearrange("p a k o -> p (a k o)"), PS[7][:, 128:192], ["ps7"], ["gn"])

        BIG = sb("BIG", [128, 16 * NT], BF16)
        XT = BIG.reshape([128, 16, NT])
        CT = sb("CT", [128, 16, NTO], BF16)
        MKT = sb("MKT", [128, 4, 256], BF16)
        MV = sb("MV", [128, 2, 512], BF16)
        QTS = sb("QTS", [128, 8, 64], BF16)
        KTS = sb("KTS", [128, 8, 64], BF16)
        CW = sb("CW", [128, 24, 4], F32)
        SCT = sb("SCT", [128, 24, 48], F32)

        wst = [sb("wst%d" % i, [128, 512], F32) for i in range(4)]
        wcnt = [0]
        cvt_rot = [0]

        def load_w(dst, dkey, src, k0, nk, col0, ncols, gains=None, dcol0=0):
            srcv = src.rearrange("(k p) n -> p k n", p=128)
            per = max(1, 512 // ncols)
            kk = k0
            while kk < k0 + nk:
                n = min(per, k0 + nk - kk)
                i = wcnt[0] % 4
                wcnt[0] += 1
                st = wst[i]
                stv = st[:, 0:n * ncols].rearrange("p (k n) -> p k n", k=n)
                dma("sp", stv, srcv[:, kk:kk + n, col0:col0 + ncols], [], ["wst%d" % i])
                eng = ("dve", "pool", "act")[cvt_rot[0] % 3]
                cvt_rot[0] += 1
                o = dst[:, kk:kk + n, dcol0:dcol0 + ncols]
                if gains is None:
                    cp(eng, o, stv, ["wst%d" % i], [dkey])
                else:
                    if eng == "act":
                        eng = "dve"
                    g = gains[:, kk:kk + n, :].broadcast_to([128, n, ncols])
                    tt(eng, o, stv, g, ALU.mult, ["wst%d" % i, "gn"], [dkey])
                kk += n

        def rms_to_fm(st, src_rows, n, dstT, c0, dkey, xin, xkey, tag):
            dma("sp", xin[0:n, :], src_rows, [], [xkey])
            junk = st["junk"]
            ss = st["ss"]
            act(junk[0:n, :], xin[0:n, :], AF.Square, [xkey], ["junk", "ss"], accum=ss[0:n, :])
            act(ss[0:n, :], ss[0:n, :], AF.Ln, ["ss"], ["ss"], bias=EPS, scale=1.0 / 2048)
            act(ss[0:n, :], ss[0:n, :], AF.Exp, ["ss"], ["ss"], scale=-0.5)
            xb = st["xb"]
            ts("dve", xb[0:n, :], xin[0:n, :], ss[0:n, 0:1], None, ALU.mult, None, [xkey, "ss"], ["xb"])
            for half in range(2):
                bank = 6 + half
                pv = psb(bank)
                for k in range(8):
                    kk = half * 8 + k
                    tr(pv[:, k * 128:k * 128 + n], xb[0:n, kk * 128:(kk + 1) * 128], ident_b[0:n, 0:n],
                       ["xb", "ident_b"], [PSK[bank]])
                src = pv[:, 0:1024].rearrange("p (a b) -> p a b", a=8)[:, :, 0:n]
                cp("act" if half == 0 else "dve", dstT[:, half * 8:half * 8 + 8, c0:c0 + n], src, [PSK[bank]],
                   [(dkey, tag, half)])

        def fm_norm(psk, ps_ap, T, sc, out, wkeys, st, rextra=(), mean=True):
            sq = st["sq"]
            rr = st["rr"]
            act(sq[:, 0:T], ps_ap, AF.Square, [psk], ["sq"])
            mm(PS[5][:, 0:T], ones_b[:], sq[:, 0:T], True, True, ["sq", "ones_b"], ["ps5"])
            act(rr[:, 0:T], PS[5][:, 0:T], AF.Ln, ["ps5"], ["rr"], bias=EPS, scale=(1.0 / 128 if mean else 1.0))
            act(rr[:, 0:T], rr[:, 0:T], AF.Exp, ["rr"], ["rr"], scale=-0.5)
            stt(out, ps_ap, sc, rr[:, 0:T], ALU.mult, ALU.mult, [psk, "rr"] + list(rextra), wkeys)

        with contextlib.ExitStack() as ph:
            rowbuf = sb("rowbuf", [128, 3072], F32, ph)
            memset("pool", rowbuf[:], 0.0, ["rowbuf"])
            dma("sp", rowbuf[0:4, :], D["gdn_conv_w"][:, :], [], ["rowbuf"])
            for c4 in range(6):
                for j in range(4):
                    c = c4 * 4 + j
                    tr(PS[4][:, j * 128:(j + 1) * 128], rowbuf[:, c * 128:(c + 1) * 128], ident_f[:], ["rowbuf", "ident_f"], ["ps4"])
                cp("dve", CW[:, c4 * 4:c4 * 4 + 4, :], PS[4][:, :].rearrange("p (j x) -> p j x", j=4)[:, :, 0:4], ["ps4"], ["CW"])
            dma("sp", rowbuf[0:48, :], D["sc"][:, :], ["rowbuf"], ["rowbuf"])
            for c4 in range(6):
                for j in range(4):
                    c = c4 * 4 + j
                    tr(PS[4][:, j * 128:(j + 1) * 128], rowbuf[:, c * 128:(c + 1) * 128], ident_f[:], ["rowbuf", "ident_f"], ["ps4"])
                cp("dve", SCT[:, c4 * 4:c4 * 4 + 4, :], PS[4][:, :].rearrange("p (j x) -> p j x", j=4)[:, :, 0:48], ["ps4"], ["SCT"])
            P.emit()

        with contextlib.ExitStack() as ph:
            st1 = {
                "junk": sb("junk", [128, 2048], BF16, ph),
                "ss": sb("ss", [128, 1], F32, ph),
                "xb": sb("xb", [128, 2048], BF16, ph),
                "sq": sb("sq1", [128, 512], BF16, ph),
                "rr": sb("rr1", [128, 512], F32, ph),
            }
            xin = [sb("xin%d" % i, [128, 2048], F32, ph) for i in range(2)]
            KD = int(os.environ.get("KDBG", "9"))
            for t in range(17 if KD >= 3 else (1 if KD == 2 else 0)):
                if t < 16:
                    rows, n, c0 = D["xp"][t * 128:(t + 1) * 128, :], 128, t * 128
                else:
                    rows, n, c0 = D["xs"][:, :], 64, SMP0
                rms_to_fm(st1, rows, n, XT, c0, "XT", xin[t % 2], "xin%d" % (t % 2), t)
            MT = sb("MT", [128, 16, 256], BF16, ph)
            for t in range(2 if KD >= 4 else 0):
                rms_to_fm(st1, D["mem"][t * 128:(t + 1) * 128, :], 128, MT, t * 128, "MT", xin[t % 2], "xin%d" % (t % 2), t)
            wmk = sb("wmk", [128, 16, 512], BF16, ph)
            wmv = sb("wmv", [128, 16, 512], BF16, ph)
            load_w(wmk, "wmk", D["w_mk"], 0, 16, 0, 512, gn[:, 3])
            load_w(wmv, "wmv", D["w_mv"], 0, 16, 0, 512, gn[:, 3])
            MTK = [("MT", t, h2) for t in range(2) for h2 in range(2)]
            mk32 = sb("mk32", [128, 256], F32, ph)
            mko = sb("mko", [128, 2, 512], F32, ph)
            mvo = sb("mvo", [128, 2, 512], F32, ph)
            for h in range(4 if KD >= 5 else 0):
                for k in range(16):
                    mm(PS[0][:, 0:256], wmk[:, k, h * 128:(h + 1) * 128], MT[:, k, :], k == 0, k == 15,
                       ["wmk"] + MTK, ["ps0"])
                fm_norm("ps0", PS[0][:, 0:256], 256, hg[:, 3:4], mk32[:, :], ["mk32"], st1, ["hg"])
                cp("act", MKT[:, h, :], mk32[:, :], ["mk32"], [("MKT", h)])
                for t in range(2):
                    tr(PS[1][:, t * 128:(t + 1) * 128], mk32[:, t * 128:(t + 1) * 128], ident_f[:], ["mk32", "ident_f"], ["ps1"])
                cp("dve", mko[:, :, h * 128:(h + 1) * 128], PS[1][:, 0:256].rearrange("p (t d) -> p t d", t=2), ["ps1"], ["mko"])
            if KD >= 5:
                dma("sp", D["mkp"].rearrange("(t p) c -> p t c", p=128), mko[:], ["mko"], [])
            for t in range(2 if KD >= 6 else 0):
                for k in range(16):
                    mm(PS[2 + t][:, :], MT[:, k, t * 128:(t + 1) * 128], wmv[:, k, :], k == 0, k == 15, ["wmv"] + MTK, [PSK[2 + t]])
                cp("dve", mvo[:, t, :], PS[2 + t][:, :], [PSK[2 + t]], ["mvo"])
                cp("act", MV[:, t, :], mvo[:, t, :], ["mvo"], [("MV", t)])
            if KD >= 6:
                dma("sp", D["mvp"].rearrange("(t p) c -> p t c", p=128), mvo[:], ["mvo"], [])
            P.emit()

        SCALE = 128.0 ** -0.5
        with contextlib.ExitStack() as ph:
            st2 = {"sq": sb("sq2", [128, 512], BF16, ph), "rr": sb("rr2", [128, 512], F32, ph)}
            WB = [sb("wb%d" % i, [128, 16, 128], BF16, ph) for i in range(5)]
            wbi = [0]

            def wb_next():
                i = wbi[0] % 5
                wbi[0] += 1
                return WB[i], "wb%d" % i

            pa = [0]

            def pa_next():
                i = pa[0] % 2
                pa[0] += 1
                return i

            def proj_fm(bank, w, wkey, xc0, T):
                for k in range(16):
                    mm(PS[bank][:, 0:T], w[:, k, :], XT[:, k, xc0:xc0 + T], k == 0, k == 15, [wkey], [PSK[bank]])

            with contextlib.ExitStack() as ph2a:
                ph2 = ph
                ph = ph2a
                MSK = sb("MSK", [128, 4, 512], BF16, ph)
                memset("pool", MSK[:], 1.0, ["MSK"])
                for kd in range(4):
                    asel(MSK[:, kd, :], MSK[:, kd, :], [[1, 512]], ALU.is_ge, -128 * kd - 1, -1, ["MSK"], ["MSK"])
                QT = sb("QT", [128, NTO], BF16, ph)
                KT = sb("KT", [128, NT], BF16, ph)
                VH = sb("VH", [128, 17, 128], BF16, ph)
                kn32 = sb("kn32", [128, 512], F32, ph)
                kout = sb("kout", [128, 9, 128], F32, ph)
                vout = sb("vout", [128, 9, 128], F32, ph)
                EXb = [sb("EX%d" % i, [128, 512], F32, ph) for i in range(2)]
                T1b = [sb("T1%d" % i, [128, 512], F32, ph) for i in range(2)]
                Wbb = [sb("Wb%d" % i, [128, 512], BF16, ph) for i in range(2)]
                Rrs = [sb("Rr%d" % i, [128, 512], F32, ph) for i in range(2)]
                n_sb = 8 if stage >= 2 else 0
                for h in range(n_sb):
                    wq, wqk = wb_next()
                    wk, wkk = wb_next()
                    wv, wvk = wb_next()
                    load_w(wq, wqk, D["w_in"], 0, 16, C_SQ + h * 128, 128, gn[:, 0])
                    load_w(wk, wkk, D["w_in"], 0, 16, C_SK + h * 128, 128, gn[:, 0])
                    load_w(wv, wvk, D["w_in"], 0, 16, C_SV + h * 128, 128, gn[:, 0])
                    for xc0, T, qc0 in ((OWN0, 512, 0), (OWN0 + 512, 512, 512), (SMP0, 64, 1024)):
                        bk = pa_next()
                        proj_fm(bk, wq, wqk, xc0, T)
                        fm_norm(PSK[bk], PS[bk][:, 0:T], T, hg[:, 0:1], QT[:, qc0:qc0 + T], [("QT", qc0)], st2, ["hg"])
                    cp("pool", QTS[:, h, :], QT[:, 1024:1088], [("QT", 1024)], [("QTS", h)])
                    for gi, (xc0, T) in enumerate(((0, 512), (512, 512), (OWN0, 512), (OWN0 + 512, 512), (SMP0, 64))):
                        bk = pa_next()
                        proj_fm(bk, wk, wkk, xc0, T)
                        fm_norm(PSK[bk], PS[bk][:, 0:T], T, hg[:, 1:2], kn32[:, 0:T], ["kn32"], st2, ["hg"])
                        cp("act", KT[:, xc0:xc0 + T], kn32[:, 0:T], ["kn32"], [("KT", gi)])
                        if gi >= 2:
                            nt = T // 128 if T >= 128 else 1
                            w_ = 128 if T >= 128 else T
                            for j in range(nt):
                                tr(PS[7][0:w_, j * 128:(j + 1) * 128], kn32[:, j * 128:j * 128 + w_], ident_f[:], ["kn32", "ident_f"], ["ps7"])
                            t0 = (gi - 2) * 4
                            cp("dve", kout[0:w_, t0:t0 + nt, :], PS[7][0:w_, 0:nt * 128].rearrange("p (t d) -> p t d", t=nt),
                               ["ps7"], ["kout"])
                    cp("pool", KTS[:, h, :], KT[:, SMP0:SMP0 + 64], [("KT", 4)], [("KTS", h)])
                    dma("sp", D["skp"][:, h * 128:(h + 1) * 128].rearrange("(t p) d -> p t d", p=128), kout[:, 0:8, :], ["kout"], [])
                    dma("sp", D["sks"][:, h * 128:(h + 1) * 128], kout[0:64, 8, :], ["kout"], [])
                    for g4 in range(5):
                        bk = pa_next()
                        tiles = list(range(g4 * 4, min(g4 * 4 + 4, 17)))
                        for ti, t in enumerate(tiles):
                            n = 128 if t < 16 else 64
                            for k in range(16):
                                mm(PS[bk][0:n, ti * 128:(ti + 1) * 128], XT[:, k, t * 128:t * 128 + n], wv[:, k, :], k == 0, k == 15,
                                   [wvk], [PSK[bk]])
                        nt = len(tiles)
                        n = 128 if g4 < 4 else 64
                        src = PS[bk][0:n, 0:nt * 128].rearrange("p (t d) -> p t d", t=nt)
                        if g4 >= 2:
                            vo = vout[0:n, g4 * 4 - 8:g4 * 4 - 8 + nt, :]
                            cp("dve", vo, src, [PSK[bk]], ["vout"])
                            cp("act", VH[0:n, g4 * 4:g4 * 4 + nt, :], vo, ["vout"], [("VH", g4)])
                        else:
                            cp("act", VH[0:n, g4 * 4:g4 * 4 + nt, :], src, [PSK[bk]], [("VH", g4)])
                    dma("sp", D["svp"][:, h * 128:(h + 1) * 128].rearrange("(t p) d -> p t d", p=128), vout[:, 0:8, :], ["vout"], [])
                    dma("sp", D["svs"][:, h * 128:(h + 1) * 128], vout[0:64, 8, :], ["vout"], [("dram", "svs")])
                    VHK = [("VH", g) for g in range(5)]
                    KTK = [("KT", g) for g in range(5)]
                    tl = [list(range(8 + 4 * q_ + 3, 7, -1)) + list(range(7, -1, -1)) for q_ in range(2)]
                    banks = ((2, 4, 6), (3, 7, 0))
                    for idx in range(max(len(tl[0]), len(tl[1]))):
                        act_ch = [q_ for q_ in range(2) if idx < len(tl[q_])]
                        info = {}
                        for stq in act_ch:
                            tiles = tl[stq]
                            j = tiles[idx]
                            kd = (j - 8) - 4 * stq
                            info[stq] = dict(j=j, first=idx == 0, last=idx == len(tiles) - 1, kd=kd, diag=(j >= 8 and kd >= 0),
                                             bias=(sbb_pre[:, h:h + 1] if j < 8 else sbb[:, h:h + 1]), bk=("sbb_pre" if j < 8 else "sbb"))
                        for stq in act_ch:
                            I_ = info[stq]
                            zb = banks[stq][0]
                            EX, exk = EXb[stq], "EX%d" % stq
                            mm(PS[zb][:, :], KT[:, I_["j"] * 128:(I_["j"] + 1) * 128], QT[:, stq * 512:(stq + 1) * 512], True, True,
                               KTK + [("QT", stq * 512)], [PSK[zb]])
                            act(EX[:].bitcast(F32R), PS[zb][:, :], AF.Exp, [PSK[zb], I_["bk"]], [exk], bias=I_["bias"], scale=SCALE)
                            act(EX[:].bitcast(F32R), EX[:], AF.Ln, [exk], [exk], bias=1.0)
                            if I_["diag"]:
                                tt("dve", EX[:].bitcast(F32R), EX[:], MSK[:, I_["kd"], :], ALU.mult, [exk, "MSK"], [exk])
                        for stq in act_ch:
                            I_ = info[stq]
                            cb_ = banks[stq][1]
                            EX, exk = EXb[stq], "EX%d" % stq
                            Aq, ak_ = Rrs[stq], "Rr%d" % stq
                            mm(PS[cb_][:, :], tri_gt_r[:].bitcast(F32R), EX[:].bitcast(F32R), True, I_["first"], [exk, "tri_gt_r"], [PSK[cb_]])
                            if not I_["first"]:
                                mm(PS[cb_][:, :], ones_r[:].bitcast(F32R), Aq[:].bitcast(F32R), False, True, [ak_, "ones_r"], [PSK[cb_]])
                        for stq in act_ch:
                            I_ = info[stq]
                            zb, cb_ = banks[stq][0], banks[stq][1]
                            EX, exk = EXb[stq], "EX%d" % stq
                            T1, t1k = T1b[stq], "T1%d" % stq
                            Wb, wbk = Wbb[stq], "Wb%d" % stq
                            Aq, ak_ = Rrs[stq], "Rr%d" % stq
                            stt(T1[:], PS[zb][:, :], SCALE, EX[:], ALU.mult, ALU.subtract, [PSK[zb], exk], [t1k])
                            tt("dve", T1[:], T1[:], PS[cb_][:, :], ALU.subtract, [t1k, PSK[cb_]], [t1k])
                            act(Wb[:], T1[:], AF.Exp, [t1k, I_["bk"]], [wbk], bias=I_["bias"])
                            if I_["diag"]:
                                tt("pool", Wb[:], Wb[:], MSK[:, I_["kd"], :], ALU.mult, [wbk, "MSK"], [wbk])
                            if not I_["last"]:
                                if I_["first"]:
                                    cp("dve", Aq[:].bitcast(F32R), EX[:], [exk], [ak_])
                                else:
                                    tt("dve", Aq[:].bitcast(F32R), Aq[:], EX[:], ALU.add, [ak_, exk], [ak_])
                        for stq in act_ch:
                            I_ = info[stq]
                            ob_ = banks[stq][2]
                            Wb, wbk = Wbb[stq], "Wb%d" % stq
                            mm(PS[ob_][:, :], VH[:, I_["j"], :], Wb[:], I_["first"], I_["last"], [wbk] + VHK, [PSK[ob_]])
                            if I_["last"]:
                                cp("act", CT[:, h, stq * 512:(stq + 1) * 512], PS[ob_][:, :], [PSK[ob_]], [("CT", h, stq)])
                P.emit()
                ph = ph2

            with contextlib.ExitStack() as ph2b:
                ph = ph2b
                n_gdn = 8 if stage >= 3 else 0

                def T_(name, shape, dt):
                    return sb(name, shape, dt, ph2b)

                tri_le3 = T_("tri_le3", [128, 1, 128], F32)
                cp("pool", tri_le3[:, 0, :], tri_le[:], ["tri_le"], ["tri_le3"])
                identf3 = T_("identf3", [128, 1, 128], F32)
                cp("pool", identf3[:, 0, :], ident_f[:], ["ident_f"], ["identf3"])
                MI = T_("MI", [128, 4, 128], BF16)
                MS = T_("MS", [128, 4, 128], BF16)
                MI4 = T_("MI4", [128, 16, 4], BF16)
                MS4 = T_("MS4", [128, 16, 4], BF16)
                for m_, nm_, pat_, base_ in ((MI, "MI", [[0, 4], [1, 128]], 0), (MS, "MS", [[0, 4], [1, 128]], -1),
                                             (MI4, "MI4", [[0, 16], [1, 4]], 0), (MS4, "MS4", [[0, 16], [1, 4]], -1)):
                    memset("pool", m_[:], 1.0, [nm_])
                    asel(m_[:], m_[:], pat_, ALU.is_ge, base_, -1, [nm_], [nm_])
                wgab = T_("wgab", [128, 16, 16], BF16)
                GD = T_("GD", [128, 17, 8], F32)
                BT = T_("BT", [128, 17, 8], F32)
                NBm = T_("NBm", [128, 17, 8], F32)
                gtm = T_("gtm", [128, 16], F32)
                GDS = T_("GDS", [128, 16, 24], F32)
                if n_gdn:
                    load_w(wgab, "wgab", D["w_in"], 0, 16, C_GA, 16, gn[:, 0])
                    for t in range(17):
                        n = 128 if t < 16 else 64
                        for k in range(16):
                            mm(PS[0][0:n, 0:16], XT[:, k, t * 128:t * 128 + n], wgab[:, k, :], k == 0, k == 15, ["wgab"], ["ps0"])
                        cp("dve", gtm[0:n, :], PS[0][0:n, 0:16], ["ps0"], ["gtm"])
                        tt("dve", gtm[0:n, 0:8], gtm[0:n, 0:8], dtb[0:n, :], ALU.add, ["gtm", "dtb"], ["gtm"])
                        act(gtm[0:n, 0:8], gtm[0:n, 0:8], AF.Exp, ["gtm"], ["gtm"])
                        act(gtm[0:n, 0:8], gtm[0:n, 0:8], AF.Ln, ["gtm"], ["gtm"], bias=1.0)
                        tt("dve", GD[0:n, t, :], gtm[0:n, 0:8], nea[0:n, :], ALU.mult, ["gtm", "nea"], [("GD", t)])
                        act(gtm[0:n, 8:16], gtm[0:n, 8:16], AF.Exp, ["gtm"], ["gtm"], scale=-1.0)
                        ts("dve", gtm[0:n, 8:16], gtm[0:n, 8:16], 1.0, None, ALU.add, None, ["gtm"], ["gtm"])
                        P.op("dve", (lambda o, i: (lambda e: e.reciprocal(out=o, in_=i)))(BT[0:n, t, :], gtm[0:n, 8:16]), ["gtm"], [("BT", t)])
                        ts("dve", NBm[0:n, t, :], BT[0:n, t, :], -1.0, None, ALU.mult, None, [("BT", t)], [("NBm", t)])
                    dma("sp", D["scr_gab"][:, 0:8], GD[0:64, 16, :], [("GD", 16)], [("dram", "scr")])
                    dma("sp", D["scr_gab"][:, 8:16], BT[0:64, 16, :], [("BT", 16)], [("dram", "scr2")])
                    dma("sp", GDS[0:4, :, 0:16], D["scr_gab"].rearrange("(i t) c -> t i c", t=4), [("dram", "scr"), ("dram", "scr2")], ["GDS"])
                    ts("dve", GDS[0:4, :, 16:24], GDS[0:4, :, 8:16], -1.0, None, ALU.mult, None, ["GDS"], ["GDS"])
                GDK = [("GD", t) for t in range(17)]
                BTK = [("BT", t) for t in range(17)]
                NBK = [("NBm", t) for t in range(17)]
                G3 = T_("G3", [128, 512], F32)
                CB = T_("CB", [128, 512], F32)
                ECB = T_("ECB", [128, 512], F32)
                EI = T_("EI", [128, 512], BF16)
                ES = T_("ES", [128, 512], BF16)
                CC = T_("CC", [128, 16, 1], F32)
                DEX = T_("DEX", [128, 16, 1], F32)
                GCc = T_("GCc", [128, 16, 1], F32)
                BCc = T_("BCc", [128, 16, 1], F32)
                NBc = T_("NBc", [128, 16, 1], F32)
                MA = [T_("MA%d" % i, [128, 512], F32) for i in range(2)]
                MTt = [T_("MT%d" % i, [128, 512], F32) for i in range(2)]
                X32 = T_("X32", [128, 512], F32)
                XB = T_("XB", [128, 512], BF16)
                QKD = T_("QKD", [128, 512], BF16)
                KCT = T_("KCT", [128, 512], BF16)
                QDT = T_("QDT", [128, 512], BF16)
                KTM = T_("KTM", [128, 8, 128], BF16)
                VTM = T_("VTM", [128, 8, 128], BF16)
                KEND = T_("KEND", [128, 8, 128], BF16)
                RT = T_("RT", [128, 128], BF16)
                VN = T_("VN", [128, 128], BF16)
                S32 = T_("S32", [128, 128], F32)
                Sb_ = T_("Sb_", [128, 128], BF16)
                S32A = T_("S32A", [128, 16, 128], F32)
                RAW = [T_("RAW%d" % i, [128, 515], F32) for i in range(3)]
                RAWS = [T_("RAWS%d" % i, [128, 16, 7], F32) for i in range(3)]
                ACC = T_("ACC", [128, 512], F32)
                CQ = T_("CQ", [128, 512], F32)
                QTg = T_("QTg", [128, 512], BF16)
                KTg = T_("KTg", [128, 512], BF16)
                CVb = T_("CVb", [128, 512], BF16)
                ZS = T_("ZS", [128, 512], BF16)
                gco = T_("gco", [128, 128], F32)
                t48 = T_("t48", [128, 48], F32)

                def gdn_group(C, G, own, h, gsrc, bsrc, nbsrc, gkeys, mi, ms, S_of, ctc0, c0=0, chain=True):
                    T = C * G
                    nsq = {128: 6, 4: 1}[C]

                    def v3(t):
                        return t[0:C, 0:T].rearrange("p (g c) -> p g c", g=G)

                    def p3(bank):
                        return PS[bank][0:C, 0:T].rearrange("p (g c) -> p g c", g=G)

                    cp("pool", GCc[0:C, 0:G, :], gsrc, gkeys, ["GCc"])
                    cp("pool", BCc[0:C, 0:G, :], bsrc, gkeys, ["BCc"])
                    cp("pool", NBc[0:C, 0:G, :], nbsrc, gkeys, ["NBc"])
                    tt("pool", v3(G3), GCc[0:C, 0:G, :].broadcast_to([C, G, C]), tri_le3[0:C, :, 0:C].broadcast_to([C, G, C]), ALU.mult,
                       ["GCc", "tri_le3"], ["G3"])
                    mm(PS[2][:, 0:T], ones_f[0:C, :], G3[0:C, 0:T], True, True, ["G3", "ones_f"], ["ps2"])
                    mm(PS[7][0:C, 0:G], tri_le[0:C, 0:C], GCc[0:C, 0:G, 0], True, True, ["GCc", "tri_le"], ["ps7"])
                    cp("dve", CC[0:C, 0:G, 0], PS[7][0:C, 0:G], ["ps7"], ["CC"])
                    cp("dve", CB[:, 0:T], PS[2][:, 0:T], ["ps2"], ["CB"])
                    act(ECB[:, 0:T], CB[:, 0:T], AF.Exp, ["CB"], ["ECB"])
                    cb3 = CB[0:C, 0:T].rearrange("p (g c) -> p g c", g=G)
                    tt("dve", DEX[0:C, 0:G, 0], cb3[:, :, C - 1], CC[0:C, 0:G, 0], ALU.subtract, ["CB", "CC"], ["DEX"])
                    act(DEX[0:C, 0:G, :], DEX[0:C, 0:G, :], AF.Exp, ["DEX"], ["DEX"])
                    tt("dve", v3(CB), v3(CB), CC[0:C, 0:G, :].broadcast_to([C, G, C]), ALU.subtract, ["CB", "CC"], ["CB"])
                    ts("pool", v3(CB), v3(CB), 0.0, None, ALU.min, None, ["CB"], ["CB"])
                    act(v3(CB), v3(CB), AF.Exp, ["CB"], ["CB"])
                    tt("pool", v3(EI), v3(CB), mi, ALU.mult, ["CB", "MI", "MI4"], ["EI"])
                    tt("pool", v3(ES), v3(CB), ms, ALU.mult, ["CB", "MS", "MS4"], ["ES"])
                    for g in range(G):
                        mm(PS[3][0:C, g * C:(g + 1) * C], KTg[:, c0 + g * C:c0 + (g + 1) * C], KTg[:, c0 + g * C:c0 + (g + 1) * C], True, True, ["KTg"], ["ps3"])
                    tt("dve", v3(G3), p3(3), v3(ES), ALU.mult, ["ps3", "ES"], ["G3"])
                    tt("dve", v3(MTt[0]).bitcast(F32R), v3(G3), NBc[0:C, 0:G, :].broadcast_to([C, G, C]), ALU.mult,
                       ["G3", "NBc"], ["MT0"])
                    for g in range(G):
                        mm(PS[4][0:C, g * C:(g + 1) * C], KTg[:, c0 + g * C:c0 + (g + 1) * C], QTg[:, c0 + g * C:c0 + (g + 1) * C], True, True, ["KTg", "QTg"], ["ps4"])
                    tt("dve", v3(QKD), p3(4), v3(EI), ALU.mult, ["ps4", "EI"], ["QKD"])
                    tt("pool", KCT[:, 0:T], KTg[:, c0:c0 + T], ECB[:, 0:T], ALU.mult, ["KTg", "ECB"], ["KCT"])
                    tt("pool", QDT[:, 0:T], QTg[:, c0:c0 + T], ECB[:, 0:T], ALU.mult, ["QTg", "ECB"], ["QDT"])
                    for g0 in range(0, G, 8):
                        ng = min(8, G - g0)
                        for g in range(g0, g0 + ng):
                            tr(psb(6)[0:C, (g - g0) * 128:(g - g0 + 1) * 128], KTg[:, c0 + g * C:c0 + (g + 1) * C], ident_b[:, :], ["KTg", "ident_b"], ["ps6"])
                        cp("act", KTM[0:C, g0:g0 + ng, :], psb(6)[0:C, 0:ng * 128].rearrange("p (g d) -> p g d", g=ng), ["ps6"], ["KTM"])
                        for g in range(g0, g0 + ng):
                            tr(psb(7)[0:C, (g - g0) * 128:(g - g0 + 1) * 128], CVb[:, c0 + g * C:c0 + (g + 1) * C], ident_b[:, :], ["CVb", "ident_b"], ["ps7"])
                        cp("dve", VTM[0:C, g0:g0 + ng, :], psb(7)[0:C, 0:ng * 128].rearrange("p (g d) -> p g d", g=ng), ["ps7"], ["VTM"])
                    tt("pool", KEND[0:C, 0:G, :], KTM[0:C, 0:G, :], DEX[0:C, 0:G, :].broadcast_to([C, G, 128]), ALU.mult, ["KTM", "DEX"], ["KEND"])
                    RD = F32R if C == 128 else F32

                    def rv(ap):
                        return ap.bitcast(RD) if RD is F32R else ap

                    def wv(ap):
                        return ap.bitcast(F32R)

                    for g in range(G):
                        tr(PS[3][0:C, g * C:(g + 1) * C], MTt[0][0:C, g * C:(g + 1) * C], ident_f[0:C, 0:C], ["MT0", "ident_f"], ["ps3"])
                    cp("act", wv(MA[0][0:C, 0:T]), PS[3][0:C, 0:T], ["ps3"], ["MA0"])
                    tt("dve", wv(v3(X32)), v3(MTt[0]), identf3[0:C, :, 0:C].broadcast_to([C, G, C]), ALU.add, ["MT0", "identf3"], ["X32"])
                    cur = 0
                    for r in range(nsq):
                        M, Mk = MA[cur], "MA%d" % cur
                        Mt, Mtk = MTt[cur], "MT%d" % cur
                        Mn, Mnk = MA[1 - cur], "MA%d" % (1 - cur)
                        Mtn, Mtnk = MTt[1 - cur], "MT%d" % (1 - cur)
                        for g in range(G):
                            sl = slice(g * C, (g + 1) * C)
                            mm(PS[3][0:C, sl], rv(Mt[0:C, sl]), rv(M[0:C, sl]), True, True, [Mk, Mtk], ["ps3"])
                        if r < nsq - 1:
                            for g in range(G):
                                sl = slice(g * C, (g + 1) * C)
                                mm(PS[4][0:C, sl], rv(M[0:C, sl]), rv(Mt[0:C, sl]), True, True, [Mk, Mtk], ["ps4"])
                        cp("act", wv(Mn[0:C, 0:T]), PS[3][0:C, 0:T], ["ps3"], [Mnk])
                        if r < nsq - 1:
                            cp("dve", wv(Mtn[0:C, 0:T]), PS[4][0:C, 0:T], ["ps4"], [Mtnk])
                        for g in range(G):
                            sl = slice(g * C, (g + 1) * C)
                            mm(PS[5][0:C, sl], rv(Mn[0:C, sl]), rv(X32[0:C, sl]), True, True, [Mnk, "X32"], ["ps5"])
                        tt("dve", wv(X32[0:C, 0:T]), X32[0:C, 0:T], PS[5][0:C, 0:T], ALU.add, ["X32", "ps5"], ["X32"])
                        cur = 1 - cur
                    cp("act", XB[0:C, 0:T], X32[0:C, 0:T], ["X32"], ["XB"])
                    for g in range(G):
                        S32g, Sbg, skey, sbkey = S_of(g)
                        sl = slice(g * C, (g + 1) * C)
                        if not chain:
                            cp("act", Sbg, S32g, [skey], [sbkey])
                        kb = pa_next()
                        mm(PS[kb][0:C, 0:128], KCT[:, sl], Sbg, True, True, ["KCT", sbkey], [PSK[kb]])
                        tt("dve", RT[0:C, :], VTM[0:C, g, :], PS[kb][0:C, 0:128], ALU.subtract, ["VTM", PSK[kb]], ["RT"])
                        kb2 = pa_next()
                        mm(PS[kb2][0:C, 0:128], XB[0:C, sl], RT[0:C, :], True, True, ["XB", "RT"], [PSK[kb2]])
                        ts("dve", VN[0:C, :], PS[kb2][0:C, 0:128], BCc[0:C, g, :], None, ALU.mult, None, [PSK[kb2], "BCc"], ["VN"])
                        if own:
                            mm(PS[6][:, sl], Sbg, QDT[:, sl], True, False, [sbkey, "QDT"], ["ps6"])
                            mm(PS[6][:, sl], VN[0:C, :], QKD[0:C, sl], False, True, ["VN", "QKD"], ["ps6"])
                        mm(PS[7][:, 0:128], KEND[0:C, g, :], VN[0:C, :], True, True, ["KEND", "VN"], ["ps7"])
                        stt(S32g, S32g, ECB[:, g * C + C - 1:g * C + C], PS[7][:, 0:128], ALU.mult, ALU.add, [skey, "ECB", "ps7"], [skey])
                        if chain:
                            cp("act", Sbg, S32g, [skey], [sbkey])
                    if own:
                        fm_norm("ps6", PS[6][:, 0:T], T, hg[:, 4:5], ACC[:, 0:T], ["ACC"], st2, ["hg"])
                        tt("pool", CT[:, 8 + h, ctc0:ctc0 + T], ACC[:, 0:T], ZS[:, c0:c0 + T], ALU.mult, ["ACC", "ZS"], [("CT", 8 + h, ctc0)])

                def conv_silu(flat_in, L, cidx, out_ap_fn, key_in):
                    ts("dve", ACC[:, 0:L], flat_in[:, 3:3 + L], CW[:, cidx, 3:4], None, ALU.mult, None, [key_in, "CW"], ["ACC"])
                    for j in (2, 1, 0):
                        stt(ACC[:, 0:L], flat_in[:, j:j + L], CW[:, cidx, j:j + 1], ACC[:, 0:L], ALU.mult, ALU.add, [key_in, "CW", "ACC"], ["ACC"])

                for h in range(n_gdn):
                    ws = []
                    for c0_ in (C_GQ, C_GK, C_GV, C_GZ):
                        w_, wk_ = wb_next()
                        load_w(w_, wk_, D["w_in"], 0, 16, c0_ + h * 128, 128, gn[:, 0])
                        ws.append((w_, wk_))
                    memset("pool", S32[:], 0.0, ["S"])
                    memset("pool", Sb_[:], 0.0, [("S", "b")])
                    for i in range(3):
                        memset("pool", RAW[i][:, 0:3], 0.0, ["RAW%d" % i])
                    for grp in range(4):
                        own = grp >= 2
                        xc0 = grp * 512
                        if grp == 2:
                            for i in range(3):
                                ts("dve", RAW[i][:, 0:3], RAW[i][:, 0:3], flg[:, 0:1], None, ALU.mult, None, ["RAW%d" % i, "flg"], ["RAW%d" % i])
                            ts("dve", S32[:], S32[:], flg[:, 0:1], None, ALU.mult, None, ["S", "flg"], ["S"])
                            cp("act", Sb_[:], S32[:], ["S"], [("S", "b")])
                        for comp in range(3):
                            w_, wk_ = ws[comp]
                            rk = "RAW%d" % comp
                            bk = pa_next()
                            proj_fm(bk, w_, wk_, xc0, 512)
                            cp("act", RAW[comp][:, 3:515], PS[bk][:, 0:512], [PSK[bk]], [rk])
                            cidx = comp * 8 + h
                            if grp == 3:
                                tr(PS[7][0:3, 0:128], RAW[comp][:, 512:515], ident_f[:], [rk, "ident_f"], ["ps7"])
                                cp("dve", gco[0:3, :], PS[7][0:3, 0:128], ["ps7"], ["gco"])
                                dma("sp", D["gcp"][:, cidx * 128:(cidx + 1) * 128], gco[0:3, :], ["gco"], [])
                            conv_silu(RAW[comp], 512, cidx, None, rk)
                            cp("pool", RAW[comp][:, 0:3], RAW[comp][:, 512:515], [rk], [rk])
                            if comp == 0:
                                act(CQ[:, :], ACC[:, :], AF.Silu, ["ACC"], ["CQ"])
                                fm_norm("CQ", CQ[:, :], 512, 128.0 ** -0.5, QTg[:, :], ["QTg"], st2, mean=False)
                            elif comp == 1:
                                act(CQ[:, :], ACC[:, :], AF.Silu, ["ACC"], ["CQ"])
                                fm_norm("CQ", CQ[:, :], 512, 1.0, KTg[:, :], ["KTg"], st2, mean=False)
                            else:
                                act(CVb[:, :], ACC[:, :], AF.Silu, ["ACC"], ["CVb"])
                        if own:
                            bk = pa_next()
                            proj_fm(bk, ws[3][0], ws[3][1], xc0, 512)
                            act(ZS[:, :], PS[bk][:, 0:512], AF.Silu, [PSK[bk]], ["ZS"])
                        t0 = grp * 4
                        gdn_group(128, 4, own, h, GD[:, t0:t0 + 4, h:h + 1], BT[:, t0:t0 + 4, h:h + 1], NBm[:, t0:t0 + 4, h:h + 1],
                                  GDK + BTK + NBK, MI[:, :, :], MS[:, :, :], lambda g: (S32[:], Sb_[:], "S", ("S", "b")), (grp - 2) * 512)
                    dma("sp", D["gsp"][h], S32[:], ["S"], [])
                    dma("sp", S32A[:], D["sg"][:, h].rearrange("i d e -> d i e"), [], [("SA", g) for g in range(16)])
                    for comp in range(3):
                        w_, wk_ = ws[comp]
                        rk = "RAWS%d" % comp
                        cidx = comp * 8 + h
                        bk = pa_next()
                        proj_fm(bk, w_, wk_, SMP0, 64)
                        cp("act", RAWS[comp][:, :, 3:7], PS[bk][:, 0:64].rearrange("p (i t) -> p i t", t=4), [PSK[bk]], [rk])
                        cp("dve", RAWS[comp][:, :, 0:3], SCT[:, cidx, :].rearrange("p (i r) -> p i r", r=3), ["SCT"], [rk])
                        cp("pool", t48[:, :].rearrange("p (i r) -> p i r", r=3), RAWS[comp][:, :, 4:7], [rk], ["t48"])
                        tr(PS[7][0:48, 0:128], t48[:, :], ident_f[:], ["t48", "ident_f"], ["ps7"])
                        cp("dve", gco[0:48, :], PS[7][0:48, 0:128], ["ps7"], ["gco"])
                        dma("sp", D["gcs"][:, cidx * 128:(cidx + 1) * 128], gco[0:48, :], ["gco"], [])
                        flat = RAWS[comp][:, :, :].rearrange("p i s -> p (i s)")
                        conv_silu(flat, 109, cidx, None, rk)
                        accv = ACC[:, 0:112].rearrange("p (i s) -> p i s", s=7)[:, :, 0:4]
                        if comp == 0:
                            act(CQ[:, 0:64].rearrange("p (i t) -> p i t", t=4), accv, AF.Silu, ["ACC"], ["CQ"])
                            fm_norm("CQ", CQ[:, 0:64], 64, 128.0 ** -0.5, QTg[:, 0:64], ["QTg"], st2, mean=False)
                        elif comp == 1:
                            act(CQ[:, 0:64].rearrange("p (i t) -> p i t", t=4), accv, AF.Silu, ["ACC"], ["CQ"])
                            fm_norm("CQ", CQ[:, 0:64], 64, 1.0, KTg[:, 0:64], ["KTg"], st2, mean=False)
                        else:
                            act(CVb[:, 0:64].rearrange("p (i t) -> p i t", t=4), accv, AF.Silu, ["ACC"], ["CVb"])
                    bk = pa_next()
                    proj_fm(bk, ws[3][0], ws[3][1], SMP0, 64)
                    act(ZS[:, 0:64], PS[bk][:, 0:64], AF.Silu, [PSK[bk]], ["ZS"])
                    for i0 in (0, 8):
                        gdn_group(4, 8, True, h, GDS[0:4, i0:i0 + 8, h:h + 1], GDS[0:4, i0:i0 + 8, 8 + h:9 + h],
                                  GDS[0:4, i0:i0 + 8, 16 + h:17 + h], ["GDS"], MI4[0:4, 0:8, :], MS4[0:4, 0:8, :],
                                  (lambda i0_: (lambda g: (S32A[:, i0_ + g, :], Sb_[:], ("SA", i0_ + g), ("S", "b"))))(i0), 1024 + i0 * 4,
                                  c0=i0 * 4, chain=False)
                    dma("sp", D["gss"][:, h].rearrange("i d e -> d i e"), S32A[:], [("SA", g) for g in range(16)], [])
                P.emit()

        with contextlib.ExitStack() as ph2c:
            n_ssb = 16 if stage >= 4 else 0

            def U_(name, shape, dt):
                return sb(name, shape, dt, ph2c)

            ptb = U_("ptb", [128, 256], I32)
            IDX = U_("IDX", [128, 256], I32)
            pid = U_("pid", [128, 1], F32)
            BH512 = U_("BH512", [128, 16, 8, 4], F32)
            MN = U_("MN", [128, 8, 4], F32)
            KPb = [U_("KPb%d" % i, [128, 1024], BF16) for i in range(2)]
            KTall = U_("KTall", [128, 16, 8, 128], BF16)
            VBall = U_("VBall", [128, 16, 1024], BF16)
            VNb = U_("VNb", [128, 1024], BF16)
            ZP = U_("ZP", [128, 512], F32)
            SPt = U_("SPt", [128, 512], F32)
            TOa = U_("TOa", [128, 17, 32], F32)
            TOb = U_("TOb", [128, 17, 32], F32)
            Wsb = U_("Wsb", [128, 512], BF16)
            OACC = U_("OACC", [128, 32], F32)
            if n_ssb:
                dma("sp", ptb[:], D["pt"][0:1, :].partition_broadcast(128), [], ["ptb"])
                P.op("pool", lambda e: e.iota(pid[:], pattern=[[0, 1]], base=0, channel_multiplier=1, allow_small_or_imprecise_dtypes=True), [], ["pid"])
                ptf = SPt[:, 0:256]
                cp("dve", ptf, ptb[:], ["ptb"], ["SPt"])
                ts("dve", ptf, ptf, 128.0, pid[:, 0:1], ALU.mult, ALU.add, ["SPt", "pid"], ["SPt"])
                cp("dve", IDX[:], ptf, ["SPt"], ["IDX"])
                for j in range(16):
                    cp("dve", BH512[:, j, :, :], sbb[:, :].rearrange("p (h o) -> p h o", o=1).broadcast_to([128, 8, 4]), ["sbb"], ["BH512"])
                memset("pool", MN[:], 1.0, ["MN"])
                asel(MN[:], MN[:], [[0, 8], [1, 4]], ALU.is_ge, -1, -1, ["MN"], ["MN"])
                memset("pool", TOa[:], 0.0, ["TOa"])
                memset("pool", TOb[:], 0.0, ["TOb"])
            BHf = BH512[:].rearrange("p j h t -> p (j h t)")
            MN2 = MN[:].rearrange("p h t -> p (h t)")

            for i in range(n_ssb):
                for j in range(16):
                    b = j % 2
                    col = i * 16 + j
                    P.op("pool", (lambda o, ix: (lambda e: e.indirect_dma_start(out=o, out_offset=None, in_=D["ck"],
                         in_offset=bass.IndirectOffsetOnAxis(ap=ix, axis=0))))(KPb[b][:, :], IDX[:, col:col + 1]), ["IDX"], ["KPb%d" % b], dma=True)
                    P.op("pool", (lambda o, ix: (lambda e: e.indirect_dma_start(out=o, out_offset=None, in_=D["cv"],
                         in_offset=bass.IndirectOffsetOnAxis(ap=ix, axis=0))))(VBall[:, j, :], IDX[:, col:col + 1]), ["IDX"], [("VB", j)], dma=True)
                    bank = j % 2
                    pv = psb(bank)
                    for h in range(8):
                        tr(pv[:, h * 128:(h + 1) * 128], KPb[b][:, h * 128:(h + 1) * 128], ident_b[:], ["KPb%d" % b, "ident_b"], [PSK[bank]])
                    cp("act" if j % 2 == 0 else "dve", KTall[:, j, :, :], pv[:, 0:1024].rearrange("p (h s) -> p h s", h=8), [PSK[bank]], [("KT", j)])
                dma("pool", VNb[0:4, :], D["svs"][4 * i:4 * i + 4, :], [], ["VNb"])
                for h in range(8):
                    mm(PS[2][0:4, h * 4:(h + 1) * 4], KTS[:, h, 4 * i:4 * i + 4], QTS[:, h, 4 * i:4 * i + 4], True, True, [], ["ps2"])
                stt(ZP[0:4, 0:32], PS[2][0:4, 0:32], SCALE, BHf[0:4, 0:32], ALU.mult, ALU.add, ["ps2", "BH512"], ["ZP"])
                act(SPt[0:4, 0:32], ZP[0:4, 0:32], AF.Exp, ["ZP"], ["SPt"])
                act(SPt[0:4, 0:32], SPt[0:4, 0:32], AF.Ln, ["SPt"], ["SPt"], bias=1.0)
                tt("dve", SPt[0:4, 0:32], SPt[0:4, 0:32], MN2[0:4, :], ALU.mult, ["SPt", "MN"], ["SPt"])
                mm(PS[3][0:4, 0:32], tri_gt[0:4, 0:4], SPt[0:4, 0:32], True, True, ["SPt", "tri_gt"], ["ps3"])
                mm(PS[4][:, 0:32], ones_f[0:4, :], SPt[0:4, 0:32], True, True, ["SPt", "ones_f"], ["ps4"])
                tt("dve", ZP[0:4, 0:32], ZP[0:4, 0:32], SPt[0:4, 0:32], ALU.subtract, ["ZP", "SPt"], ["ZP"])
                tt("dve", ZP[0:4, 0:32], ZP[0:4, 0:32], PS[3][0:4, 0:32], ALU.subtract, ["ZP", "ps3"], ["ZP"])
                cp("dve", TOa[:, 16, :], PS[4][:, 0:32], ["ps4"], ["TOa"])
                act(Wsb[0:4, 0:32], ZP[0:4, 0:32], AF.Exp, ["ZP"], ["Wsb"])
                tt("dve", Wsb[0:4, 0:32], Wsb[0:4, 0:32], MN2[0:4, :], ALU.mult, ["Wsb", "MN"], ["Wsb"])
                for h in range(8):
                    mm(PS[5][:, h * 4:(h + 1) * 4], VNb[0:4, h * 128:(h + 1) * 128], Wsb[0:4, h * 4:(h + 1) * 4], True, True, ["Wsb", "VNb"], ["ps5"])
                cp("dve", OACC[:], PS[5][:, 0:32], ["ps5"], ["OACC"])
                KTK_ = [("KT", j) for j in range(16)]
                for j in range(16):
                    for h in range(8):
                        c = (j * 8 + h) * 4
                        mm(PS[2][:, c:c + 4], KTall[:, j, h, :], QTS[:, h, 4 * i:4 * i + 4], True, True, KTK_, ["ps2"])
                stt(ZP[:, :], PS[2][:, :], SCALE, BHf[:, :], ALU.mult, ALU.add, ["ps2", "BH512"], ["ZP"])
                act(SPt[:, :], ZP[:, :], AF.Exp, ["ZP"], ["SPt"])
                act(SPt[:, :], SPt[:, :], AF.Ln, ["SPt"], ["SPt"], bias=1.0)
                mm(PS[3][:, :], tri_gt[:], SPt[:, :], True, True, ["SPt", "tri_gt"], ["ps3"])
                mm(PS[4][:, :], ones_f[:], SPt[:, :], True, True, ["SPt", "ones_f"], ["ps4"])
                tt("dve", ZP[:, :], ZP[:, :], SPt[:, :], ALU.subtract, ["ZP", "SPt"], ["ZP"])
                tt("dve", ZP[:, :], ZP[:, :], PS[3][:, :], ALU.subtract, ["ZP", "ps3"], ["ZP"])
                cp("dve", TOa[:, 0:16, :], PS[4][:, :].rearrange("p (j c) -> p j c", j=16), ["ps4"], ["TOa"])
                src_, sk_, dst_, dk_ = TOa, "TOa", TOb, "TOb"
                for sh in (1, 2, 4, 8, 16):
                    n_ = 17 - sh
                    tt("pool", dst_[:, 0:n_, :], src_[:, 0:n_, :], src_[:, sh:17, :], ALU.add, [sk_], [dk_])
                    cp("pool", dst_[:, n_:17, :], src_[:, n_:17, :], [sk_], [dk_])
                    src_, sk_, dst_, dk_ = dst_, dk_, src_, sk_
                tt("dve", ZP[:, :].rearrange("p (j c) -> p j c", j=16), ZP[:, :].rearrange("p (j c) -> p j c", j=16), src_[:, 1:17, :],
                   ALU.subtract, ["ZP", sk_], ["ZP"])
                act(Wsb[:, :], ZP[:, :], AF.Exp, ["ZP"], ["Wsb"])
                VBK_ = [("VB", j) for j in range(16)]
                for h in range(8):
                    for j in range(16):
                        c = (j * 8 + h) * 4
                        mm(PS[5][:, h * 4:(h + 1) * 4], VBall[:, j, h * 128:(h + 1) * 128], Wsb[:, c:c + 4], j == 0, j == 15, ["Wsb"] + VBK_, ["ps5"])
                tt("dve", OACC[:], OACC[:], PS[5][:, 0:32], ALU.add, ["OACC", "ps5"], ["OACC"])
                cp("act", CT[:, 0:8, 1024 + 4 * i:1024 + 4 * i + 4], OACC[:].rearrange("p (h t) -> p h t", t=4), ["OACC"], [("CTs", i)])
            P.emit()

        if stage >= 5:
            X1 = BIG.bitcast(F32).reshape([128, 8, 2112])
            with contextlib.ExitStack() as ph3:
                X1s = sb("X1s", [128, 2048], F32, ph3)

                def x1_tile(t):
                    if t < 8:
                        return X1[:, t, 0:2048], 128, ("X1", t)
                    return X1s[0:64, :], 64, ("X1", 8)

                def ct_cols(t):
                    return (t * 128, 128) if t < 8 else (1024, 64)

                with contextlib.ExitStack() as pa3:
                    WO = [sb("WO%d" % i, [128, 16, 512], BF16, pa3) for i in range(2)]
                    xres = [sb("xres%d" % i, [128, 512], F32, pa3) for i in range(2)]
                    cnt = 0
                    for n in range(4):
                        wo, wok = WO[n % 2], "WO%d" % (n % 2)
                        load_w(wo, wok, D["w_out"], 0, 16, n * 512, 512, None)
                        for t in range(9):
                            xa, nn, xk = x1_tile(t)
                            c0, _ = ct_cols(t)
                            xr, xrk = xres[cnt % 2], "xres%d" % (cnt % 2)
                            bk = cnt % 2
                            cnt += 1
                            src = D["xp"][1024 + t * 128:1024 + (t + 1) * 128, n * 512:(n + 1) * 512] if t < 8 else D["xs"][:, n * 512:(n + 1) * 512]
                            dma("sp", xr[0:nn, :], src, [], [xrk])
                            for k in range(16):
                                mm(PS[bk][0:nn, :], CT[:, k, c0:c0 + nn], wo[:, k, :], k == 0, k == 15, [wok], [PSK[bk]])
                            tt("dve", xa[:, n * 512:(n + 1) * 512], PS[bk][0:nn, :], xr[0:nn, :], ALU.add, [PSK[bk], xrk], [xk + (n,)])
                    P.emit()

                def norm_to_ct(st):
                    for t in range(9):
                        xa, nn, xk = x1_tile(t)
                        c0, _ = ct_cols(t)
                        junk, ss, xb = st["junk"], st["ss"], st["xb"]
                        act(xb[0:nn, :], xa, AF.Square, [xk], ["xb", "ss"], accum=ss[0:nn, :])
                        act(ss[0:nn, :], ss[0:nn, :], AF.Ln, ["ss"], ["ss"], bias=EPS, scale=1.0 / 2048)
                        act(ss[0:nn, :], ss[0:nn, :], AF.Exp, ["ss"], ["ss"], scale=-0.5)
                        ts("dve", xb[0:nn, :], xa, ss[0:nn, 0:1], None, ALU.mult, None, [xk, "ss"], ["xb"])
                        for half in range(2):
                            bank = 6 + half
                            pv = psb(bank)
                            for k in range(8):
                                kk = half * 8 + k
                                tr(pv[:, k * 128:k * 128 + nn], xb[0:nn, kk * 128:(kk + 1) * 128], ident_b[0:nn, 0:nn], ["xb", "ident_b"], [PSK[bank]])
                            srcv = pv[:, 0:1024].rearrange("p (a b) -> p a b", a=8)[:, :, 0:nn]
                            cp("act" if half == 0 else "dve", CT[:, half * 8:half * 8 + 8, c0:c0 + nn], srcv, [PSK[bank]], [("CT", t, half)])

                CTK = [("CT", t, hf_) for t in range(9) for hf_ in range(2)]
                with contextlib.ExitStack() as pb3:
                    st3 = {"junk": None, "ss": sb("ss3", [128, 1], F32, pb3), "xb": sb("xb3", [128, 2048], BF16, pb3),
                           "sq": sb("sq3", [128, 512], BF16, pb3), "rr": sb("rr3", [128, 512], F32, pb3)}
                    XQ = sb("XQ", [128, 4, NTO], BF16, pb3)
                    XO = sb("XO", [128, 4, NTO], BF16, pb3)
                    wxq = [sb("wxq%d" % i, [128, 16, 128], BF16, pb3) for i in range(2)]
                    WXO = sb("WXO", [128, 4, 2048], BF16, pb3)
                    EM = [sb("EM%d" % i, [128, 512], BF16, pb3) for i in range(2)]
                    rden = sb("rden", [128, 512], F32, pb3)
                    CK = [sb("CK%d" % i, [128, 2, 512], F32, pb3) for i in range(2)]
                    CV = [sb("CVm%d" % i, [128, 2, 512], F32, pb3) for i in range(2)]
                    KTm = sb("KTm", [128, 4, 256], BF16, pb3)
                    Vbm = sb("Vbm", [128, 2, 512], BF16, pb3)
                    Es = sb("Es", [128, 32], BF16, pb3)
                    norm_to_ct(st3)
                    for n_ in range(4):
                        load_w(WXO, "WXO", D["w_xo"], 0, 4, n_ * 512, 512, None, dcol0=n_ * 512)
                    for h in range(4):
                        w_, wk_ = wxq[h % 2], "wxq%d" % (h % 2)
                        load_w(w_, wk_, D["w_xq"], 0, 16, h * 128, 128, gn[:, 1])
                        for c0, T in ((0, 512), (512, 512), (1024, 64)):
                            bk = pa_next()
                            for k in range(16):
                                mm(PS[bk][:, 0:T], w_[:, k, :], CT[:, k, c0:c0 + T], k == 0, k == 15, [wk_] + CTK, [PSK[bk]])
                            fm_norm(PSK[bk], PS[bk][:, 0:T], T, hg[:, 2:3], XQ[:, h, c0:c0 + T], [("XQ", h, c0)], st3, ["hg"])
                        for g in range(2):
                            c0 = g * 512
                            for mt in range(2):
                                mm(PS[2 + mt][:, :], MKT[:, h, mt * 128:(mt + 1) * 128], XQ[:, h, c0:c0 + 512], True, True, [("XQ", h, c0)], [PSK[2 + mt]])
                                act(EM[mt][:, :], PS[2 + mt][:, :], AF.Exp, [PSK[2 + mt]], ["EM%d" % mt], scale=SCALE)
                            for mt in range(2):
                                mm(PS[4][:, :], ones_b[:], EM[mt][:, :], mt == 0, mt == 1, ["EM%d" % mt, "ones_b"], ["ps4"])
                            for mt in range(2):
                                mm(PS[5][:, :], MV[:, mt, h * 128:(h + 1) * 128], EM[mt][:, :], mt == 0, mt == 1, ["EM%d" % mt], ["ps5"])
                            P.op("dve", (lambda o, i_: (lambda e: e.reciprocal(out=o, in_=i_)))(rden[:, :], PS[4][:, :]), ["ps4"], ["rden"])
                            tt("dve", XO[:, h, c0:c0 + 512], PS[5][:, :], rden[:, :], ALU.mult, ["ps5", "rden"], [("XO", h, c0)])
                    XQK = [("XQ", h, 1024) for h in range(4)]
                    for i in range(16):
                        ck_, ckk = CK[i % 2], "CK%d" % (i % 2)
                        cv_, cvk = CV[i % 2], "CVm%d" % (i % 2)
                        dma("sp", ck_[:], D["cmk"][i].rearrange("(t p) c -> p t c", p=128), [], [ckk])
                        dma("sp", cv_[:], D["cmv"][i].rearrange("(t p) c -> p t c", p=128), [], [cvk])
                        for mt in range(2):
                            for h in range(4):
                                tr(PS[mt][:, h * 128:(h + 1) * 128], ck_[:, mt, h * 128:(h + 1) * 128], ident_f[:], [ckk, "ident_f"], [PSK[mt]])
                            cp("act" if mt == 0 else "dve", KTm[:, :, mt * 128:(mt + 1) * 128], PS[mt][:, :].rearrange("p (h m) -> p h m", h=4),
                               [PSK[mt]], [("KTm", mt)])
                        cp("pool", Vbm[:], cv_[:], [cvk], ["Vbm"])
                        for mt in range(2):
                            for h in range(4):
                                mm(PS[2][:, mt * 16 + h * 4:mt * 16 + h * 4 + 4], KTm[:, h, mt * 128:(mt + 1) * 128], XQ[:, h, 1024 + 4 * i:1024 + 4 * i + 4],
                                   True, True, [("KTm", 0), ("KTm", 1)] + XQK, ["ps2"])
                        act(Es[:, :], PS[2][:, 0:32], AF.Exp, ["ps2"], ["Es"], scale=SCALE)
                        for mt in range(2):
                            mm(PS[4][:, 0:16], ones_b[:], Es[:, mt * 16:(mt + 1) * 16], mt == 0, mt == 1, ["Es", "ones_b"], ["ps4"])
                        for h in range(4):
                            for mt in range(2):
                                mm(PS[5][:, h * 4:(h + 1) * 4], Vbm[:, mt, h * 128:(h + 1) * 128], Es[:, mt * 16 + h * 4:mt * 16 + h * 4 + 4],
                                   mt == 0, mt == 1, ["Es", "Vbm"], ["ps5"])
                        P.op("dve", (lambda o, i_: (lambda e: e.reciprocal(out=o, in_=i_)))(rden[:, 0:16], PS[4][:, 0:16]), ["ps4"], ["rden"])
                        tt("dve", XO[:, :, 1024 + 4 * i:1024 + 4 * i + 4], PS[5][:, 0:16].rearrange("p (h t) -> p h t", t=4),
                           rden[:, 0:16].rearrange("p (h t) -> p h t", t=4), ALU.mult, ["ps5", "rden"], [("XOs", i)])
                    XOK = [("XO", h, c0) for h in range(4) for c0 in (0, 512)] + [("XOs", i) for i in range(16)]
                    for t in range(9):
                        xa, nn, xk = x1_tile(t)
                        c0, _ = ct_cols(t)
                        for n in range(4):
                            bk = pa_next()
                            for k in range(4):
                                mm(PS[bk][0:nn, :], XO[:, k, c0:c0 + nn], WXO[:, k, n * 512:(n + 1) * 512], k == 0, k == 3, ["WXO"] + XOK, [PSK[bk]])
                            tt("dve", xa[:, n * 512:(n + 1) * 512], xa[:, n * 512:(n + 1) * 512], PS[bk][0:nn, :], ALU.add, [PSK[bk], xk], [xk])
                    P.emit()

                with contextlib.ExitStack() as pc3:
                    st4 = {"junk": None, "ss": sb("ss4", [128, 1], F32, pc3), "xb": sb("xb4", [128, 2048], BF16, pc3)}
                    norm_to_ct(st4)
                    P.emit()
                with contextlib.ExitStack() as pc3:
                    HT = sb("HT", [128, 44, 576], BF16, pc3)
                    FH = 5632
                    for half in range(2):
                        segs = ((0, 512, 0), (1024, 64, 512)) if half == 0 else ((512, 512, 0),)
                        with contextlib.ExitStack() as pg3:
                            WG = [sb("WG%d_%d" % (i, half), [128, 16, 128], BF16, pg3) for i in range(4)]
                            SG = [sb("SG%d_%d" % (i, half), [128, 576], F32, pg3) for i in range(2)]
                            for hc in range(44):
                                wg, wgk = WG[(2 * hc) % 4], "WG%d" % ((2 * hc) % 4)
                                wu, wuk = WG[(2 * hc + 1) % 4], "WG%d" % ((2 * hc + 1) % 4)
                                load_w(wg, wgk, D["w_gate_up"], 0, 16, hc * 128, 128, gn[:, 2])
                                load_w(wu, wuk, D["w_gate_up"], 0, 16, FH + hc * 128, 128, gn[:, 2])
                                sg, sgk = SG[hc % 2], "SG%d" % (hc % 2)
                                pb_ = (hc % 2) * 4
                                for si_, (c0, T, o0) in enumerate(segs):
                                    bg, bu = pb_ + 2 * si_, pb_ + 2 * si_ + 1
                                    for k in range(16):
                                        mm(PS[bg][:, 0:T], wg[:, k, :], CT[:, k, c0:c0 + T], k == 0, k == 15, [wgk], [PSK[bg]])
                                    for k in range(16):
                                        mm(PS[bu][:, 0:T], wu[:, k, :], CT[:, k, c0:c0 + T], k == 0, k == 15, [wuk], [PSK[bu]])
                                    act(sg[:, o0:o0 + T], PS[bg][:, 0:T], AF.Silu, [PSK[bg]], [(sgk, o0)])
                                    tt("dve", HT[:, hc, o0:o0 + T], PS[bu][:, 0:T], sg[:, o0:o0 + T], ALU.mult, [PSK[bu], (sgk, o0)], [("HT", hc, o0)])
                            P.emit()
                        with contextlib.ExitStack() as pd3:
                            WD = [sb("WD%d_%d" % (i, half), [128, 44, 128], BF16, pd3) for i in range(2)]
                            YT = [sb("YT%d_%d" % (i, half), [128, 576], F32, pd3) for i in range(2)]
                            for ocb in range(16):
                                wd, wdk = WD[ocb % 2], "WD%d" % (ocb % 2)
                                load_w(wd, wdk, D["w_down"], 0, 44, ocb * 128, 128, None)
                                yt, ytk = YT[ocb % 2], "YT%d" % (ocb % 2)
                                b0 = (ocb % 2) * 2
                                for si_, (c0, T, o0) in enumerate(segs):
                                    bk = b0 + si_
                                    for k in range(44):
                                        mm(PS[bk][:, 0:T], wd[:, k, :], HT[:, k, o0:o0 + T], k == 0, k == 43, [wdk], [PSK[bk]])
                                    cp("act", yt[:, o0:o0 + T], PS[bk][:, 0:T], [PSK[bk]], [(ytk, o0)])
                                tb = 4 + (ocb % 2)
                                for j in range(4):
                                    tr(PS[tb][:, j * 128:(j + 1) * 128], yt[:, j * 128:(j + 1) * 128], ident_f[:], [(ytk, 0), "ident_f"], [PSK[tb]])
                                for j in range(4):
                                    t = half * 4 + j
                                    xs_ = X1[:, t, ocb * 128:(ocb + 1) * 128]
                                    tt("dve", xs_, xs_, PS[tb][:, j * 128:(j + 1) * 128], ALU.add, [PSK[tb], ("X1", t)], [("X1", t)])
                                if half == 0:
                                    tr(PS[6 + (ocb % 2)][0:64, 0:128], yt[:, 512:576], ident_f[:], [(ytk, 512), "ident_f"], [PSK[6 + (ocb % 2)]])
                                    xs_ = X1s[0:64, ocb * 128:(ocb + 1) * 128]
                                    tt("dve", xs_, xs_, PS[6 + (ocb % 2)][0:64, 0:128], ALU.add, [PSK[6 + (ocb % 2)], ("X1", 8)], [("X1", 8)])
                            for j in range(4):
                                t = half * 4 + j
                                dma("sp", D["yp"][t * 128:(t + 1) * 128, :], X1[:, t, 0:2048], [("X1", t)], [])
                            P.emit()
                    dma("sp", D["ys"][:, :], X1s[0:64, :], [("X1", 8)], [])
                    P.emit()

        P.emit(final=True)
    return nc


def _prep_inputs(inp):
    n_phys = inp["cache_sb_k"].shape[1]
    ck = np.ascontiguousarray(inp["cache_sb_k"][0]).reshape(n_phys * 128, 1024)
    cv = np.ascontiguousarray(inp["cache_sb_v"][0]).reshape(n_phys * 128, 1024)
    maps = []
    for c in range(8):
        b, hf = c // 2, c % 2
        xp = np.concatenate([inp["x_prompt"][b, 0:1024], inp["x_prompt"][b, hf * 1024:hf * 1024 + 1024]], axis=0)
        m = {
            "xp": np.ascontiguousarray(xp, dtype=np.float32),
            "xs": np.ascontiguousarray(inp["x_sample"][16 * c:16 * c + 16]).reshape(64, 2048),
            "mem": np.ascontiguousarray(inp["mem_prompt"][b]),
            "ck": ck, "cv": cv,
            "pt": np.ascontiguousarray(inp["page_table"][16 * c:16 * c + 16]).reshape(1, 256).astype(np.int32),
            "sg": np.ascontiguousarray(inp["state_gdn"][0, 16 * c:16 * c + 16]),
            "sc": np.ascontiguousarray(inp["state_gdn_conv"][0, 16 * c:16 * c + 16]).reshape(48, 3072),
            "cmk": np.ascontiguousarray(inp["cache_mem_k"][0, 16 * c:16 * c + 16]).reshape(16, 256, 512),
            "cmv": np.ascontiguousarray(inp["cache_mem_v"][0, 16 * c:16 * c + 16]).reshape(16, 256, 512),
            "flags": np.array([[float(hf), 0.0 if hf else NEG, 0.0, 0.0]], np.float32),
        }
        for n, s in W_NAMES:
            m[n] = np.ascontiguousarray(inp[n][0]).reshape(s)
        maps.append(m)
    return n_phys, maps


def _assemble(res):
    R = res.results
    f = np.float32
    yp = np.zeros((4, 2048, 2048), f); skp = np.zeros((1, 4, 2048, 8, 128), f); svp = np.zeros((1, 4, 2048, 8, 128), f)
    ys = np.zeros((128, 4, 2048), f); sks = np.zeros((1, 128, 4, 8, 128), f); svs = np.zeros((1, 128, 4, 8, 128), f)
    gsp = np.zeros((1, 4, 8, 128, 128), f); gcp = np.zeros((1, 4, 3, 3072), f)
    gss = np.zeros((1, 128, 8, 128, 128), f); gcs = np.zeros((1, 128, 3, 3072), f)
    mkp = np.zeros((1, 4, 256, 4, 128), f); mvp = np.zeros((1, 4, 256, 4, 128), f)
    for c in range(8):
        b, hf = c // 2, c % 2
        r = R[c]
        sl = slice(hf * 1024, hf * 1024 + 1024)
        yp[b, sl] = r["yp"]
        skp[0, b, sl] = r["skp"].reshape(1024, 8, 128)
        svp[0, b, sl] = r["svp"].reshape(1024, 8, 128)
        ss = slice(16 * c, 16 * c + 16)
        ys[ss] = r["ys"].reshape(16, 4, 2048)
        sks[0, ss] = r["sks"].reshape(16, 4, 8, 128)
        svs[0, ss] = r["svs"].reshape(16, 4, 8, 128)
        gss[0, ss] = r["gss"]
        gcs[0, ss] = r["gcs"].reshape(16, 3, 3072)
        if hf == 1:
            gsp[0, b] = r["gsp"]
            gcp[0, b] = r["gcp"]
        else:
            mkp[0, b] = r["mkp"].reshape(256, 4, 128)
            mvp[0, b] = r["mvp"].reshape(256, 4, 128)
    return (yp, ys, skp, svp, sks, svs, gsp, gcp, gss, gcs, mkp, mvp)


def kernel(**inputs):
    inp = {k: np.asarray(v) for k, v in inputs.items()}
    n_phys, maps = _prep_inputs(inp)
    nc = build(n_phys)
    res = run_bass_kernel_spmd(nc, maps, core_ids=list(range(8)))
    return _assemble(res)
```

```python
import contextlib
import os
import numpy as np
import concourse.bass as bass
import concourse.mybir as mybir
from concourse.bass_utils import run_bass_kernel_spmd

F32 = mybir.dt.float32
BF16 = mybir.dt.bfloat16
I32 = mybir.dt.int32
F32R = mybir.dt.float32r
AF = mybir.ActivationFunctionType
ALU = mybir.AluOpType

ENGS = ("pe", "act", "dve", "pool", "sp")
NDSEM = 12
EPS = 1e-6
NEG = -30000.0


class Instr:
    __slots__ = ("eng", "fn", "dma", "deps", "signal", "count", "dsem", "dcount", "barrier")

    def __init__(self, eng, fn, dma):
        self.eng = eng
        self.fn = fn
        self.dma = dma
        self.deps = []
        self.signal = False
        self.count = 0
        self.dsem = None
        self.dcount = 0
        self.barrier = None


class Prog:
    def __init__(self, nc, stack):
        self.nc = nc
        self.streams = {e: [] for e in ENGS}
        self.last_writer = {}
        self.readers = {}
        self.ndma = {e: 0 for e in ENGS}
        self.dma_hist = {e: [] for e in ENGS}
        self.ccount = {e: 0 for e in ENGS}
        self.waited = {e: {} for e in ENGS}
        self.sems = {e: stack.enter_context(nc.semaphore("s_" + e)) for e in ENGS}
        self.dsems = {}
        for e in ("sp", "pool", "act"):
            for i in range(NDSEM):
                self.dsems[(e, i)] = stack.enter_context(nc.semaphore("d_%s_%d" % (e, i)))
        self.pending_barrier = None

    def op(self, eng, fn, reads=(), writes=(), dma=False):
        ins = Instr(eng, fn, dma)
        deps = {}
        px = [k for k in reads if isinstance(k, str) and k[:2] == "ps" and k[2:].isdigit()]
        if px:
            reads = [k for k in reads if k not in px]
            writes = list(writes) + px
        for k in reads:
            w = self.last_writer.get(k)
            if w is not None:
                deps[id(w)] = w
        for k in writes:
            w = self.last_writer.get(k)
            if w is not None:
                deps[id(w)] = w
            rd = self.readers.get(k)
            if rd:
                for r in rd.values():
                    deps[id(r)] = r
        ins.deps = list(deps.values())
        rkey = ("d", eng, self.ndma[eng] % NDSEM) if dma else ("c", eng)
        for k in reads:
            self.readers.setdefault(k, {})[rkey] = ins
        for k in writes:
            self.last_writer[k] = ins
            self.readers[k] = {}
        if dma:
            i = self.ndma[eng]
            self.ndma[eng] += 1
            ins.dsem = (eng, i % NDSEM)
            ins.dcount = 16 * (i // NDSEM + 1)
            hist = self.dma_hist[eng]
            if i >= NDSEM:
                ins.deps.append(hist[i - NDSEM])
            hist.append(ins)
        self.streams[eng].append(ins)
        return ins

    def emit(self, final=False):
        nc = self.nc
        streams = self.streams
        for e in ENGS:
            for ins in streams[e]:
                for d in ins.deps:
                    if d.dma:
                        continue
                    if d.eng == ins.eng and not ins.dma and d.eng == "pe":
                        continue
                    d.signal = True
            for ins in reversed(streams[e]):
                if not ins.dma:
                    ins.signal = True
                    break
        for e in ENGS:
            c = self.ccount[e]
            for ins in streams[e]:
                if not ins.dma and ins.signal:
                    c += 1
                    ins.count = c
            self.ccount[e] = c
        bar = []
        for e in ENGS:
            if self.ccount[e]:
                bar.append((("c", e), self.sems[e], self.ccount[e]))
            hist = self.dma_hist[e]
            for i in range(max(0, len(hist) - NDSEM), len(hist)):
                d = hist[i]
                bar.append((("d",) + d.dsem, self.dsems[d.dsem], d.dcount))
        prev_bar = self.pending_barrier

        def run(engobj, e):
            waited = self.waited[e]

            def wait(key, sem, val):
                if waited.get(key, 0) >= val:
                    return
                waited[key] = val
                engobj.wait_ge(sem, val)

            if prev_bar is not None:
                for key, sem, val in prev_bar:
                    if key == ("c", e):
                        continue
                    wait(key, sem, val)
            for ins in streams[e]:
                for d in ins.deps:
                    if d.dma:
                        wait(("d",) + d.dsem, self.dsems[d.dsem], d.dcount)
                    else:
                        if d.eng == e and not ins.dma and e == "pe":
                            continue
                        wait(("c", d.eng), self.sems[d.eng], d.count)
                r = ins.fn(engobj)
                if ins.dma:
                    r.then_inc(self.dsems[ins.dsem], 16)
                elif ins.signal:
                    r.then_inc(self.sems[e], 1)
            if final and e == "sp":
                for key, sem, val in bar:
                    wait(key, sem, val)

        with nc.Block() as block:
            @block.tensor
            def _(eng):
                run(eng, "pe")

            @block.scalar
            def _(eng):
                run(eng, "act")

            @block.vector
            def _(eng):
                run(eng, "dve")

            @block.gpsimd
            def _(eng):
                run(eng, "pool")

            @block.sync
            def _(eng):
                run(eng, "sp")

        self.pending_barrier = bar
        self.streams = {e: [] for e in ENGS}
        self.last_writer = {}
        self.readers = {}


C_SQ, C_SK, C_SV, C_GQ, C_GK, C_GV, C_GA, C_GB, C_GZ = 0, 1024, 2048, 3072, 4096, 5120, 6144, 6152, 6160
NT = 2112
OWN0, SMP0 = 1024, 2048
NTO = 1088

W_NAMES = [
    ("norm_mix_g", [1, 2048]), ("w_in", [2048, 7184]), ("sb_q_norm_g", [1, 128]), ("sb_k_norm_g", [1, 128]),
    ("sb_logit_bias", [1, 8]), ("gdn_conv_w", [4, 3072]), ("gdn_a_log", [1, 8]), ("gdn_dt_bias", [1, 8]),
    ("gdn_out_norm_g", [1, 128]), ("w_out", [2048, 2048]), ("norm_x_g", [1, 2048]), ("norm_mem_g", [1, 2048]),
    ("w_xq", [2048, 512]), ("w_mk", [2048, 512]), ("w_mv", [2048, 512]), ("x_q_norm_g", [1, 128]),
    ("x_k_norm_g", [1, 128]), ("w_xo", [512, 2048]), ("norm_ffn_g", [1, 2048]), ("w_gate_up", [2048, 11264]),
    ("w_down", [5632, 2048]),
]
OUT_SPECS = [
    ("yp", [1024, 2048]), ("ys", [64, 2048]), ("skp", [1024, 1024]), ("svp", [1024, 1024]),
    ("sks", [64, 1024]), ("svs", [64, 1024]), ("gsp", [8, 128, 128]), ("gcp", [3, 3072]),
    ("gss", [16, 8, 128, 128]), ("gcs", [48, 3072]), ("mkp", [256, 512]), ("mvp", [256, 512]),
]


def build(n_phys, stage=99):
    nc = bass.Bass("TRN2", target_bir_lowering=False)

    def din(n, s, dt=F32):
        return nc.dram_tensor(n, list(s), dt, kind="ExternalInput").ap()

    def dout(n, s, dt=F32):
        return nc.dram_tensor(n, list(s), dt, kind="ExternalOutput").ap()

    D = {}
    D["xp"] = din("xp", [2048, 2048])
    D["xs"] = din("xs", [64, 2048])
    D["mem"] = din("mem", [256, 2048])
    D["ck"] = din("ck", [n_phys * 128, 1024])
    D["cv"] = din("cv", [n_phys * 128, 1024])
    D["pt"] = din("pt", [1, 256], I32)
    D["sg"] = din("sg", [16, 8, 128, 128])
    D["sc"] = din("sc", [48, 3072])
    D["cmk"] = din("cmk", [16, 256, 512])
    D["cmv"] = din("cmv", [16, 256, 512])
    D["flags"] = din("flags", [1, 4])
    for n, s in W_NAMES:
        D[n] = din(n, s)
    for n, s in OUT_SPECS:
        D[n] = dout(n, s)
    D["scr_gab"] = nc.dram_tensor("scr_gab", [64, 16], F32, kind="Internal").ap()

    with contextlib.ExitStack() as perm:
        P = Prog(nc, perm)

        def sb(name, shape, dt, st=perm):
            return st.enter_context(nc.sbuf_tensor(name, list(shape), dt))

        def act(out, in_, func, r, w, bias=0.0, scale=1.0, accum=None):
            if accum is None:
                P.op("act", lambda e: e.activation(out=out, in_=in_, func=func, bias=bias, scale=scale), r, w)
            else:
                P.op("act", lambda e: e.activation(out=out, in_=in_, func=func, bias=bias, scale=scale, accum_out=accum), r, w)

        def cp(eng, out, in_, r, w):
            if eng == "act":
                P.op("act", lambda e: e.activation(out=out, in_=in_, func=AF.Copy), r, w)
            else:
                P.op(eng, lambda e: e.tensor_copy(out=out, in_=in_), r, w)

        def tt(eng, out, a, b, op, r, w):
            P.op(eng, lambda e: e.tensor_tensor(out=out, in0=a, in1=b, op=op), r, w)

        def ts(eng, out, a, s1, s2, op0, op1, r, w):
            if s2 is None:
                P.op(eng, lambda e: e.tensor_scalar(out=out, in0=a, scalar1=s1, scalar2=None, op0=op0), r, w)
            else:
                P.op(eng, lambda e: e.tensor_scalar(out=out, in0=a, scalar1=s1, scalar2=s2, op0=op0, op1=op1), r, w)

        def stt(out, a, s, b, op0, op1, r, w):
            P.op("dve", lambda e: e.scalar_tensor_tensor(out=out, in0=a, scalar=s, in1=b, op0=op0, op1=op1), r, w)

        def mm(out, lhsT, rhs, start, stop, r, w):
            P.op("pe", lambda e: e.matmul(out, lhsT=lhsT, rhs=rhs, start=start, stop=stop), r, w)

        def tr(out, in_, ident, r, w):
            P.op("pe", lambda e: e.transpose(out=out, in_=in_, identity=ident), r, w)

        def dma(eng, out, in_, r, w):
            P.op(eng, lambda e: e.dma_start(out=out, in_=in_), r, w, dma=True)

        def memset(eng, ap, val, w):
            P.op(eng, lambda e: e.memset(ap, val), (), w)

        def asel(out, in_, pattern, cmp, base, cm, r, w, fill=0.0):
            P.op("pool", lambda e: e.affine_select(out=out, in_=in_, pattern=pattern, compare_op=cmp, fill=fill,
                                                   base=base, channel_multiplier=cm), r, w)

        PS = [perm.enter_context(nc.psum_tensor("ps%d" % i, [128, 512], F32)) for i in range(8)]
        PSK = ["ps%d" % i for i in range(8)]

        def psb(i):
            return PS[i][:].bitcast(BF16)

        ident_f = sb("ident_f", [128, 128], F32)
        ident_b = sb("ident_b", [128, 128], BF16)
        ones_f = sb("ones_f", [128, 128], F32)
        ones_b = sb("ones_b", [128, 128], BF16)
        tri_gt = sb("tri_gt", [128, 128], F32)
        tri_le = sb("tri_le", [128, 128], F32)
        memset("pool", ones_f[:], 1.0, ["ones_f"])
        memset("pool", ones_b[:], 1.0, ["ones_b"])
        memset("pool", ident_f[:], 0.0, ["ident_f"])
        asel(ident_f[:], ident_f[:], [[-1, 128]], ALU.not_equal, 0, 1, ["ident_f"], ["ident_f"], fill=1.0)
        cp("dve", ident_b[:], ident_f[:], ["ident_f"], ["ident_b"])
        asel(tri_gt[:], ones_f[:], [[-1, 128]], ALU.is_gt, 0, 1, ["ones_f"], ["tri_gt"])
        asel(tri_le[:], ones_f[:], [[1, 128]], ALU.is_ge, 0, -1, ["ones_f"], ["tri_le"])
        tri_gt_r = sb("tri_gt_r", [128, 128], F32)
        ones_r = sb("ones_r", [128, 128], F32)
        cp("dve", tri_gt_r[:].bitcast(F32R), tri_gt[:], ["tri_gt"], ["tri_gt_r"])
        cp("dve", ones_r[:].bitcast(F32R), ones_f[:], ["ones_f"], ["ones_r"])

        flg = sb("flg", [128, 4], F32)
        dma("sp", flg[:], D["flags"][0:1, :].partition_broadcast(128), [], ["flg"])
        sbb = sb("sbb", [128, 8], F32)
        dma("sp", sbb[:], D["sb_logit_bias"][0:1, :].partition_broadcast(128), [], ["sbb"])
        sbb_pre = sb("sbb_pre", [128, 8], F32)
        ts("dve", sbb_pre[:], sbb[:], flg[:, 1:2], None, ALU.add, None, ["sbb", "flg"], ["sbb_pre"])
        alog = sb("alog", [128, 8], F32)
        dma("sp", alog[:], D["gdn_a_log"][0:1, :].partition_broadcast(128), [], ["alog"])
        dtb = sb("dtb", [128, 8], F32)
        dma("sp", dtb[:], D["gdn_dt_bias"][0:1, :].partition_broadcast(128), [], ["dtb"])
        nea = sb("nea", [128, 8], F32)
        act(nea[:], alog[:], AF.Exp, ["alog"], ["nea"])
        ts("dve", nea[:], nea[:], -1.0, None, ALU.mult, None, ["nea"], ["nea"])
        hg_row = sb("hg_row", [128, 128], F32)
        memset("pool", hg_row[:], 0.0, ["hg_row"])
        for i, n in enumerate(["sb_q_norm_g", "sb_k_norm_g", "x_q_norm_g", "x_k_norm_g", "gdn_out_norm_g"]):
            dma("sp", hg_row[i:i + 1, :], D[n][0:1, :], [], ["hg_row"])
        hg = sb("hg", [128, 8], F32)
        tr(PS[7][:, 0:128], hg_row[:, :], ident_f[:, :], ["hg_row", "ident_f"], ["ps7"])
        cp("dve", hg[:], PS[7][:, 0:8], ["ps7"], ["hg"])
        gn = sb("gn", [128, 4, 16, 1], F32)
        gn_row = sb("gn_row", [128, 128], F32)
        memset("pool", gn_row[:], 0.0, ["gn_row"])
        for i, n in enumerate(["norm_mix_g", "norm_x_g", "norm_ffn_g", "norm_mem_g"]):
            dma("sp", gn_row[16 * i:16 * i + 16, :], D[n].rearrange("o (k p) -> (o k) p", p=128), [], ["gn_row"])
        tr(PS[7][:, 128:256], gn_row[:, :], ident_f[:, :], ["gn_row", "ident_f"], ["ps7"])
        cp("dve", gn[:].rearrange("p a k o -> p (a k o)"), PS[7][:, 128:192], ["ps7"], ["gn"])

        BIG = sb("BIG", [128, 16 * NT], BF16)
        XT = BIG.reshape([128, 16, NT])
        CT = sb("CT", [128, 16, NTO], BF16)
        MKT = sb("MKT", [128, 4, 256], BF16)
        MV = sb("MV", [128, 2, 512], BF16)
        QTS = sb("QTS", [128, 8, 64], BF16)
        KTS = sb("KTS", [128, 8, 64], BF16)
        CW = sb("CW", [128, 24, 4], F32)
        SCT = sb("SCT", [128, 24, 48], F32)

        wst = [sb("wst%d" % i, [128, 512], F32) for i in range(4)]
        wcnt = [0]
        cvt_rot = [0]

        def load_w(dst, dkey, src, k0, nk, col0, ncols, gains=None, dcol0=0):
            srcv = src.rearrange("(k p) n -> p k n", p=128)
            per = max(1, 512 // ncols)
            kk = k0
            while kk < k0 + nk:
                n = min(per, k0 + nk - kk)
                i = wcnt[0] % 4
                wcnt[0] += 1
                st = wst[i]
                stv = st[:, 0:n * ncols].rearrange("p (k n) -> p k n", k=n)
                dma("sp", stv, srcv[:, kk:kk + n, col0:col0 + ncols], [], ["wst%d" % i])
                eng = ("dve", "pool", "act")[cvt_rot[0] % 3]
                cvt_rot[0] += 1
                o = dst[:, kk:kk + n, dcol0:dcol0 + ncols]
                if gains is None:
                    cp(eng, o, stv, ["wst%d" % i], [dkey])
                else:
                    if eng == "act":
                        eng = "dve"
                    g = gains[:, kk:kk + n, :].broadcast_to([128, n, ncols])
                    tt(eng, o, stv, g, ALU.mult, ["wst%d" % i, "gn"], [dkey])
                kk += n

        def rms_to_fm(st, src_rows, n, dstT, c0, dkey, xin, xkey, tag):
            dma("sp", xin[0:n, :], src_rows, [], [xkey])
            junk = st["junk"]
            ss = st["ss"]
            act(junk[0:n, :], xin[0:n, :], AF.Square, [xkey], ["junk", "ss"], accum=ss[0:n, :])
            act(ss[0:n, :], ss[0:n, :], AF.Ln, ["ss"], ["ss"], bias=EPS, scale=1.0 / 2048)
            act(ss[0:n, :], ss[0:n, :], AF.Exp, ["ss"], ["ss"], scale=-0.5)
            xb = st["xb"]
            ts("dve", xb[0:n, :], xin[0:n, :], ss[0:n, 0:1], None, ALU.mult, None, [xkey, "ss"], ["xb"])
            for half in range(2):
                bank = 6 + half
                pv = psb(bank)
                for k in range(8):
                    kk = half * 8 + k
                    tr(pv[:, k * 128:k * 128 + n], xb[0:n, kk * 128:(kk + 1) * 128], ident_b[0:n, 0:n],
                       ["xb", "ident_b"], [PSK[bank]])
                src = pv[:, 0:1024].rearrange("p (a b) -> p a b", a=8)[:, :, 0:n]
                cp("act" if half == 0 else "dve", dstT[:, half * 8:half * 8 + 8, c0:c0 + n], src, [PSK[bank]],
                   [(dkey, tag, half)])

        def fm_norm(psk, ps_ap, T, sc, out, wkeys, st, rextra=(), mean=True):
            sq = st["sq"]
            rr = st["rr"]
            act(sq[:, 0:T], ps_ap, AF.Square, [psk], ["sq"])
            mm(PS[5][:, 0:T], ones_b[:], sq[:, 0:T], True, True, ["sq", "ones_b"], ["ps5"])
            act(rr[:, 0:T], PS[5][:, 0:T], AF.Ln, ["ps5"], ["rr"], bias=EPS, scale=(1.0 / 128 if mean else 1.0))
            act(rr[:, 0:T], rr[:, 0:T], AF.Exp, ["rr"], ["rr"], scale=-0.5)
            stt(out, ps_ap, sc, rr[:, 0:T], ALU.mult, ALU.mult, [psk, "rr"] + list(rextra), wkeys)

        with contextlib.ExitStack() as ph:
            rowbuf = sb("rowbuf", [128, 3072], F32, ph)
            memset("pool", rowbuf[:], 0.0, ["rowbuf"])
            dma("sp", rowbuf[0:4, :], D["gdn_conv_w"][:, :], [], ["rowbuf"])
            for c4 in range(6):
                for j in range(4):
                    c = c4 * 4 + j
                    tr(PS[4][:, j * 128:(j + 1) * 128], rowbuf[:, c * 128:(c + 1) * 128], ident_f[:], ["rowbuf", "ident_f"], ["ps4"])
                cp("dve", CW[:, c4 * 4:c4 * 4 + 4, :], PS[4][:, :].rearrange("p (j x) -> p j x", j=4)[:, :, 0:4], ["ps4"], ["CW"])
            dma("sp", rowbuf[0:48, :], D["sc"][:, :], ["rowbuf"], ["rowbuf"])
            for c4 in range(6):
                for j in range(4):
                    c = c4 * 4 + j
                    tr(PS[4][:, j * 128:(j + 1) * 128], rowbuf[:, c * 128:(c + 1) * 128], ident_f[:], ["rowbuf", "ident_f"], ["ps4"])
                cp("dve", SCT[:, c4 * 4:c4 * 4 + 4, :], PS[4][:, :].rearrange("p (j x) -> p j x", j=4)[:, :, 0:48], ["ps4"], ["SCT"])
            P.emit()

        with contextlib.ExitStack() as ph:
            st1 = {
                "junk": sb("junk", [128, 2048], BF16, ph),
                "ss": sb("ss", [128, 1], F32, ph),
                "xb": sb("xb", [128, 2048], BF16, ph),
                "sq": sb("sq1", [128, 512], BF16, ph),
                "rr": sb("rr1", [128, 512], F32, ph),
            }
            xin = [sb("xin%d" % i, [128, 2048], F32, ph) for i in range(2)]
            KD = int(os.environ.get("KDBG", "9"))
            for t in range(17 if KD >= 3 else (1 if KD == 2 else 0)):
                if t < 16:
                    rows, n, c0 = D["xp"][t * 128:(t + 1) * 128, :], 128, t * 128
                else:
                    rows, n, c0 = D["xs"][:, :], 64, SMP0
                rms_to_fm(st1, rows, n, XT, c0, "XT", xin[t % 2], "xin%d" % (t % 2), t)
            MT = sb("MT", [128, 16, 256], BF16, ph)
            for t in range(2 if KD >= 4 else 0):
                rms_to_fm(st1, D["mem"][t * 128:(t + 1) * 128, :], 128, MT, t * 128, "MT", xin[t % 2], "xin%d" % (t % 2), t)
            wmk = sb("wmk", [128, 16, 512], BF16, ph)
            wmv = sb("wmv", [128, 16, 512], BF16, ph)
            load_w(wmk, "wmk", D["w_mk"], 0, 16, 0, 512, gn[:, 3])
            load_w(wmv, "wmv", D["w_mv"], 0, 16, 0, 512, gn[:, 3])
            MTK = [("MT", t, h2) for t in range(2) for h2 in range(2)]
            mk32 = sb("mk32", [128, 256], F32, ph)
            mko = sb("mko", [128, 2, 512], F32, ph)
            mvo = sb("mvo", [128, 2, 512], F32, ph)
            for h in range(4 if KD >= 5 else 0):
                for k in range(16):
                    mm(PS[0][:, 0:256], wmk[:, k, h * 128:(h + 1) * 128], MT[:, k, :], k == 0, k == 15,
                       ["wmk"] + MTK, ["ps0"])
                fm_norm("ps0", PS[0][:, 0:256], 256, hg[:, 3:4], mk32[:, :], ["mk32"], st1, ["hg"])
                cp("act", MKT[:, h, :], mk32[:, :], ["mk32"], [("MKT", h)])
                for t in range(2):
                    tr(PS[1][:, t * 128:(t + 1) * 128], mk32[:, t * 128:(t + 1) * 128], ident_f[:], ["mk32", "ident_f"], ["ps1"])
                cp("dve", mko[:, :, h * 128:(h + 1) * 128], PS[1][:, 0:256].rearrange("p (t d) -> p t d", t=2), ["ps1"], ["mko"])
            if KD >= 5:
                dma("sp", D["mkp"].rearrange("(t p) c -> p t c", p=128), mko[:], ["mko"], [])
            for t in range(2 if KD >= 6 else 0):
                for k in range(16):
                    mm(PS[2 + t][:, :], MT[:, k, t * 128:(t + 1) * 128], wmv[:, k, :], k == 0, k == 15, ["wmv"] + MTK, [PSK[2 + t]])
                cp("dve", mvo[:, t, :], PS[2 + t][:, :], [PSK[2 + t]], ["mvo"])
                cp("act", MV[:, t, :], mvo[:, t, :], ["mvo"], [("MV", t)])
            if KD >= 6:
                dma("sp", D["mvp"].rearrange("(t p) c -> p t c", p=128), mvo[:], ["mvo"], [])
            P.emit()

        SCALE = 128.0 ** -0.5
        with contextlib.ExitStack() as ph:
            st2 = {"sq": sb("sq2", [128, 512], BF16, ph), "rr": sb("rr2", [128, 512], F32, ph)}
            WB = [sb("wb%d" % i, [128, 16, 128], BF16, ph) for i in range(5)]
            wbi = [0]

            def wb_next():
                i = wbi[0] % 5
                wbi[0] += 1
                return WB[i], "wb%d" % i

            pa = [0]

            def pa_next():
                i = pa[0] % 2
                pa[0] += 1
                return i

            def proj_fm(bank, w, wkey, xc0, T):
                for k in range(16):
                    mm(PS[bank][:, 0:T], w[:, k, :], XT[:, k, xc0:xc0 + T], k == 0, k == 15, [wkey], [PSK[bank]])

            with contextlib.ExitStack() as ph2a:
                ph2 = ph
                ph = ph2a
                MSK = sb("MSK", [128, 4, 512], BF16, ph)
                memset("pool", MSK[:], 1.0, ["MSK"])
                for kd in range(4):
                    asel(MSK[:, kd, :], MSK[:, kd, :], [[1, 512]], ALU.is_ge, -128 * kd - 1, -1, ["MSK"], ["MSK"])
                QT = sb("QT", [128, NTO], BF16, ph)
                KT = sb("KT", [128, NT], BF16, ph)
                VH = sb("VH", [128, 17, 128], BF16, ph)
                kn32 = sb("kn32", [128, 512], F32, ph)
                kout = sb("kout", [128, 9, 128], F32, ph)
                vout = sb("vout", [128, 9, 128], F32, ph)
                EXb = [sb("EX%d" % i, [128, 512], F32, ph) for i in range(2)]
                T1b = [sb("T1%d" % i, [128, 512], F32, ph) for i in range(2)]
                Wbb = [sb("Wb%d" % i, [128, 512], BF16, ph) for i in range(2)]
                Rrs = [sb("Rr%d" % i, [128, 512], F32, ph) for i in range(2)]
                n_sb = 8 if stage >= 2 else 0
                for h in range(n_sb):
                    wq, wqk = wb_next()
                    wk, wkk = wb_next()
                    wv, wvk = wb_next()
                    load_w(wq, wqk, D["w_in"], 0, 16, C_SQ + h * 128, 128, gn[:, 0])
                    load_w(wk, wkk, D["w_in"], 0, 16, C_SK + h * 128, 128, gn[:, 0])
                    load_w(wv, wvk, D["w_in"], 0, 16, C_SV + h * 128, 128, gn[:, 0])
                    for xc0, T, qc0 in ((OWN0, 512, 0), (OWN0 + 512, 512, 512), (SMP0, 64, 1024)):
                        bk = pa_next()
                        proj_fm(bk, wq, wqk, xc0, T)
                        fm_norm(PSK[bk], PS[bk][:, 0:T], T, hg[:, 0:1], QT[:, qc0:qc0 + T], [("QT", qc0)], st2, ["hg"])
                    cp("pool", QTS[:, h, :], QT[:, 1024:1088], [("QT", 1024)], [("QTS", h)])
                    for gi, (xc0, T) in enumerate(((0, 512), (512, 512), (OWN0, 512), (OWN0 + 512, 512), (SMP0, 64))):
                        bk = pa_next()
                        proj_fm(bk, wk, wkk, xc0, T)
                        fm_norm(PSK[bk], PS[bk][:, 0:T], T, hg[:, 1:2], kn32[:, 0:T], ["kn32"], st2, ["hg"])
                        cp("act", KT[:, xc0:xc0 + T], kn32[:, 0:T], ["kn32"], [("KT", gi)])
                        if gi >= 2:
                            nt = T // 128 if T >= 128 else 1
                            w_ = 128 if T >= 128 else T
                            for j in range(nt):
                                tr(PS[7][0:w_, j * 128:(j + 1) * 128], kn32[:, j * 128:j * 128 + w_], ident_f[:], ["kn32", "ident_f"], ["ps7"])
                            t0 = (gi - 2) * 4
                            cp("dve", kout[0:w_, t0:t0 + nt, :], PS[7][0:w_, 0:nt * 128].rearrange("p (t d) -> p t d", t=nt),
                               ["ps7"], ["kout"])
                    cp("pool", KTS[:, h, :], KT[:, SMP0:SMP0 + 64], [("KT", 4)], [("KTS", h)])
                    dma("sp", D["skp"][:, h * 128:(h + 1) * 128].rearrange("(t p) d -> p t d", p=128), kout[:, 0:8, :], ["kout"], [])
                    dma("sp", D["sks"][:, h * 128:(h + 1) * 128], kout[0:64, 8, :], ["kout"], [])
                    for g4 in range(5):
                        bk = pa_next()
                        tiles = list(range(g4 * 4, min(g4 * 4 + 4, 17)))
                        for ti, t in enumerate(tiles):
                            n = 128 if t < 16 else 64
                            for k in range(16):
                                mm(PS[bk][0:n, ti * 128:(ti + 1) * 128], XT[:, k, t * 128:t * 128 + n], wv[:, k, :], k == 0, k == 15,
                                   [wvk], [PSK[bk]])
                        nt = len(tiles)
                        n = 128 if g4 < 4 else 64
                        src = PS[bk][0:n, 0:nt * 128].rearrange("p (t d) -> p t d", t=nt)
                        if g4 >= 2:
                            vo = vout[0:n, g4 * 4 - 8:g4 * 4 - 8 + nt, :]
                            cp("dve", vo, src, [PSK[bk]], ["vout"])
                            cp("act", VH[0:n, g4 * 4:g4 * 4 + nt, :], vo, ["vout"], [("VH", g4)])
                        else:
                            cp("act", VH[0:n, g4 * 4:g4 * 4 + nt, :], src, [PSK[bk]], [("VH", g4)])
                    dma("sp", D["svp"][:, h * 128:(h + 1) * 128].rearrange("(t p) d -> p t d", p=128), vout[:, 0:8, :], ["vout"], [])
                    dma("sp", D["svs"][:, h * 128:(h + 1) * 128], vout[0:64, 8, :], ["vout"], [("dram", "svs")])
                    VHK = [("VH", g) for g in range(5)]
                    KTK = [("KT", g) for g in range(5)]
                    tl = [list(range(8 + 4 * q_ + 3, 7, -1)) + list(range(7, -1, -1)) for q_ in range(2)]
                    banks = ((2, 4, 6), (3, 7, 0))
                    for idx in range(max(len(tl[0]), len(tl[1]))):
                        act_ch = [q_ for q_ in range(2) if idx < len(tl[q_])]
                        info = {}
                        for stq in act_ch:
                            tiles = tl[stq]
                            j = tiles[idx]
                            kd = (j - 8) - 4 * stq
                            info[stq] = dict(j=j, first=idx == 0, last=idx == len(tiles) - 1, kd=kd, diag=(j >= 8 and kd >= 0),
                                             bias=(sbb_pre[:, h:h + 1] if j < 8 else sbb[:, h:h + 1]), bk=("sbb_pre" if j < 8 else "sbb"))
                        for stq in act_ch:
                            I_ = info[stq]
                            zb = banks[stq][0]
                            EX, exk = EXb[stq], "EX%d" % stq
                            mm(PS[zb][:, :], KT[:, I_["j"] * 128:(I_["j"] + 1) * 128], QT[:, stq * 512:(stq + 1) * 512], True, True,
                               KTK + [("QT", stq * 512)], [PSK[zb]])
                            act(EX[:].bitcast(F32R), PS[zb][:, :], AF.Exp, [PSK[zb], I_["bk"]], [exk], bias=I_["bias"], scale=SCALE)
                            act(EX[:].bitcast(F32R), EX[:], AF.Ln, [exk], [exk], bias=1.0)
                            if I_["diag"]:
                                tt("dve", EX[:].bitcast(F32R), EX[:], MSK[:, I_["kd"], :], ALU.mult, [exk, "MSK"], [exk])
                        for stq in act_ch:
                            I_ = info[stq]
                            cb_ = banks[stq][1]
                            EX, exk = EXb[stq], "EX%d" % stq
                            Aq, ak_ = Rrs[stq], "Rr%d" % stq
                            mm(PS[cb_][:, :], tri_gt_r[:].bitcast(F32R), EX[:].bitcast(F32R), True, I_["first"], [exk, "tri_gt_r"], [PSK[cb_]])
                            if not I_["first"]:
                                mm(PS[cb_][:, :], ones_r[:].bitcast(F32R), Aq[:].bitcast(F32R), False, True, [ak_, "ones_r"], [PSK[cb_]])
                        for stq in act_ch:
                            I_ = info[stq]
                            zb, cb_ = banks[stq][0], banks[stq][1]
                            EX, exk = EXb[stq], "EX%d" % stq
                            T1, t1k = T1b[stq], "T1%d" % stq
                            Wb, wbk = Wbb[stq], "Wb%d" % stq
                            Aq, ak_ = Rrs[stq], "Rr%d" % stq
                            stt(T1[:], PS[zb][:, :], SCALE, EX[:], ALU.mult, ALU.subtract, [PSK[zb], exk], [t1k])
                            tt("dve", T1[:], T1[:], PS[cb_][:, :], ALU.subtract, [t1k, PSK[cb_]], [t1k])
                            act(Wb[:], T1[:], AF.Exp, [t1k, I_["bk"]], [wbk], bias=I_["bias"])
                            if I_["diag"]:
                                tt("pool", Wb[:], Wb[:], MSK[:, I_["kd"], :], ALU.mult, [wbk, "MSK"], [wbk])
                            if not I_["last"]:
                                if I_["first"]:
                                    cp("dve", Aq[:].bitcast(F32R), EX[:], [exk], [ak_])
                                else:
                                    tt("dve", Aq[:].bitcast(F32R), Aq[:], EX[:], ALU.add, [ak_, exk], [ak_])
                        for stq in act_ch:
                            I_ = info[stq]
                            ob_ = banks[stq][2]
                            Wb, wbk = Wbb[stq], "Wb%d" % stq
                            mm(PS[ob_][:, :], VH[:, I_["j"], :], Wb[:], I_["first"], I_["last"], [wbk] + VHK, [PSK[ob_]])
                            if I_["last"]:
                                cp("act", CT[:, h, stq * 512:(stq + 1) * 512], PS[ob_][:, :], [PSK[ob_]], [("CT", h, stq)])
                P.emit()
                ph = ph2

            with contextlib.ExitStack() as ph2b:
                ph = ph2b
                n_gdn = 8 if stage >= 3 else 0

                def T_(name, shape, dt):
                    return sb(name, shape, dt, ph2b)

                tri_le3 = T_("tri_le3", [128, 1, 128], F32)
                cp("pool", tri_le3[:, 0, :], tri_le[:], ["tri_le"], ["tri_le3"])
                identf3 = T_("identf3", [128, 1, 128], F32)
                cp("pool", identf3[:, 0, :], ident_f[:], ["ident_f"], ["identf3"])
                MI = T_("MI", [128, 4, 128], BF16)
                MS = T_("MS", [128, 4, 128], BF16)
                MI4 = T_("MI4", [128, 16, 4], BF16)
                MS4 = T_("MS4", [128, 16, 4], BF16)
                for m_, nm_, pat_, base_ in ((MI, "MI", [[0, 4], [1, 128]], 0), (MS, "MS", [[0, 4], [1, 128]], -1),
                                             (MI4, "MI4", [[0, 16], [1, 4]], 0), (MS4, "MS4", [[0, 16], [1, 4]], -1)):
                    memset("pool", m_[:], 1.0, [nm_])
                    asel(m_[:], m_[:], pat_, ALU.is_ge, base_, -1, [nm_], [nm_])
                wgab = T_("wgab", [128, 16, 16], BF16)
                GD = T_("GD", [128, 17, 8], F32)
                BT = T_("BT", [128, 17, 8], F32)
                NBm = T_("NBm", [128, 17, 8], F32)
                gtm = T_("gtm", [128, 16], F32)
                GDS = T_("GDS", [128, 16, 24], F32)
                if n_gdn:
                    load_w(wgab, "wgab", D["w_in"], 0, 16, C_GA, 16, gn[:, 0])
                    for t in range(17):
                        n = 128 if t < 16 else 64
                        for k in range(16):
                            mm(PS[0][0:n, 0:16], XT[:, k, t * 128:t * 128 + n], wgab[:, k, :], k == 0, k == 15, ["wgab"], ["ps0"])
                        cp("dve", gtm[0:n, :], PS[0][0:n, 0:16], ["ps0"], ["gtm"])
                        tt("dve", gtm[0:n, 0:8], gtm[0:n, 0:8], dtb[0:n, :], ALU.add, ["gtm", "dtb"], ["gtm"])
                        act(gtm[0:n, 0:8], gtm[0:n, 0:8], AF.Exp, ["gtm"], ["gtm"])
                        act(gtm[0:n, 0:8], gtm[0:n, 0:8], AF.Ln, ["gtm"], ["gtm"], bias=1.0)
                        tt("dve", GD[0:n, t, :], gtm[0:n, 0:8], nea[0:n, :], ALU.mult, ["gtm", "nea"], [("GD", t)])
                        act(gtm[0:n, 8:16], gtm[0:n, 8:16], AF.Exp, ["gtm"], ["gtm"], scale=-1.0)
                        ts("dve", gtm[0:n, 8:16], gtm[0:n, 8:16], 1.0, None, ALU.add, None, ["gtm"], ["gtm"])
                        P.op("dve", (lambda o, i: (lambda e: e.reciprocal(out=o, in_=i)))(BT[0:n, t, :], gtm[0:n, 8:16]), ["gtm"], [("BT", t)])
                        ts("dve", NBm[0:n, t, :], BT[0:n, t, :], -1.0, None, ALU.mult, None, [("BT", t)], [("NBm", t)])
                    dma("sp", D["scr_gab"][:, 0:8], GD[0:64, 16, :], [("GD", 16)], [("dram", "scr")])
                    dma("sp", D["scr_gab"][:, 8:16], BT[0:64, 16, :], [("BT", 16)], [("dram", "scr2")])
                    dma("sp", GDS[0:4, :, 0:16], D["scr_gab"].rearrange("(i t) c -> t i c", t=4), [("dram", "scr"), ("dram", "scr2")], ["GDS"])
                    ts("dve", GDS[0:4, :, 16:24], GDS[0:4, :, 8:16], -1.0, None, ALU.mult, None, ["GDS"], ["GDS"])
                GDK = [("GD", t) for t in range(17)]
                BTK = [("BT", t) for t in range(17)]
                NBK = [("NBm", t) for t in range(17)]
                G3 = T_("G3", [128, 512], F32)
                CB = T_("CB", [128, 512], F32)
                ECB = T_("ECB", [128, 512], F32)
                EI = T_("EI", [128, 512], BF16)
                ES = T_("ES", [128, 512], BF16)
                CC = T_("CC", [128, 16, 1], F32)
                DEX = T_("DEX", [128, 16, 1], F32)
                GCc = T_("GCc", [128, 16, 1], F32)
                BCc = T_("BCc", [128, 16, 1], F32)
                NBc = T_("NBc", [128, 16, 1], F32)
                MA = [T_("MA%d" % i, [128, 512], F32) for i in range(2)]
                MTt = [T_("MT%d" % i, [128, 512], F32) for i in range(2)]
                X32 = T_("X32", [128, 512], F32)
                XB = T_("XB", [128, 512], BF16)
                QKD = T_("QKD", [128, 512], BF16)
                KCT = T_("KCT", [128, 512], BF16)
                QDT = T_("QDT", [128, 512], BF16)
                KTM = T_("KTM", [128, 8, 128], BF16)
                VTM = T_("VTM", [128, 8, 128], BF16)
                KEND = T_("KEND", [128, 8, 128], BF16)
                RT = T_("RT", [128, 128], BF16)
                VN = T_("VN", [128, 128], BF16)
                S32 = T_("S32", [128, 128], F32)
                Sb_ = T_("Sb_", [128, 128], BF16)
                S32A = T_("S32A", [128, 16, 128], F32)
                RAW = [T_("RAW%d" % i, [128, 515], F32) for i in range(3)]
                RAWS = [T_("RAWS%d" % i, [128, 16, 7], F32) for i in range(3)]
                ACC = T_("ACC", [128, 512], F32)
                CQ = T_("CQ", [128, 512], F32)
                QTg = T_("QTg", [128, 512], BF16)
                KTg = T_("KTg", [128, 512], BF16)
                CVb = T_("CVb", [128, 512], BF16)
                ZS = T_("ZS", [128, 512], BF16)
                gco = T_("gco", [128, 128], F32)
                t48 = T_("t48", [128, 48], F32)

                def gdn_group(C, G, own, h, gsrc, bsrc, nbsrc, gkeys, mi, ms, S_of, ctc0, c0=0, chain=True):
                    T = C * G
                    nsq = {128: 6, 4: 1}[C]

                    def v3(t):
                        return t[0:C, 0:T].rearrange("p (g c) -> p g c", g=G)

                    def p3(bank):
                        return PS[bank][0:C, 0:T].rearrange("p (g c) -> p g c", g=G)

                    cp("pool", GCc[0:C, 0:G, :], gsrc, gkeys, ["GCc"])
                    cp("pool", BCc[0:C, 0:G, :], bsrc, gkeys, ["BCc"])
                    cp("pool", NBc[0:C, 0:G, :], nbsrc, gkeys, ["NBc"])
                    tt("pool", v3(G3), GCc[0:C, 0:G, :].broadcast_to([C, G, C]), tri_le3[0:C, :, 0:C].broadcast_to([C, G, C]), ALU.mult,
                       ["GCc", "tri_le3"], ["G3"])
                    mm(PS[2][:, 0:T], ones_f[0:C, :], G3[0:C, 0:T], True, True, ["G3", "ones_f"], ["ps2"])
                    mm(PS[7][0:C, 0:G], tri_le[0:C, 0:C], GCc[0:C, 0:G, 0], True, True, ["GCc", "tri_le"], ["ps7"])
                    cp("dve", CC[0:C, 0:G, 0], PS[7][0:C, 0:G], ["ps7"], ["CC"])
                    cp("dve", CB[:, 0:T], PS[2][:, 0:T], ["ps2"], ["CB"])
                    act(ECB[:, 0:T], CB[:, 0:T], AF.Exp, ["CB"], ["ECB"])
                    cb3 = CB[0:C, 0:T].rearrange("p (g c) -> p g c", g=G)
                    tt("dve", DEX[0:C, 0:G, 0], cb3[:, :, C - 1], CC[0:C, 0:G, 0], ALU.subtract, ["CB", "CC"], ["DEX"])
                    act(DEX[0:C, 0:G, :], DEX[0:C, 0:G, :], AF.Exp, ["DEX"], ["DEX"])
                    tt("dve", v3(CB), v3(CB), CC[0:C, 0:G, :].broadcast_to([C, G, C]), ALU.subtract, ["CB", "CC"], ["CB"])
                    ts("dve", v3(CB), v3(CB), 0.0, None, ALU.min, None, ["CB"], ["CB"])
                    act(v3(CB), v3(CB), AF.Exp, ["CB"], ["CB"])
                    tt("pool", v3(EI), v3(CB), mi, ALU.mult, ["CB", "MI", "MI4"], ["EI"])
                    tt("pool", v3(ES), v3(CB), ms, ALU.mult, ["CB", "MS", "MS4"], ["ES"])
                    for g in range(G):
                        mm(PS[3][0:C, g * C:(g + 1) * C], KTg[:, c0 + g * C:c0 + (g + 1) * C], KTg[:, c0 + g * C:c0 + (g + 1) * C], True, True, ["KTg"], ["ps3"])
                    tt("dve", v3(G3), p3(3), v3(ES), ALU.mult, ["ps3", "ES"], ["G3"])
                    tt("dve", v3(MTt[0]).bitcast(F32R), v3(G3), NBc[0:C, 0:G, :].broadcast_to([C, G, C]), ALU.mult,
                       ["G3", "NBc"], ["MT0"])
                    for g in range(G):
                        mm(PS[4][0:C, g * C:(g + 1) * C], KTg[:, c0 + g * C:c0 + (g + 1) * C], QTg[:, c0 + g * C:c0 + (g + 1) * C], True, True, ["KTg", "QTg"], ["ps4"])
                    tt("dve", v3(QKD), p3(4), v3(EI), ALU.mult, ["ps4", "EI"], ["QKD"])
                    tt("pool", KCT[:, 0:T], KTg[:, c0:c0 + T], ECB[:, 0:T], ALU.mult, ["KTg", "ECB"], ["KCT"])
                    tt("pool", QDT[:, 0:T], QTg[:, c0:c0 + T], ECB[:, 0:T], ALU.mult, ["QTg", "ECB"], ["QDT"])
                    for g0 in range(0, G, 8):
                        ng = min(8, G - g0)
                        for g in range(g0, g0 + ng):
                            tr(psb(6)[0:C, (g - g0) * 128:(g - g0 + 1) * 128], KTg[:, c0 + g * C:c0 + (g + 1) * C], ident_b[:, :], ["KTg", "ident_b"], ["ps6"])
                        cp("act", KTM[0:C, g0:g0 + ng, :], psb(6)[0:C, 0:ng * 128].rearrange("p (g d) -> p g d", g=ng), ["ps6"], ["KTM"])
                        for g in range(g0, g0 + ng):
                            tr(psb(7)[0:C, (g - g0) * 128:(g - g0 + 1) * 128], CVb[:, c0 + g * C:c0 + (g + 1) * C], ident_b[:, :], ["CVb", "ident_b"], ["ps7"])
                        cp("dve", VTM[0:C, g0:g0 + ng, :], psb(7)[0:C, 0:ng * 128].rearrange("p (g d) -> p g d", g=ng), ["ps7"], ["VTM"])
                    tt("pool", KEND[0:C, 0:G, :], KTM[0:C, 0:G, :], DEX[0:C, 0:G, :].broadcast_to([C, G, 128]), ALU.mult, ["KTM", "DEX"], ["KEND"])
                    RD = F32R if C == 128 else F32

                    def rv(ap):
                        return ap.bitcast(RD) if RD is F32R else ap

                    def wv(ap):
                        return ap.bitcast(F32R)

                    for g in range(G):
                        tr(PS[3][0:C, g * C:(g + 1) * C], MTt[0][0:C, g * C:(g + 1) * C], ident_f[0:C, 0:C], ["MT0", "ident_f"], ["ps3"])
                    cp("act", wv(MA[0][0:C, 0:T]), PS[3][0:C, 0:T], ["ps3"], ["MA0"])
                    tt("dve", wv(v3(X32)), v3(MTt[0]), identf3[0:C, :, 0:C].broadcast_to([C, G, C]), ALU.add, ["MT0", "identf3"], ["X32"])
                    cur = 0
                    for r in range(nsq):
                        M, Mk = MA[cur], "MA%d" % cur
                        Mt, Mtk = MTt[cur], "MT%d" % cur
                        Mn, Mnk = MA[1 - cur], "MA%d" % (1 - cur)
                        Mtn, Mtnk = MTt[1 - cur], "MT%d" % (1 - cur)
                        for g in range(G):
                            sl = slice(g * C, (g + 1) * C)
                            mm(PS[3][0:C, sl], rv(Mt[0:C, sl]), rv(M[0:C, sl]), True, True, [Mk, Mtk], ["ps3"])
                        if r < nsq - 1:
                            for g in range(G):
                                sl = slice(g * C, (g + 1) * C)
                                mm(PS[4][0:C, sl], rv(M[0:C, sl]), rv(Mt[0:C, sl]), True, True, [Mk, Mtk], ["ps4"])
                        cp("act", wv(Mn[0:C, 0:T]), PS[3][0:C, 0:T], ["ps3"], [Mnk])
                        if r < nsq - 1:
                            cp("dve", wv(Mtn[0:C, 0:T]), PS[4][0:C, 0:T], ["ps4"], [Mtnk])
                        for g in range(G):
                            sl = slice(g * C, (g + 1) * C)
                            mm(PS[5][0:C, sl], rv(Mn[0:C, sl]), rv(X32[0:C, sl]), True, True, [Mnk, "X32"], ["ps5"])
                        tt("dve", wv(X32[0:C, 0:T]), X32[0:C, 0:T], PS[5][0:C, 0:T], ALU.add, ["X32", "ps5"], ["X32"])
                        cur = 1 - cur
                    cp("act", XB[0:C, 0:T], X32[0:C, 0:T], ["X32"], ["XB"])
                    for g in range(G):
                        S32g, Sbg, skey, sbkey = S_of(g)
                        sl = slice(g * C, (g + 1) * C)
                        if not chain:
                            cp("act", Sbg, S32g, [skey], [sbkey])
                        kb = pa_next()
                        mm(PS[kb][0:C, 0:128], KCT[:, sl], Sbg, True, True, ["KCT", sbkey], [PSK[kb]])
                        tt("dve", RT[0:C, :], VTM[0:C, g, :], PS[kb][0:C, 0:128], ALU.subtract, ["VTM", PSK[kb]], ["RT"])
                        kb2 = pa_next()
                        mm(PS[kb2][0:C, 0:128], XB[0:C, sl], RT[0:C, :], True, True, ["XB", "RT"], [PSK[kb2]])
                        ts("dve", VN[0:C, :], PS[kb2][0:C, 0:128], BCc[0:C, g, :], None, ALU.mult, None, [PSK[kb2], "BCc"], ["VN"])
                        if own:
                            mm(PS[6][:, sl], Sbg, QDT[:, sl], True, False, [sbkey, "QDT"], ["ps6"])
                            mm(PS[6][:, sl], VN[0:C, :], QKD[0:C, sl], False, True, ["VN", "QKD"], ["ps6"])
                        mm(PS[7][:, 0:128], KEND[0:C, g, :], VN[0:C, :], True, True, ["KEND", "VN"], ["ps7"])
                        stt(S32g, S32g, ECB[:, g * C + C - 1:g * C + C], PS[7][:, 0:128], ALU.mult, ALU.add, [skey, "ECB", "ps7"], [skey])
                        if chain:
                            cp("act", Sbg, S32g, [skey], [sbkey])
                    if own:
                        fm_norm("ps6", PS[6][:, 0:T], T, hg[:, 4:5], ACC[:, 0:T], ["ACC"], st2, ["hg"])
                        tt("pool", CT[:, 8 + h, ctc0:ctc0 + T], ACC[:, 0:T], ZS[:, c0:c0 + T], ALU.mult, ["ACC", "ZS"], [("CT", 8 + h, ctc0)])

                def conv_silu(flat_in, L, cidx, out_ap_fn, key_in):
                    ts("dve", ACC[:, 0:L], flat_in[:, 3:3 + L], CW[:, cidx, 3:4], None, ALU.mult, None, [key_in, "CW"], ["ACC"])
                    for j in (2, 1, 0):
                        stt(ACC[:, 0:L], flat_in[:, j:j + L], CW[:, cidx, j:j + 1], ACC[:, 0:L], ALU.mult, ALU.add, [key_in, "CW", "ACC"], ["ACC"])

                for h in range(n_gdn):
                    ws = []
                    for c0_ in (C_GQ, C_GK, C_GV, C_GZ):
                        w_, wk_ = wb_next()
                        load_w(w_, wk_, D["w_in"], 0, 16, c0_ + h * 128, 128, gn[:, 0])
                        ws.append((w_, wk_))
                    memset("pool", S32[:], 0.0, ["S"])
                    memset("pool", Sb_[:], 0.0, [("S", "b")])
                    for i in range(3):
                        memset("pool", RAW[i][:, 0:3], 0.0, ["RAW%d" % i])
                    for grp in range(4):
                        own = grp >= 2
                        xc0 = grp * 512
                        if grp == 2:
                            for i in range(3):
                                ts("dve", RAW[i][:, 0:3], RAW[i][:, 0:3], flg[:, 0:1], None, ALU.mult, None, ["RAW%d" % i, "flg"], ["RAW%d" % i])
                            ts("dve", S32[:], S32[:], flg[:, 0:1], None, ALU.mult, None, ["S", "flg"], ["S"])
                            cp("act", Sb_[:], S32[:], ["S"], [("S", "b")])
                        for comp in range(3):
                            w_, wk_ = ws[comp]
                            rk = "RAW%d" % comp
                            bk = pa_next()
                            proj_fm(bk, w_, wk_, xc0, 512)
                            cp("act", RAW[comp][:, 3:515], PS[bk][:, 0:512], [PSK[bk]], [rk])
                            cidx = comp * 8 + h
                            if grp == 3:
                                tr(PS[7][0:3, 0:128], RAW[comp][:, 512:515], ident_f[:], [rk, "ident_f"], ["ps7"])
                                cp("dve", gco[0:3, :], PS[7][0:3, 0:128], ["ps7"], ["gco"])
                                dma("sp", D["gcp"][:, cidx * 128:(cidx + 1) * 128], gco[0:3, :], ["gco"], [])
                            conv_silu(RAW[comp], 512, cidx, None, rk)
                            cp("pool", RAW[comp][:, 0:3], RAW[comp][:, 512:515], [rk], [rk])
                            if comp == 0:
                                act(CQ[:, :], ACC[:, :], AF.Silu, ["ACC"], ["CQ"])
                                fm_norm("CQ", CQ[:, :], 512, 128.0 ** -0.5, QTg[:, :], ["QTg"], st2, mean=False)
                            elif comp == 1:
                                act(CQ[:, :], ACC[:, :], AF.Silu, ["ACC"], ["CQ"])
                                fm_norm("CQ", CQ[:, :], 512, 1.0, KTg[:, :], ["KTg"], st2, mean=False)
                            else:
                                act(CVb[:, :], ACC[:, :], AF.Silu, ["ACC"], ["CVb"])
                        if own:
                            bk = pa_next()
                            proj_fm(bk, ws[3][0], ws[3][1], xc0, 512)
                            act(ZS[:, :], PS[bk][:, 0:512], AF.Silu, [PSK[bk]], ["ZS"])
                        t0 = grp * 4
                        gdn_group(128, 4, own, h, GD[:, t0:t0 + 4, h:h + 1], BT[:, t0:t0 + 4, h:h + 1], NBm[:, t0:t0 + 4, h:h + 1],
                                  GDK + BTK + NBK, MI[:, :, :], MS[:, :, :], lambda g: (S32[:], Sb_[:], "S", ("S", "b")), (grp - 2) * 512)
                    dma("sp", D["gsp"][h], S32[:], ["S"], [])
                    dma("sp", S32A[:], D["sg"][:, h].rearrange("i d e -> d i e"), [], [("SA", g) for g in range(16)])
                    for comp in range(3):
                        w_, wk_ = ws[comp]
                        rk = "RAWS%d" % comp
                        cidx = comp * 8 + h
                        bk = pa_next()
                        proj_fm(bk, w_, wk_, SMP0, 64)
                        cp("act", RAWS[comp][:, :, 3:7], PS[bk][:, 0:64].rearrange("p (i t) -> p i t", t=4), [PSK[bk]], [rk])
                        cp("dve", RAWS[comp][:, :, 0:3], SCT[:, cidx, :].rearrange("p (i r) -> p i r", r=3), ["SCT"], [rk])
                        cp("pool", t48[:, :].rearrange("p (i r) -> p i r", r=3), RAWS[comp][:, :, 4:7], [rk], ["t48"])
                        tr(PS[7][0:48, 0:128], t48[:, :], ident_f[:], ["t48", "ident_f"], ["ps7"])
                        cp("dve", gco[0:48, :], PS[7][0:48, 0:128], ["ps7"], ["gco"])
                        dma("sp", D["gcs"][:, cidx * 128:(cidx + 1) * 128], gco[0:48, :], ["gco"], [])
                        flat = RAWS[comp][:, :, :].rearrange("p i s -> p (i s)")
                        conv_silu(flat, 109, cidx, None, rk)
                        accv = ACC[:, 0:112].rearrange("p (i s) -> p i s", s=7)[:, :, 0:4]
                        if comp == 0:
                            act(CQ[:, 0:64].rearrange("p (i t) -> p i t", t=4), accv, AF.Silu, ["ACC"], ["CQ"])
                            fm_norm("CQ", CQ[:, 0:64], 64, 128.0 ** -0.5, QTg[:, 0:64], ["QTg"], st2, mean=False)
                        elif comp == 1:
                            act(CQ[:, 0:64].rearrange("p (i t) -> p i t", t=4), accv, AF.Silu, ["ACC"], ["CQ"])
                            fm_norm("CQ", CQ[:, 0:64], 64, 1.0, KTg[:, 0:64], ["KTg"], st2, mean=False)
                        else:
                            act(CVb[:, 0:64].rearrange("p (i t) -> p i t", t=4), accv, AF.Silu, ["ACC"], ["CVb"])
                    bk = pa_next()
                    proj_fm(bk, ws[3][0], ws[3][1], SMP0, 64)
                    act(ZS[:, 0:64], PS[bk][:, 0:64], AF.Silu, [PSK[bk]], ["ZS"])
                    for i0 in (0, 8):
                        gdn_group(4, 8, True, h, GDS[0:4, i0:i0 + 8, h:h + 1], GDS[0:4, i0:i0 + 8, 8 + h:9 + h],
                                  GDS[0:4, i0:i0 + 8, 16 + h:17 + h], ["GDS"], MI4[0:4, 0:8, :], MS4[0:4, 0:8, :],
                                  (lambda i0_: (lambda g: (S32A[:, i0_ + g, :], Sb_[:], ("SA", i0_ + g), ("S", "b"))))(i0), 1024 + i0 * 4,
                                  c0=i0 * 4, chain=False)
                    dma("sp", D["gss"][:, h].rearrange("i d e -> d i e"), S32A[:], [("SA", g) for g in range(16)], [])
                P.emit()

        with contextlib.ExitStack() as ph2c:
            n_ssb = 16 if stage >= 4 else 0

            def U_(name, shape, dt):
                return sb(name, shape, dt, ph2c)

            ptb = U_("ptb", [128, 256], I32)
            IDX = U_("IDX", [128, 256], I32)
            pid = U_("pid", [128, 1], F32)
            BH512 = U_("BH512", [128, 16, 8, 4], F32)
            MN = U_("MN", [128, 8, 4], F32)
            KPb = [U_("KPb%d" % i, [128, 1024], BF16) for i in range(2)]
            KTall = U_("KTall", [128, 16, 8, 128], BF16)
            VBall = U_("VBall", [128, 16, 1024], BF16)
            VNb = U_("VNb", [128, 1024], BF16)
            ZP = U_("ZP", [128, 512], F32)
            SPt = U_("SPt", [128, 512], F32)
            TOa = U_("TOa", [128, 17, 32], F32)
            TOb = U_("TOb", [128, 17, 32], F32)
            Wsb = U_("Wsb", [128, 512], BF16)
            OACC = U_("OACC", [128, 32], F32)
            if n_ssb:
                dma("sp", ptb[:], D["pt"][0:1, :].partition_broadcast(128), [], ["ptb"])
                P.op("pool", lambda e: e.iota(pid[:], pattern=[[0, 1]], base=0, channel_multiplier=1, allow_small_or_imprecise_dtypes=True), [], ["pid"])
                ptf = SPt[:, 0:256]
                cp("dve", ptf, ptb[:], ["ptb"], ["SPt"])
                ts("dve", ptf, ptf, 128.0, pid[:, 0:1], ALU.mult, ALU.add, ["SPt", "pid"], ["SPt"])
                cp("dve", IDX[:], ptf, ["SPt"], ["IDX"])
                for j in range(16):
                    cp("dve", BH512[:, j, :, :], sbb[:, :].rearrange("p (h o) -> p h o", o=1).broadcast_to([128, 8, 4]), ["sbb"], ["BH512"])
                memset("pool", MN[:], 1.0, ["MN"])
                asel(MN[:], MN[:], [[0, 8], [1, 4]], ALU.is_ge, -1, -1, ["MN"], ["MN"])
                memset("pool", TOa[:], 0.0, ["TOa"])
                memset("pool", TOb[:], 0.0, ["TOb"])
            BHf = BH512[:].rearrange("p j h t -> p (j h t)")
            MN2 = MN[:].rearrange("p h t -> p (h t)")

            for i in range(n_ssb):
                for j in range(16):
                    b = j % 2
                    col = i * 16 + j
                    P.op("pool", (lambda o, ix: (lambda e: e.indirect_dma_start(out=o, out_offset=None, in_=D["ck"],
                         in_offset=bass.IndirectOffsetOnAxis(ap=ix, axis=0))))(KPb[b][:, :], IDX[:, col:col + 1]), ["IDX"], ["KPb%d" % b], dma=True)
                    P.op("pool", (lambda o, ix: (lambda e: e.indirect_dma_start(out=o, out_offset=None, in_=D["cv"],
                         in_offset=bass.IndirectOffsetOnAxis(ap=ix, axis=0))))(VBall[:, j, :], IDX[:, col:col + 1]), ["IDX"], [("VB", j)], dma=True)
                    bank = j % 2
                    pv = psb(bank)
                    for h in range(8):
                        tr(pv[:, h * 128:(h + 1) * 128], KPb[b][:, h * 128:(h + 1) * 128], ident_b[:], ["KPb%d" % b, "ident_b"], [PSK[bank]])
                    cp("act" if j % 2 == 0 else "dve", KTall[:, j, :, :], pv[:, 0:1024].rearrange("p (h s) -> p h s", h=8), [PSK[bank]], [("KT", j)])
                dma("pool", VNb[0:4, :], D["svs"][4 * i:4 * i + 4, :], [], ["VNb"])
                for h in range(8):
                    mm(PS[2][0:4, h * 4:(h + 1) * 4], KTS[:, h, 4 * i:4 * i + 4], QTS[:, h, 4 * i:4 * i + 4], True, True, [], ["ps2"])
                stt(ZP[0:4, 0:32], PS[2][0:4, 0:32], SCALE, BHf[0:4, 0:32], ALU.mult, ALU.add, ["ps2", "BH512"], ["ZP"])
                act(SPt[0:4, 0:32], ZP[0:4, 0:32], AF.Exp, ["ZP"], ["SPt"])
                act(SPt[0:4, 0:32], SPt[0:4, 0:32], AF.Ln, ["SPt"], ["SPt"], bias=1.0)
                tt("dve", SPt[0:4, 0:32], SPt[0:4, 0:32], MN2[0:4, :], ALU.mult, ["SPt", "MN"], ["SPt"])
                mm(PS[3][0:4, 0:32], tri_gt[0:4, 0:4], SPt[0:4, 0:32], True, True, ["SPt", "tri_gt"], ["ps3"])
                mm(PS[4][:, 0:32], ones_f[0:4, :], SPt[0:4, 0:32], True, True, ["SPt", "ones_f"], ["ps4"])
                tt("dve", ZP[0:4, 0:32], ZP[0:4, 0:32], SPt[0:4, 0:32], ALU.subtract, ["ZP", "SPt"], ["ZP"])
                tt("dve", ZP[0:4, 0:32], ZP[0:4, 0:32], PS[3][0:4, 0:32], ALU.subtract, ["ZP", "ps3"], ["ZP"])
                cp("dve", TOa[:, 16, :], PS[4][:, 0:32], ["ps4"], ["TOa"])
                act(Wsb[0:4, 0:32], ZP[0:4, 0:32], AF.Exp, ["ZP"], ["Wsb"])
                tt("dve", Wsb[0:4, 0:32], Wsb[0:4, 0:32], MN2[0:4, :], ALU.mult, ["Wsb", "MN"], ["Wsb"])
                for h in range(8):
                    mm(PS[5][:, h * 4:(h + 1) * 4], VNb[0:4, h * 128:(h + 1) * 128], Wsb[0:4, h * 4:(h + 1) * 4], True, True, ["Wsb", "VNb"], ["ps5"])
                cp("dve", OACC[:], PS[5][:, 0:32], ["ps5"], ["OACC"])
                KTK_ = [("KT", j) for j in range(16)]
                for j in range(16):
                    for h in range(8):
                        c = (j * 8 + h) * 4
                        mm(PS[2][:, c:c + 4], KTall[:, j, h, :], QTS[:, h, 4 * i:4 * i + 4], True, True, KTK_, ["ps2"])
                stt(ZP[:, :], PS[2][:, :], SCALE, BHf[:, :], ALU.mult, ALU.add, ["ps2", "BH512"], ["ZP"])
                act(SPt[:, :], ZP[:, :], AF.Exp, ["ZP"], ["SPt"])
                act(SPt[:, :], SPt[:, :], AF.Ln, ["SPt"], ["SPt"], bias=1.0)
                mm(PS[3][:, :], tri_gt[:], SPt[:, :], True, True, ["SPt", "tri_gt"], ["ps3"])
                mm(PS[4][:, :], ones_f[:], SPt[:, :], True, True, ["SPt", "ones_f"], ["ps4"])
                tt("dve", ZP[:, :], ZP[:, :], SPt[:, :], ALU.subtract, ["ZP", "SPt"], ["ZP"])
                tt("dve", ZP[:, :], ZP[:, :], PS[3][:, :], ALU.subtract, ["ZP", "ps3"], ["ZP"])
                cp("dve", TOa[:, 0:16, :], PS[4][:, :].rearrange("p (j c) -> p j c", j=16), ["ps4"], ["TOa"])
                src_, sk_, dst_, dk_ = TOa, "TOa", TOb, "TOb"
                for sh in (1, 2, 4, 8, 16):
                    n_ = 17 - sh
                    tt("pool", dst_[:, 0:n_, :], src_[:, 0:n_, :], src_[:, sh:17, :], ALU.add, [sk_], [dk_])
                    cp("pool", dst_[:, n_:17, :], src_[:, n_:17, :], [sk_], [dk_])
                    src_, sk_, dst_, dk_ = dst_, dk_, src_, sk_
                tt("dve", ZP[:, :].rearrange("p (j c) -> p j c", j=16), ZP[:, :].rearrange("p (j c) -> p j c", j=16), src_[:, 1:17, :],
                   ALU.subtract, ["ZP", sk_], ["ZP"])
                act(Wsb[:, :], ZP[:, :], AF.Exp, ["ZP"], ["Wsb"])
                VBK_ = [("VB", j) for j in range(16)]
                for h in range(8):
                    for j in range(16):
                        c = (j * 8 + h) * 4
                        mm(PS[5][:, h * 4:(h + 1) * 4], VBall[:, j, h * 128:(h + 1) * 128], Wsb[:, c:c + 4], j == 0, j == 15, ["Wsb"] + VBK_, ["ps5"])
                tt("dve", OACC[:], OACC[:], PS[5][:, 0:32], ALU.add, ["OACC", "ps5"], ["OACC"])
                cp("act", CT[:, 0:8, 1024 + 4 * i:1024 + 4 * i + 4], OACC[:].rearrange("p (h t) -> p h t", t=4), ["OACC"], [("CTs", i)])
            P.emit()

        if stage >= 5:
            X1 = BIG.bitcast(F32).reshape([128, 8, 2112])
            with contextlib.ExitStack() as ph3:
                X1s = sb("X1s", [128, 2048], F32, ph3)

                def x1_tile(t):
                    if t < 8:
                        return X1[:, t, 0:2048], 128, ("X1", t)
                    return X1s[0:64, :], 64, ("X1", 8)

                def ct_cols(t):
                    return (t * 128, 128) if t < 8 else (1024, 64)

                with contextlib.ExitStack() as pa3:
                    WO = [sb("WO%d" % i, [128, 16, 512], BF16, pa3) for i in range(2)]
                    xres = [sb("xres%d" % i, [128, 512], F32, pa3) for i in range(2)]
                    cnt = 0
                    for n in range(4):
                        wo, wok = WO[n % 2], "WO%d" % (n % 2)
                        load_w(wo, wok, D["w_out"], 0, 16, n * 512, 512, None)
                        for t in range(9):
                            xa, nn, xk = x1_tile(t)
                            c0, _ = ct_cols(t)
                            xr, xrk = xres[cnt % 2], "xres%d" % (cnt % 2)
                            bk = cnt % 2
                            cnt += 1
                            src = D["xp"][1024 + t * 128:1024 + (t + 1) * 128, n * 512:(n + 1) * 512] if t < 8 else D["xs"][:, n * 512:(n + 1) * 512]
                            dma("sp", xr[0:nn, :], src, [], [xrk])
                            for k in range(16):
                                mm(PS[bk][0:nn, :], CT[:, k, c0:c0 + nn], wo[:, k, :], k == 0, k == 15, [wok], [PSK[bk]])
                            tt("dve", xa[:, n * 512:(n + 1) * 512], PS[bk][0:nn, :], xr[0:nn, :], ALU.add, [PSK[bk], xrk], [xk + (n,)])
                    P.emit()

                def norm_to_ct(st):
                    for t in range(9):
                        xa, nn, xk = x1_tile(t)
                        c0, _ = ct_cols(t)
                        junk, ss, xb = st["junk"], st["ss"], st["xb"]
                        act(xb[0:nn, :], xa, AF.Square, [xk], ["xb", "ss"], accum=ss[0:nn, :])
                        act(ss[0:nn, :], ss[0:nn, :], AF.Ln, ["ss"], ["ss"], bias=EPS, scale=1.0 / 2048)
                        act(ss[0:nn, :], ss[0:nn, :], AF.Exp, ["ss"], ["ss"], scale=-0.5)
                        ts("dve", xb[0:nn, :], xa, ss[0:nn, 0:1], None, ALU.mult, None, [xk, "ss"], ["xb"])
                        for half in range(2):
                            bank = 6 + half
                            pv = psb(bank)
                            for k in range(8):
                                kk = half * 8 + k
                                tr(pv[:, k * 128:k * 128 + nn], xb[0:nn, kk * 128:(kk + 1) * 128], ident_b[0:nn, 0:nn], ["xb", "ident_b"], [PSK[bank]])
                            srcv = pv[:, 0:1024].rearrange("p (a b) -> p a b", a=8)[:, :, 0:nn]
                            cp("act" if half == 0 else "dve", CT[:, half * 8:half * 8 + 8, c0:c0 + nn], srcv, [PSK[bank]], [("CT", t, half)])

                CTK = [("CT", t, hf_) for t in range(9) for hf_ in range(2)]
                with contextlib.ExitStack() as pb3:
                    st3 = {"junk": None, "ss": sb("ss3", [128, 1], F32, pb3), "xb": sb("xb3", [128, 2048], BF16, pb3),
                           "sq": sb("sq3", [128, 512], BF16, pb3), "rr": sb("rr3", [128, 512], F32, pb3)}
                    XQ = sb("XQ", [128, 4, NTO], BF16, pb3)
                    XO = sb("XO", [128, 4, NTO], BF16, pb3)
                    wxq = [sb("wxq%d" % i, [128, 16, 128], BF16, pb3) for i in range(2)]
                    WXO = sb("WXO", [128, 4, 2048], BF16, pb3)
                    EM = [sb("EM%d" % i, [128, 512], BF16, pb3) for i in range(2)]
                    rden = sb("rden", [128, 512], F32, pb3)
                    CK = [sb("CK%d" % i, [128, 2, 512], F32, pb3) for i in range(2)]
                    CV = [sb("CVm%d" % i, [128, 2, 512], F32, pb3) for i in range(2)]
                    KTm = sb("KTm", [128, 4, 256], BF16, pb3)
                    Vbm = sb("Vbm", [128, 2, 512], BF16, pb3)
                    Es = sb("Es", [128, 32], BF16, pb3)
                    norm_to_ct(st3)
                    for n_ in range(4):
                        load_w(WXO, "WXO", D["w_xo"], 0, 4, n_ * 512, 512, None, dcol0=n_ * 512)
                    for h in range(4):
                        w_, wk_ = wxq[h % 2], "wxq%d" % (h % 2)
                        load_w(w_, wk_, D["w_xq"], 0, 16, h * 128, 128, gn[:, 1])
                        for c0, T in ((0, 512), (512, 512), (1024, 64)):
                            bk = pa_next()
                            for k in range(16):
                                mm(PS[bk][:, 0:T], w_[:, k, :], CT[:, k, c0:c0 + T], k == 0, k == 15, [wk_] + CTK, [PSK[bk]])
                            fm_norm(PSK[bk], PS[bk][:, 0:T], T, hg[:, 2:3], XQ[:, h, c0:c0 + T], [("XQ", h, c0)], st3, ["hg"])
                        for g in range(2):
                            c0 = g * 512
                            for mt in range(2):
                                mm(PS[2 + mt][:, :], MKT[:, h, mt * 128:(mt + 1) * 128], XQ[:, h, c0:c0 + 512], True, True, [("XQ", h, c0)], [PSK[2 + mt]])
                                act(EM[mt][:, :], PS[2 + mt][:, :], AF.Exp, [PSK[2 + mt]], ["EM%d" % mt], scale=SCALE)
                            for mt in range(2):
                                mm(PS[4][:, :], ones_b[:], EM[mt][:, :], mt == 0, mt == 1, ["EM%d" % mt, "ones_b"], ["ps4"])
                            for mt in range(2):
                                mm(PS[5][:, :], MV[:, mt, h * 128:(h + 1) * 128], EM[mt][:, :], mt == 0, mt == 1, ["EM%d" % mt], ["ps5"])
                            P.op("dve", (lambda o, i_: (lambda e: e.reciprocal(out=o, in_=i_)))(rden[:, :], PS[4][:, :]), ["ps4"], ["rden"])
                            tt("dve", XO[:, h, c0:c0 + 512], PS[5][:, :], rden[:, :], ALU.mult, ["ps5", "rden"], [("XO", h, c0)])
                    XQK = [("XQ", h, 1024) for h in range(4)]
                    for i in range(16):
                        ck_, ckk = CK[i % 2], "CK%d" % (i % 2)
                        cv_, cvk = CV[i % 2], "CVm%d" % (i % 2)
                        dma("sp", ck_[:], D["cmk"][i].rearrange("(t p) c -> p t c", p=128), [], [ckk])
                        dma("sp", cv_[:], D["cmv"][i].rearrange("(t p) c -> p t c", p=128), [], [cvk])
                        for mt in range(2):
                            for h in range(4):
                                tr(PS[mt][:, h * 128:(h + 1) * 128], ck_[:, mt, h * 128:(h + 1) * 128], ident_f[:], [ckk, "ident_f"], [PSK[mt]])
                            cp("act" if mt == 0 else "dve", KTm[:, :, mt * 128:(mt + 1) * 128], PS[mt][:, :].rearrange("p (h m) -> p h m", h=4),
                               [PSK[mt]], [("KTm", mt)])
                        cp("pool", Vbm[:], cv_[:], [cvk], ["Vbm"])
                        for mt in range(2):
                            for h in range(4):
                                mm(PS[2][:, mt * 16 + h * 4:mt * 16 + h * 4 + 4], KTm[:, h, mt * 128:(mt + 1) * 128], XQ[:, h, 1024 + 4 * i:1024 + 4 * i + 4],
                                   True, True, [("KTm", 0), ("KTm", 1)] + XQK, ["ps2"])
                        act(Es[:, :], PS[2][:, 0:32], AF.Exp, ["ps2"], ["Es"], scale=SCALE)
                        for mt in range(2):
                            mm(PS[4][:, 0:16], ones_b[:], Es[:, mt * 16:(mt + 1) * 16], mt == 0, mt == 1, ["Es", "ones_b"], ["ps4"])
                        for h in range(4):
                            for mt in range(2):
                                mm(PS[5][:, h * 4:(h + 1) * 4], Vbm[:, mt, h * 128:(h + 1) * 128], Es[:, mt * 16 + h * 4:mt * 16 + h * 4 + 4],
                                   mt == 0, mt == 1, ["Es", "Vbm"], ["ps5"])
                        P.op("dve", (lambda o, i_: (lambda e: e.reciprocal(out=o, in_=i_)))(rden[:, 0:16], PS[4][:, 0:16]), ["ps4"], ["rden"])
                        tt("dve", XO[:, :, 1024 + 4 * i:1024 + 4 * i + 4], PS[5][:, 0:16].rearrange("p (h t) -> p h t", t=4),
                           rden[:, 0:16].rearrange("p (h t) -> p h t", t=4), ALU.mult, ["ps5", "rden"], [("XOs", i)])
                    XOK = [("XO", h, c0) for h in range(4) for c0 in (0, 512)] + [("XOs", i) for i in range(16)]
                    for t in range(9):
                        xa, nn, xk = x1_tile(t)
                        c0, _ = ct_cols(t)
                        for n in range(4):
                            bk = pa_next()
                            for k in range(4):
                                mm(PS[bk][0:nn, :], XO[:, k, c0:c0 + nn], WXO[:, k, n * 512:(n + 1) * 512], k == 0, k == 3, ["WXO"] + XOK, [PSK[bk]])
                            tt("dve", xa[:, n * 512:(n + 1) * 512], xa[:, n * 512:(n + 1) * 512], PS[bk][0:nn, :], ALU.add, [PSK[bk], xk], [xk])
                    P.emit()

                with contextlib.ExitStack() as pc3:
                    st4 = {"junk": None, "ss": sb("ss4", [128, 1], F32, pc3), "xb": sb("xb4", [128, 2048], BF16, pc3)}
                    norm_to_ct(st4)
                    P.emit()
                with contextlib.ExitStack() as pc3:
                    HT = sb("HT", [128, 44, 576], BF16, pc3)
                    FH = 5632
                    for half in range(2):
                        segs = ((0, 512, 0), (1024, 64, 512)) if half == 0 else ((512, 512, 0),)
                        with contextlib.ExitStack() as pg3:
                            WG = [sb("WG%d_%d" % (i, half), [128, 16, 128], BF16, pg3) for i in range(4)]
                            SG = [sb("SG%d_%d" % (i, half), [128, 576], F32, pg3) for i in range(2)]
                            for hc in range(44):
                                wg, wgk = WG[(2 * hc) % 4], "WG%d" % ((2 * hc) % 4)
                                wu, wuk = WG[(2 * hc + 1) % 4], "WG%d" % ((2 * hc + 1) % 4)
                                load_w(wg, wgk, D["w_gate_up"], 0, 16, hc * 128, 128, gn[:, 2])
                                load_w(wu, wuk, D["w_gate_up"], 0, 16, FH + hc * 128, 128, gn[:, 2])
                                sg, sgk = SG[hc % 2], "SG%d" % (hc % 2)
                                pb_ = (hc % 2) * 4
                                for si_, (c0, T, o0) in enumerate(segs):
                                    bg, bu = pb_ + 2 * si_, pb_ + 2 * si_ + 1
                                    for k in range(16):
                                        mm(PS[bg][:, 0:T], wg[:, k, :], CT[:, k, c0:c0 + T], k == 0, k == 15, [wgk], [PSK[bg]])
                                    for k in range(16):
                                        mm(PS[bu][:, 0:T], wu[:, k, :], CT[:, k, c0:c0 + T], k == 0, k == 15, [wuk], [PSK[bu]])
                                    act(sg[:, o0:o0 + T], PS[bg][:, 0:T], AF.Silu, [PSK[bg]], [(sgk, o0)])
                                    tt("dve", HT[:, hc, o0:o0 + T], PS[bu][:, 0:T], sg[:, o0:o0 + T], ALU.mult, [PSK[bu], (sgk, o0)], [("HT", hc, o0)])
                            P.emit()
                        with contextlib.ExitStack() as pd3:
                            WD = [sb("WD%d_%d" % (i, half), [128, 44, 128], BF16, pd3) for i in range(2)]
                            YT = [sb("YT%d_%d" % (i, half), [128, 576], F32, pd3) for i in range(2)]
                            for ocb in range(16):
                                wd, wdk = WD[ocb % 2], "WD%d" % (ocb % 2)
                                load_w(wd, wdk, D["w_down"], 0, 44, ocb * 128, 128, None)
                                yt, ytk = YT[ocb % 2], "YT%d" % (ocb % 2)
                                b0 = (ocb % 2) * 2
                                for si_, (c0, T, o0) in enumerate(segs):
                                    bk = b0 + si_
                                    for k in range(44):
                                        mm(PS[bk][:, 0:T], wd[:, k, :], HT[:, k, o0:o0 + T], k == 0, k == 43, [wdk], [PSK[bk]])
                                    cp("act", yt[:, o0:o0 + T], PS[bk][:, 0:T], [PSK[bk]], [(ytk, o0)])
                                tb = 4 + (ocb % 2)
                                for j in range(4):
                                    tr(PS[tb][:, j * 128:(j + 1) * 128], yt[:, j * 128:(j + 1) * 128], ident_f[:], [(ytk, 0), "ident_f"], [PSK[tb]])
                                for j in range(4):
                                    t = half * 4 + j
                                    xs_ = X1[:, t, ocb * 128:(ocb + 1) * 128]
                                    tt("dve", xs_, xs_, PS[tb][:, j * 128:(j + 1) * 128], ALU.add, [PSK[tb], ("X1", t)], [("X1", t)])
                                if half == 0:
                                    tr(PS[6 + (ocb % 2)][0:64, 0:128], yt[:, 512:576], ident_f[:], [(ytk, 512), "ident_f"], [PSK[6 + (ocb % 2)]])
                                    xs_ = X1s[0:64, ocb * 128:(ocb + 1) * 128]
                                    tt("dve", xs_, xs_, PS[6 + (ocb % 2)][0:64, 0:128], ALU.add, [PSK[6 + (ocb % 2)], ("X1", 8)], [("X1", 8)])
                            for j in range(4):
                                t = half * 4 + j
                                dma("sp", D["yp"][t * 128:(t + 1) * 128, :], X1[:, t, 0:2048], [("X1", t)], [])
                            P.emit()
                    dma("sp", D["ys"][:, :], X1s[0:64, :], [("X1", 8)], [])
                    P.emit()

        P.emit(final=True)
    return nc


def _prep_inputs(inp):
    n_phys = inp["cache_sb_k"].shape[1]
    ck = np.ascontiguousarray(inp["cache_sb_k"][0]).reshape(n_phys * 128, 1024)
    cv = np.ascontiguousarray(inp["cache_sb_v"][0]).reshape(n_phys * 128, 1024)
    maps = []
    for c in range(8):
        b, hf = c // 2, c % 2
        xp = np.concatenate([inp["x_prompt"][b, 0:1024], inp["x_prompt"][b, hf * 1024:hf * 1024 + 1024]], axis=0)
        m = {
            "xp": np.ascontiguousarray(xp, dtype=np.float32),
            "xs": np.ascontiguousarray(inp["x_sample"][16 * c:16 * c + 16]).reshape(64, 2048),
            "mem": np.ascontiguousarray(inp["mem_prompt"][b]),
            "ck": ck, "cv": cv,
            "pt": np.ascontiguousarray(inp["page_table"][16 * c:16 * c + 16]).reshape(1, 256).astype(np.int32),
            "sg": np.ascontiguousarray(inp["state_gdn"][0, 16 * c:16 * c + 16]),
            "sc": np.ascontiguousarray(inp["state_gdn_conv"][0, 16 * c:16 * c + 16]).reshape(48, 3072),
            "cmk": np.ascontiguousarray(inp["cache_mem_k"][0, 16 * c:16 * c + 16]).reshape(16, 256, 512),
            "cmv": np.ascontiguousarray(inp["cache_mem_v"][0, 16 * c:16 * c + 16]).reshape(16, 256, 512),
            "flags": np.array([[float(hf), 0.0 if hf else NEG, 0.0, 0.0]], np.float32),
        }
        for n, s in W_NAMES:
            m[n] = np.ascontiguousarray(inp[n][0]).reshape(s)
        maps.append(m)
    return n_phys, maps


def _assemble(res):
    R = res.results
    f = np.float32
    yp = np.zeros((4, 2048, 2048), f); skp = np.zeros((1, 4, 2048, 8, 128), f); svp = np.zeros((1, 4, 2048, 8, 128), f)
    ys = np.zeros((128, 4, 2048), f); sks = np.zeros((1, 128, 4, 8, 128), f); svs = np.zeros((1, 128, 4, 8, 128), f)
    gsp = np.zeros((1, 4, 8, 128, 128), f); gcp = np.zeros((1, 4, 3, 3072), f)
    gss = np.zeros((1, 128, 8, 128, 128), f); gcs = np.zeros((1, 128, 3, 3072), f)
    mkp = np.zeros((1, 4, 256, 4, 128), f); mvp = np.zeros((1, 4, 256, 4, 128), f)
    for c in range(8):
        b, hf = c // 2, c % 2
        r = R[c]
        sl = slice(hf * 1024, hf * 1024 + 1024)
        yp[b, sl] = r["yp"]
        skp[0, b, sl] = r["skp"].reshape(1024, 8, 128)
        svp[0, b, sl] = r["svp"].reshape(1024, 8, 128)
        ss = slice(16 * c, 16 * c + 16)
        ys[ss] = r["ys"].reshape(16, 4, 2048)
        sks[0, ss] = r["sks"].reshape(16, 4, 8, 128)
        svs[0, ss] = r["svs"].reshape(16, 4, 8, 128)
        gss[0, ss] = r["gss"]
        gcs[0, ss] = r["gcs"].reshape(16, 3, 3072)
        if hf == 1:
            gsp[0, b] = r["gsp"]
            gcp[0, b] = r["gcp"]
        else:
            mkp[0, b] = r["mkp"].reshape(256, 4, 128)
            mvp[0, b] = r["mvp"].reshape(256, 4, 128)
    return (yp, ys, skp, svp, sks, svs, gsp, gcp, gss, gcs, mkp, mvp)


def kernel(**inputs):
    inp = {k: np.asarray(v) for k, v in inputs.items()}
    n_phys, maps = _prep_inputs(inp)
    nc = build(n_phys)
    res = run_bass_kernel_spmd(nc, maps, core_ids=list(range(8)))
    return _assemble(res)
```

```python
import contextlib
import os
import numpy as np
import concourse.bass as bass
import concourse.mybir as mybir
from concourse.bass_utils import run_bass_kernel_spmd

F32 = mybir.dt.float32
BF16 = mybir.dt.bfloat16
I32 = mybir.dt.int32
F32R = mybir.dt.float32r
AF = mybir.ActivationFunctionType
ALU = mybir.AluOpType

ENGS = ("pe", "act", "dve", "pool", "sp")
NDSEM = 12
EPS = 1e-6
NEG = -30000.0


class Instr:
    __slots__ = ("eng", "fn", "dma", "deps", "signal", "count", "dsem", "dcount", "barrier")

    def __init__(self, eng, fn, dma):
        self.eng = eng
        self.fn = fn
        self.dma = dma
        self.deps = []
        self.signal = False
        self.count = 0
        self.dsem = None
        self.dcount = 0
        self.barrier = None


class Prog:
    def __init__(self, nc, stack):
        self.nc = nc
        self.streams = {e: [] for e in ENGS}
        self.last_writer = {}
        self.readers = {}
        self.ndma = {e: 0 for e in ENGS}
        self.dma_hist = {e: [] for e in ENGS}
        self.ccount = {e: 0 for e in ENGS}
        self.waited = {e: {} for e in ENGS}
        self.sems = {e: stack.enter_context(nc.semaphore("s_" + e)) for e in ENGS}
        self.dsems = {}
        for e in ("sp", "pool", "act"):
            for i in range(NDSEM):
                self.dsems[(e, i)] = stack.enter_context(nc.semaphore("d_%s_%d" % (e, i)))
        self.pending_barrier = None

    def op(self, eng, fn, reads=(), writes=(), dma=False):
        ins = Instr(eng, fn, dma)
        deps = {}
        px = [k for k in reads if isinstance(k, str) and k[:2] == "ps" and k[2:].isdigit()]
        if px:
            reads = [k for k in reads if k not in px]
            writes = list(writes) + px
        for k in reads:
            w = self.last_writer.get(k)
            if w is not None:
                deps[id(w)] = w
        for k in writes:
            w = self.last_writer.get(k)
            if w is not None:
                deps[id(w)] = w
            rd = self.readers.get(k)
            if rd:
                for r in rd.values():
                    deps[id(r)] = r
        ins.deps = list(deps.values())
        rkey = ("d", eng, self.ndma[eng] % NDSEM) if dma else ("c", eng)
        for k in reads:
            self.readers.setdefault(k, {})[rkey] = ins
        for k in writes:
            self.last_writer[k] = ins
            self.readers[k] = {}
        if dma:
            i = self.ndma[eng]
            self.ndma[eng] += 1
            ins.dsem = (eng, i % NDSEM)
            ins.dcount = 16 * (i // NDSEM + 1)
            hist = self.dma_hist[eng]
            if i >= NDSEM:
                ins.deps.append(hist[i - NDSEM])
            hist.append(ins)
        self.streams[eng].append(ins)
        return ins

    def emit(self, final=False):
        nc = self.nc
        streams = self.streams
        for e in ENGS:
            for ins in streams[e]:
                for d in ins.deps:
                    if d.dma:
                        continue
                    if d.eng == ins.eng and not ins.dma and d.eng == "pe":
                        continue
                    d.signal = True
            for ins in reversed(streams[e]):
                if not ins.dma:
                    ins.signal = True
                    break
        for e in ENGS:
            c = self.ccount[e]
            for ins in streams[e]:
                if not ins.dma and ins.signal:
                    c += 1
                    ins.count = c
            self.ccount[e] = c
        bar = []
        for e in ENGS:
            if self.ccount[e]:
                bar.append((("c", e), self.sems[e], self.ccount[e]))
            hist = self.dma_hist[e]
            for i in range(max(0, len(hist) - NDSEM), len(hist)):
                d = hist[i]
                bar.append((("d",) + d.dsem, self.dsems[d.dsem], d.dcount))
        prev_bar = self.pending_barrier

        def run(engobj, e):
            waited = self.waited[e]

            def wait(key, sem, val):
                if waited.get(key, 0) >= val:
                    return
                waited[key] = val
                engobj.wait_ge(sem, val)

            if prev_bar is not None:
                for key, sem, val in prev_bar:
                    if key == ("c", e):
                        continue
                    wait(key, sem, val)
            for ins in streams[e]:
                for d in ins.deps:
                    if d.dma:
                        wait(("d",) + d.dsem, self.dsems[d.dsem], d.dcount)
                    else:
                        if d.eng == e and not ins.dma and e == "pe":
                            continue
                        wait(("c", d.eng), self.sems[d.eng], d.count)
                r = ins.fn(engobj)
                if ins.dma:
                    r.then_inc(self.dsems[ins.dsem], 16)
                elif ins.signal:
                    r.then_inc(self.sems[e], 1)
            if final and e == "sp":
                for key, sem, val in bar:
                    wait(key, sem, val)

        with nc.Block() as block:
            @block.tensor
            def _(eng):
                run(eng, "pe")

            @block.scalar
            def _(eng):
                run(eng, "act")

            @block.vector
            def _(eng):
                run(eng, "dve")

            @block.gpsimd
            def _(eng):
                run(eng, "pool")

            @block.sync
            def _(eng):
                run(eng, "sp")

        self.pending_barrier = bar
        self.streams = {e: [] for e in ENGS}
        self.last_writer = {}
        self.readers = {}


C_SQ, C_SK, C_SV, C_GQ, C_GK, C_GV, C_GA, C_GB, C_GZ = 0, 1024, 2048, 3072, 4096, 5120, 6144, 6152, 6160
NT = 2112
OWN0, SMP0 = 1024, 2048
NTO = 1088

W_NAMES = [
    ("norm_mix_g", [1, 2048]), ("w_in", [2048, 7184]), ("sb_q_norm_g", [1, 128]), ("sb_k_norm_g", [1, 128]),
    ("sb_logit_bias", [1, 8]), ("gdn_conv_w", [4, 3072]), ("gdn_a_log", [1, 8]), ("gdn_dt_bias", [1, 8]),
    ("gdn_out_norm_g", [1, 128]), ("w_out", [2048, 2048]), ("norm_x_g", [1, 2048]), ("norm_mem_g", [1, 2048]),
    ("w_xq", [2048, 512]), ("w_mk", [2048, 512]), ("w_mv", [2048, 512]), ("x_q_norm_g", [1, 128]),
    ("x_k_norm_g", [1, 128]), ("w_xo", [512, 2048]), ("norm_ffn_g", [1, 2048]), ("w_gate_up", [2048, 11264]),
    ("w_down", [5632, 2048]),
]
OUT_SPECS = [
    ("yp", [1024, 2048]), ("ys", [64, 2048]), ("skp", [1024, 1024]), ("svp", [1024, 1024]),
    ("sks", [64, 1024]), ("svs", [64, 1024]), ("gsp", [8, 128, 128]), ("gcp", [3, 3072]),
    ("gss", [16, 8, 128, 128]), ("gcs", [48, 3072]), ("mkp", [256, 512]), ("mvp", [256, 512]),
]


def build(n_phys, stage=99):
    nc = bass.Bass("TRN2", target_bir_lowering=False)

    def din(n, s, dt=F32):
        return nc.dram_tensor(n, list(s), dt, kind="ExternalInput").ap()

    def dout(n, s, dt=F32):
        return nc.dram_tensor(n, list(s), dt, kind="ExternalOutput").ap()

    D = {}
    D["xp"] = din("xp", [2048, 2048])
    D["xs"] = din("xs", [64, 2048])
    D["mem"] = din("mem", [256, 2048])
    D["ck"] = din("ck", [n_phys * 128, 1024])
    D["cv"] = din("cv", [n_phys * 128, 1024])
    D["pt"] = din("pt", [1, 256], I32)
    D["sg"] = din("sg", [16, 8, 128, 128])
    D["sc"] = din("sc", [48, 3072])
    D["cmk"] = din("cmk", [16, 256, 512])
    D["cmv"] = din("cmv", [16, 256, 512])
    D["flags"] = din("flags", [1, 4])
    for n, s in W_NAMES:
        D[n] = din(n, s)
    for n, s in OUT_SPECS:
        D[n] = dout(n, s)
    D["scr_gab"] = nc.dram_tensor("scr_gab", [64, 16], F32, kind="Internal").ap()

    with contextlib.ExitStack() as perm:
        P = Prog(nc, perm)

        def sb(name, shape, dt, st=perm):
            return st.enter_context(nc.sbuf_tensor(name, list(shape), dt))

        def act(out, in_, func, r, w, bias=0.0, scale=1.0, accum=None):
            if accum is None:
                P.op("act", lambda e: e.activation(out=out, in_=in_, func=func, bias=bias, scale=scale), r, w)
            else:
                P.op("act", lambda e: e.activation(out=out, in_=in_, func=func, bias=bias, scale=scale, accum_out=accum), r, w)

        def cp(eng, out, in_, r, w):
            if eng == "act":
                P.op("act", lambda e: e.activation(out=out, in_=in_, func=AF.Copy), r, w)
            else:
                P.op(eng, lambda e: e.tensor_copy(out=out, in_=in_), r, w)

        def tt(eng, out, a, b, op, r, w):
            P.op(eng, lambda e: e.tensor_tensor(out=out, in0=a, in1=b, op=op), r, w)

        def ts(eng, out, a, s1, s2, op0, op1, r, w):
            if s2 is None:
                P.op(eng, lambda e: e.tensor_scalar(out=out, in0=a, scalar1=s1, scalar2=None, op0=op0), r, w)
            else:
                P.op(eng, lambda e: e.tensor_scalar(out=out, in0=a, scalar1=s1, scalar2=s2, op0=op0, op1=op1), r, w)

        def stt(out, a, s, b, op0, op1, r, w):
            P.op("dve", lambda e: e.scalar_tensor_tensor(out=out, in0=a, scalar=s, in1=b, op0=op0, op1=op1), r, w)

        def mm(out, lhsT, rhs, start, stop, r, w):
            P.op("pe", lambda e: e.matmul(out, lhsT=lhsT, rhs=rhs, start=start, stop=stop), r, w)

        def tr(out, in_, ident, r, w):
            P.op("pe", lambda e: e.transpose(out=out, in_=in_, identity=ident), r, w)

        def dma(eng, out, in_, r, w):
            P.op(eng, lambda e: e.dma_start(out=out, in_=in_), r, w, dma=True)

        def memset(eng, ap, val, w):
            P.op(eng, lambda e: e.memset(ap, val), (), w)

        def asel(out, in_, pattern, cmp, base, cm, r, w, fill=0.0):
            P.op("pool", lambda e: e.affine_select(out=out, in_=in_, pattern=pattern, compare_op=cmp, fill=fill,
                                                   base=base, channel_multiplier=cm), r, w)

        PS = [perm.enter_context(nc.psum_tensor("ps%d" % i, [128, 512], F32)) for i in range(8)]
        PSK = ["ps%d" % i for i in range(8)]

        def psb(i):
            return PS[i][:].bitcast(BF16)

        ident_f = sb("ident_f", [128, 128], F32)
        ident_b = sb("ident_b", [128, 128], BF16)
        ones_f = sb("ones_f", [128, 128], F32)
        ones_b = sb("ones_b", [128, 128], BF16)
        tri_gt = sb("tri_gt", [128, 128], F32)
        tri_le = sb("tri_le", [128, 128], F32)
        memset("pool", ones_f[:], 1.0, ["ones_f"])
        memset("pool", ones_b[:], 1.0, ["ones_b"])
        memset("pool", ident_f[:], 0.0, ["ident_f"])
        asel(ident_f[:], ident_f[:], [[-1, 128]], ALU.not_equal, 0, 1, ["ident_f"], ["ident_f"], fill=1.0)
        cp("dve", ident_b[:], ident_f[:], ["ident_f"], ["ident_b"])
        asel(tri_gt[:], ones_f[:], [[-1, 128]], ALU.is_gt, 0, 1, ["ones_f"], ["tri_gt"])
        asel(tri_le[:], ones_f[:], [[1, 128]], ALU.is_ge, 0, -1, ["ones_f"], ["tri_le"])
        tri_gt_r = sb("tri_gt_r", [128, 128], F32)
        ones_r = sb("ones_r", [128, 128], F32)
        cp("dve", tri_gt_r[:].bitcast(F32R), tri_gt[:], ["tri_gt"], ["tri_gt_r"])
        cp("dve", ones_r[:].bitcast(F32R), ones_f[:], ["ones_f"], ["ones_r"])

        flg = sb("flg", [128, 4], F32)
        dma("sp", flg[:], D["flags"][0:1, :].partition_broadcast(128), [], ["flg"])
        sbb = sb("sbb", [128, 8], F32)
        dma("sp", sbb[:], D["sb_logit_bias"][0:1, :].partition_broadcast(128), [], ["sbb"])
        sbb_pre = sb("sbb_pre", [128, 8], F32)
        ts("dve", sbb_pre[:], sbb[:], flg[:, 1:2], None, ALU.add, None, ["sbb", "flg"], ["sbb_pre"])
        alog = sb("alog", [128, 8], F32)
        dma("sp", alog[:], D["gdn_a_log"][0:1, :].partition_broadcast(128), [], ["alog"])
        dtb = sb("dtb", [128, 8], F32)
        dma("sp", dtb[:], D["gdn_dt_bias"][0:1, :].partition_broadcast(128), [], ["dtb"])
        nea = sb("nea", [128, 8], F32)
        act(nea[:], alog[:], AF.Exp, ["alog"], ["nea"])
        ts("dve", nea[:], nea[:], -1.0, None, ALU.mult, None, ["nea"], ["nea"])
        hg_row = sb("hg_row", [128, 128], F32)
        memset("pool", hg_row[:], 0.0, ["hg_row"])
        for i, n in enumerate(["sb_q_norm_g", "sb_k_norm_g", "x_q_norm_g", "x_k_norm_g", "gdn_out_norm_g"]):
            dma("sp", hg_row[i:i + 1, :], D[n][0:1, :], [], ["hg_row"])
        hg = sb("hg", [128, 8], F32)
        tr(PS[7][:, 0:128], hg_row[:, :], ident_f[:, :], ["hg_row", "ident_f"], ["ps7"])
        cp("dve", hg[:], PS[7][:, 0:8], ["ps7"], ["hg"])
        gn = sb("gn", [128, 4, 16, 1], F32)
        gn_row = sb("gn_row", [128, 128], F32)
        memset("pool", gn_row[:], 0.0, ["gn_row"])
        for i, n in enumerate(["norm_mix_g", "norm_x_g", "norm_ffn_g", "norm_mem_g"]):
            dma("sp", gn_row[16 * i:16 * i + 16, :], D[n].rearrange("o (k p) -> (o k) p", p=128), [], ["gn_row"])
        tr(PS[7][:, 128:256], gn_row[:, :], ident_f[:, :], ["gn_row", "ident_f"], ["ps7"])
        cp("dve", gn[:].rearrange("p a k o -> p (a k o)"), PS[7][:, 128:192], ["ps7"], ["gn"])

        BIG = sb("BIG", [128, 16 * NT], BF16)
        XT = BIG.reshape([128, 16, NT])
        CT = sb("CT", [128, 16, NTO], BF16)
        MKT = sb("MKT", [128, 4, 256], BF16)
        MV = sb("MV", [128, 2, 512], BF16)
        QTS = sb("QTS", [128, 8, 64], BF16)
        KTS = sb("KTS", [128, 8, 64], BF16)
        CW = sb("CW", [128, 24, 4], F32)
        SCT = sb("SCT", [128, 24, 48], F32)

        wst = [sb("wst%d" % i, [128, 512], F32) for i in range(4)]
        wcnt = [0]
        cvt_rot = [0]

        def load_w(dst, dkey, src, k0, nk, col0, ncols, gains=None, dcol0=0):
            srcv = src.rearrange("(k p) n -> p k n", p=128)
            per = max(1, 512 // ncols)
            kk = k0
            while kk < k0 + nk:
                n = min(per, k0 + nk - kk)
                i = wcnt[0] % 4
                wcnt[0] += 1
                st = wst[i]
                stv = st[:, 0:n * ncols].rearrange("p (k n) -> p k n", k=n)
                dma("sp", stv, srcv[:, kk:kk + n, col0:col0 + ncols], [], ["wst%d" % i])
                eng = ("dve", "pool", "act")[cvt_rot[0] % 3]
                cvt_rot[0] += 1
                o = dst[:, kk:kk + n, dcol0:dcol0 + ncols]
                if gains is None:
                    cp(eng, o, stv, ["wst%d" % i], [dkey])
                else:
                    if eng == "act":
                        eng = "dve"
                    g = gains[:, kk:kk + n, :].broadcast_to([128, n, ncols])
                    tt(eng, o, stv, g, ALU.mult, ["wst%d" % i, "gn"], [dkey])
                kk += n

        def rms_to_fm(st, src_rows, n, dstT, c0, dkey, xin, xkey, tag):
            dma("sp", xin[0:n, :], src_rows, [], [xkey])
            junk = st["junk"]
            ss = st["ss"]
            act(junk[0:n, :], xin[0:n, :], AF.Square, [xkey], ["junk", "ss"], accum=ss[0:n, :])
            act(ss[0:n, :], ss[0:n, :], AF.Ln, ["ss"], ["ss"], bias=EPS, scale=1.0 / 2048)
            act(ss[0:n, :], ss[0:n, :], AF.Exp, ["ss"], ["ss"], scale=-0.5)
            xb = st["xb"]
            ts("dve", xb[0:n, :], xin[0:n, :], ss[0:n, 0:1], None, ALU.mult, None, [xkey, "ss"], ["xb"])
            for half in range(2):
                bank = 6 + half
                pv = psb(bank)
                for k in range(8):
                    kk = half * 8 + k
                    tr(pv[:, k * 128:k * 128 + n], xb[0:n, kk * 128:(kk + 1) * 128], ident_b[0:n, 0:n],
                       ["xb", "ident_b"], [PSK[bank]])
                src = pv[:, 0:1024].rearrange("p (a b) -> p a b", a=8)[:, :, 0:n]
                cp("act" if half == 0 else "dve", dstT[:, half * 8:half * 8 + 8, c0:c0 + n], src, [PSK[bank]],
                   [(dkey, tag, half)])

        def fm_norm(psk, ps_ap, T, sc, out, wkeys, st, rextra=(), mean=True):
            sq = st["sq"]
            rr = st["rr"]
            act(sq[:, 0:T], ps_ap, AF.Square, [psk], ["sq"])
            mm(PS[5][:, 0:T], ones_b[:], sq[:, 0:T], True, True, ["sq", "ones_b"], ["ps5"])
            act(rr[:, 0:T], PS[5][:, 0:T], AF.Ln, ["ps5"], ["rr"], bias=EPS, scale=(1.0 / 128 if mean else 1.0))
            act(rr[:, 0:T], rr[:, 0:T], AF.Exp, ["rr"], ["rr"], scale=-0.5)
            stt(out, ps_ap, sc, rr[:, 0:T], ALU.mult, ALU.mult, [psk, "rr"] + list(rextra), wkeys)

        with contextlib.ExitStack() as ph:
            rowbuf = sb("rowbuf", [128, 3072], F32, ph)
            memset("pool", rowbuf[:], 0.0, ["rowbuf"])
            dma("sp", rowbuf[0:4, :], D["gdn_conv_w"][:, :], [], ["rowbuf"])
            for c4 in range(6):
                for j in range(4):
                    c = c4 * 4 + j
                    tr(PS[4][:, j * 128:(j + 1) * 128], rowbuf[:, c * 128:(c + 1) * 128], ident_f[:], ["rowbuf", "ident_f"], ["ps4"])
                cp("dve", CW[:, c4 * 4:c4 * 4 + 4, :], PS[4][:, :].rearrange("p (j x) -> p j x", j=4)[:, :, 0:4], ["ps4"], ["CW"])
            dma("sp", rowbuf[0:48, :], D["sc"][:, :], ["rowbuf"], ["rowbuf"])
            for c4 in range(6):
                for j in range(4):
                    c = c4 * 4 + j
                    tr(PS[4][:, j * 128:(j + 1) * 128], rowbuf[:, c * 128:(c + 1) * 128], ident_f[:], ["rowbuf", "ident_f"], ["ps4"])
                cp("dve", SCT[:, c4 * 4:c4 * 4 + 4, :], PS[4][:, :].rearrange("p (j x) -> p j x", j=4)[:, :, 0:48], ["ps4"], ["SCT"])
            P.emit()

        with contextlib.ExitStack() as ph:
            st1 = {
                "junk": sb("junk", [128, 2048], BF16, ph),
                "ss": sb("ss", [128, 1], F32, ph),
                "xb": sb("xb", [128, 2048], BF16, ph),
                "sq": sb("sq1", [128, 512], BF16, ph),
                "rr": sb("rr1", [128, 512], F32, ph),
            }
            xin = [sb("xin%d" % i, [128, 2048], F32, ph) for i in range(2)]
            KD = int(os.environ.get("KDBG", "9"))
            for t in range(17 if KD >= 3 else (1 if KD == 2 else 0)):
                if t < 16:
                    rows, n, c0 = D["xp"][t * 128:(t + 1) * 128, :], 128, t * 128
                else:
                    rows, n, c0 = D["xs"][:, :], 64, SMP0
                rms_to_fm(st1, rows, n, XT, c0, "XT", xin[t % 2], "xin%d" % (t % 2), t)
            MT = sb("MT", [128, 16, 256], BF16, ph)
            for t in range(2 if KD >= 4 else 0):
                rms_to_fm(st1, D["mem"][t * 128:(t + 1) * 128, :], 128, MT, t * 128, "MT", xin[t % 2], "xin%d" % (t % 2), t)
            wmk = sb("wmk", [128, 16, 512], BF16, ph)
            wmv = sb("wmv", [128, 16, 512], BF16, ph)
            load_w(wmk, "wmk", D["w_mk"], 0, 16, 0, 512, gn[:, 3])
            load_w(wmv, "wmv", D["w_mv"], 0, 16, 0, 512, gn[:, 3])
            MTK = [("MT", t, h2) for t in range(2) for h2 in range(2)]
            mk32 = sb("mk32", [128, 256], F32, ph)
            mko = sb("mko", [128, 2, 512], F32, ph)
            mvo = sb("mvo", [128, 2, 512], F32, ph)
            for h in range(4 if KD >= 5 else 0):
                for k in range(16):
                    mm(PS[0][:, 0:256], wmk[:, k, h * 128:(h + 1) * 128], MT[:, k, :], k == 0, k == 15,
                       ["wmk"] + MTK, ["ps0"])
                fm_norm("ps0", PS[0][:, 0:256], 256, hg[:, 3:4], mk32[:, :], ["mk32"], st1, ["hg"])
                cp("act", MKT[:, h, :], mk32[:, :], ["mk32"], [("MKT", h)])
                for t in range(2):
                    tr(PS[1][:, t * 128:(t + 1) * 128], mk32[:, t * 128:(t + 1) * 128], ident_f[:], ["mk32", "ident_f"], ["ps1"])
                cp("dve", mko[:, :, h * 128:(h + 1) * 128], PS[1][:, 0:256].rearrange("p (t d) -> p t d", t=2), ["ps1"], ["mko"])
            if KD >= 5:
                dma("sp", D["mkp"].rearrange("(t p) c -> p t c", p=128), mko[:], ["mko"], [])
            for t in range(2 if KD >= 6 else 0):
                for k in range(16):
                    mm(PS[2 + t][:, :], MT[:, k, t * 128:(t + 1) * 128], wmv[:, k, :], k == 0, k == 15, ["wmv"] + MTK, [PSK[2 + t]])
                cp("dve", mvo[:, t, :], PS[2 + t][:, :], [PSK[2 + t]], ["mvo"])
                cp("act", MV[:, t, :], mvo[:, t, :], ["mvo"], [("MV", t)])
            if KD >= 6:
                dma("sp", D["mvp"].rearrange("(t p) c -> p t c", p=128), mvo[:], ["mvo"], [])
            P.emit()

        SCALE = 128.0 ** -0.5
        with contextlib.ExitStack() as ph:
            st2 = {"sq": sb("sq2", [128, 512], BF16, ph), "rr": sb("rr2", [128, 512], F32, ph)}
            WB = [sb("wb%d" % i, [128, 16, 128], BF16, ph) for i in range(5)]
            wbi = [0]

            def wb_next():
                i = wbi[0] % 5
                wbi[0] += 1
                return WB[i], "wb%d" % i

            pa = [0]

            def pa_next():
                i = pa[0] % 2
                pa[0] += 1
                return i

            def proj_fm(bank, w, wkey, xc0, T):
                for k in range(16):
                    mm(PS[bank][:, 0:T], w[:, k, :], XT[:, k, xc0:xc0 + T], k == 0, k == 15, [wkey], [PSK[bank]])

            with contextlib.ExitStack() as ph2a:
                ph2 = ph
                ph = ph2a
                MSK = sb("MSK", [128, 4, 512], BF16, ph)
                memset("pool", MSK[:], 1.0, ["MSK"])
                for kd in range(4):
                    asel(MSK[:, kd, :], MSK[:, kd, :], [[1, 512]], ALU.is_ge, -128 * kd - 1, -1, ["MSK"], ["MSK"])
                QT = sb("QT", [128, NTO], BF16, ph)
                KT = sb("KT", [128, NT], BF16, ph)
                VH = sb("VH", [128, 17, 128], BF16, ph)
                kn32 = sb("kn32", [128, 512], F32, ph)
                kout = sb("kout", [128, 9, 128], F32, ph)
                vout = sb("vout", [128, 9, 128], F32, ph)
                EXb = [sb("EX%d" % i, [128, 512], F32, ph) for i in range(2)]
                T1b = [sb("T1%d" % i, [128, 512], F32, ph) for i in range(2)]
                Wbb = [sb("Wb%d" % i, [128, 512], BF16, ph) for i in range(2)]
                Rrs = [sb("Rr%d" % i, [128, 512], F32, ph) for i in range(2)]
                n_sb = 8 if stage >= 2 else 0
                for h in range(n_sb):
                    wq, wqk = wb_next()
                    wk, wkk = wb_next()
                    wv, wvk = wb_next()
                    load_w(wq, wqk, D["w_in"], 0, 16, C_SQ + h * 128, 128, gn[:, 0])
                    load_w(wk, wkk, D["w_in"], 0, 16, C_SK + h * 128, 128, gn[:, 0])
                    load_w(wv, wvk, D["w_in"], 0, 16, C_SV + h * 128, 128, gn[:, 0])
                    for xc0, T, qc0 in ((OWN0, 512, 0), (OWN0 + 512, 512, 512), (SMP0, 64, 1024)):
                        bk = pa_next()
                        proj_fm(bk, wq, wqk, xc0, T)
                        fm_norm(PSK[bk], PS[bk][:, 0:T], T, hg[:, 0:1], QT[:, qc0:qc0 + T], [("QT", qc0)], st2, ["hg"])
                    cp("pool", QTS[:, h, :], QT[:, 1024:1088], [("QT", 1024)], [("QTS", h)])
                    for gi, (xc0, T) in enumerate(((0, 512), (512, 512), (OWN0, 512), (OWN0 + 512, 512), (SMP0, 64))):
                        bk = pa_next()
                        proj_fm(bk, wk, wkk, xc0, T)
                        fm_norm(PSK[bk], PS[bk][:, 0:T], T, hg[:, 1:2], kn32[:, 0:T], ["kn32"], st2, ["hg"])
                        cp("act", KT[:, xc0:xc0 + T], kn32[:, 0:T], ["kn32"], [("KT", gi)])
                        if gi >= 2:
                            nt = T // 128 if T >= 128 else 1
                            w_ = 128 if T >= 128 else T
                            for j in range(nt):
                                tr(PS[7][0:w_, j * 128:(j + 1) * 128], kn32[:, j * 128:j * 128 + w_], ident_f[:], ["kn32", "ident_f"], ["ps7"])
                            t0 = (gi - 2) * 4
                            cp("dve", kout[0:w_, t0:t0 + nt, :], PS[7][0:w_, 0:nt * 128].rearrange("p (t d) -> p t d", t=nt),
                               ["ps7"], ["kout"])
                    cp("pool", KTS[:, h, :], KT[:, SMP0:SMP0 + 64], [("KT", 4)], [("KTS", h)])
                    dma("sp", D["skp"][:, h * 128:(h + 1) * 128].rearrange("(t p) d -> p t d", p=128), kout[:, 0:8, :], ["kout"], [])
                    dma("sp", D["sks"][:, h * 128:(h + 1) * 128], kout[0:64, 8, :], ["kout"], [])
                    for g4 in range(5):
                        bk = pa_next()
                        tiles = list(range(g4 * 4, min(g4 * 4 + 4, 17)))
                        for ti, t in enumerate(tiles):
                            n = 128 if t < 16 else 64
                            for k in range(16):
                                mm(PS[bk][0:n, ti * 128:(ti + 1) * 128], XT[:, k, t * 128:t * 128 + n], wv[:, k, :], k == 0, k == 15,
                                   [wvk], [PSK[bk]])
                        nt = len(tiles)
                        n = 128 if g4 < 4 else 64
                        src = PS[bk][0:n, 0:nt * 128].rearrange("p (t d) -> p t d", t=nt)
                        if g4 >= 2:
                            vo = vout[0:n, g4 * 4 - 8:g4 * 4 - 8 + nt, :]
                            cp("dve", vo, src, [PSK[bk]], ["vout"])
                            cp("act", VH[0:n, g4 * 4:g4 * 4 + nt, :], vo, ["vout"], [("VH", g4)])
                        else:
                            cp("act", VH[0:n, g4 * 4:g4 * 4 + nt, :], src, [PSK[bk]], [("VH", g4)])
                    dma("sp", D["svp"][:, h * 128:(h + 1) * 128].rearrange("(t p) d -> p t d", p=128), vout[:, 0:8, :], ["vout"], [])
                    dma("sp", D["svs"][:, h * 128:(h + 1) * 128], vout[0:64, 8, :], ["vout"], [("dram", "svs")])
                    VHK = [("VH", g) for g in range(5)]
                    KTK = [("KT", g) for g in range(5)]
                    tl = [list(range(8 + 4 * q_ + 3, 7, -1)) + list(range(7, -1, -1)) for q_ in range(2)]
                    banks = ((2, 4, 6), (3, 7, 0))
                    for idx in range(max(len(tl[0]), len(tl[1]))):
                        act_ch = [q_ for q_ in range(2) if idx < len(tl[q_])]
                        info = {}
                        for stq in act_ch:
                            tiles = tl[stq]
                            j = tiles[idx]
                            kd = (j - 8) - 4 * stq
                            info[stq] = dict(j=j, first=idx == 0, last=idx == len(tiles) - 1, kd=kd, diag=(j >= 8 and kd >= 0),
                                             bias=(sbb_pre[:, h:h + 1] if j < 8 else sbb[:, h:h + 1]), bk=("sbb_pre" if j < 8 else "sbb"))
                        for stq in act_ch:
                            I_ = info[stq]
                            zb = banks[stq][0]
                            EX, exk = EXb[stq], "EX%d" % stq
                            mm(PS[zb][:, :], KT[:, I_["j"] * 128:(I_["j"] + 1) * 128], QT[:, stq * 512:(stq + 1) * 512], True, True,
                               KTK + [("QT", stq * 512)], [PSK[zb]])
                            act(EX[:].bitcast(F32R), PS[zb][:, :], AF.Exp, [PSK[zb], I_["bk"]], [exk], bias=I_["bias"], scale=SCALE)
                            act(EX[:].bitcast(F32R), EX[:], AF.Ln, [exk], [exk], bias=1.0)
                            if I_["diag"]:
                                tt("dve", EX[:].bitcast(F32R), EX[:], MSK[:, I_["kd"], :], ALU.mult, [exk, "MSK"], [exk])
                        for stq in act_ch:
                            I_ = info[stq]
                            cb_ = banks[stq][1]
                            EX, exk = EXb[stq], "EX%d" % stq
                            Aq, ak_ = Rrs[stq], "Rr%d" % stq
                            mm(PS[cb_][:, :], tri_gt_r[:].bitcast(F32R), EX[:].bitcast(F32R), True, I_["first"], [exk, "tri_gt_r"], [PSK[cb_]])
                            if not I_["first"]:
                                mm(PS[cb_][:, :], ones_r[:].bitcast(F32R), Aq[:].bitcast(F32R), False, True, [ak_, "ones_r"], [PSK[cb_]])
                        for stq in act_ch:
                            I_ = info[stq]
                            zb, cb_ = banks[stq][0], banks[stq][1]
                            EX, exk = EXb[stq], "EX%d" % stq
                            T1, t1k = T1b[stq], "T1%d" % stq
                            Wb, wbk = Wbb[stq], "Wb%d" % stq
                            Aq, ak_ = Rrs[stq], "Rr%d" % stq
                            stt(T1[:], PS[zb][:, :], SCALE, EX[:], ALU.mult, ALU.subtract, [PSK[zb], exk], [t1k])
                            tt("dve", T1[:], T1[:], PS[cb_][:, :], ALU.subtract, [t1k, PSK[cb_]], [t1k])
                            act(Wb[:], T1[:], AF.Exp, [t1k, I_["bk"]], [wbk], bias=I_["bias"])
                            if I_["diag"]:
                                tt("pool", Wb[:], Wb[:], MSK[:, I_["kd"], :], ALU.mult, [wbk, "MSK"], [wbk])
                            if not I_["last"]:
                                if I_["first"]:
                                    cp("dve", Aq[:].bitcast(F32R), EX[:], [exk], [ak_])
                                else:
                                    tt("dve", Aq[:].bitcast(F32R), Aq[:], EX[:], ALU.add, [ak_, exk], [ak_])
                        for stq in act_ch:
                            I_ = info[stq]
                            ob_ = banks[stq][2]
                            Wb, wbk = Wbb[stq], "Wb%d" % stq
                            mm(PS[ob_][:, :], VH[:, I_["j"], :], Wb[:], I_["first"], I_["last"], [wbk] + VHK, [PSK[ob_]])
                            if I_["last"]:
                                cp("act", CT[:, h, stq * 512:(stq + 1) * 512], PS[ob_][:, :], [PSK[ob_]], [("CT", h, stq)])
                P.emit()
                ph = ph2

            with contextlib.ExitStack() as ph2b:
                ph = ph2b
                n_gdn = 8 if stage >= 3 else 0

                def T_(name, shape, dt):
                    return sb(name, shape, dt, ph2b)

                tri_le3 = T_("tri_le3", [128, 1, 128], F32)
                cp("pool", tri_le3[:, 0, :], tri_le[:], ["tri_le"], ["tri_le3"])
                identf3 = T_("identf3", [128, 1, 128], F32)
                cp("pool", identf3[:, 0, :], ident_f[:], ["ident_f"], ["identf3"])
                MI = T_("MI", [128, 4, 128], BF16)
                MS = T_("MS", [128, 4, 128], BF16)
                MI4 = T_("MI4", [128, 16, 4], BF16)
                MS4 = T_("MS4", [128, 16, 4], BF16)
                for m_, nm_, pat_, base_ in ((MI, "MI", [[0, 4], [1, 128]], 0), (MS, "MS", [[0, 4], [1, 128]], -1),
                                             (MI4, "MI4", [[0, 16], [1, 4]], 0), (MS4, "MS4", [[0, 16], [1, 4]], -1)):
                    memset("pool", m_[:], 1.0, [nm_])
                    asel(m_[:], m_[:], pat_, ALU.is_ge, base_, -1, [nm_], [nm_])
                wgab = T_("wgab", [128, 16, 16], BF16)
                GD = T_("GD", [128, 17, 8], F32)
                BT = T_("BT", [128, 17, 8], F32)
                NBm = T_("NBm", [128, 17, 8], F32)
                gtm = T_("gtm", [128, 16], F32)
                GDS = T_("GDS", [128, 16, 24], F32)
                if n_gdn:
                    load_w(wgab, "wgab", D["w_in"], 0, 16, C_GA, 16, gn[:, 0])
                    for t in range(17):
                        n = 128 if t < 16 else 64
                        for k in range(16):
                            mm(PS[0][0:n, 0:16], XT[:, k, t * 128:t * 128 + n], wgab[:, k, :], k == 0, k == 15, ["wgab"], ["ps0"])
                        cp("dve", gtm[0:n, :], PS[0][0:n, 0:16], ["ps0"], ["gtm"])
                        tt("dve", gtm[0:n, 0:8], gtm[0:n, 0:8], dtb[0:n, :], ALU.add, ["gtm", "dtb"], ["gtm"])
                        act(gtm[0:n, 0:8], gtm[0:n, 0:8], AF.Exp, ["gtm"], ["gtm"])
                        act(gtm[0:n, 0:8], gtm[0:n, 0:8], AF.Ln, ["gtm"], ["gtm"], bias=1.0)
                        tt("dve", GD[0:n, t, :], gtm[0:n, 0:8], nea[0:n, :], ALU.mult, ["gtm", "nea"], [("GD", t)])
                        act(gtm[0:n, 8:16], gtm[0:n, 8:16], AF.Exp, ["gtm"], ["gtm"], scale=-1.0)
                        ts("dve", gtm[0:n, 8:16], gtm[0:n, 8:16], 1.0, None, ALU.add, None, ["gtm"], ["gtm"])
                        P.op("dve", (lambda o, i: (lambda e: e.reciprocal(out=o, in_=i)))(BT[0:n, t, :], gtm[0:n, 8:16]), ["gtm"], [("BT", t)])
                        ts("dve", NBm[0:n, t, :], BT[0:n, t, :], -1.0, None, ALU.mult, None, [("BT", t)], [("NBm", t)])
                    dma("sp", D["scr_gab"][:, 0:8], GD[0:64, 16, :], [("GD", 16)], [("dram", "scr")])
                    dma("sp", D["scr_gab"][:, 8:16], BT[0:64, 16, :], [("BT", 16)], [("dram", "scr2")])
                    dma("sp", GDS[0:4, :, 0:16], D["scr_gab"].rearrange("(i t) c -> t i c", t=4), [("dram", "scr"), ("dram", "scr2")], ["GDS"])
                    ts("dve", GDS[0:4, :, 16:24], GDS[0:4, :, 8:16], -1.0, None, ALU.mult, None, ["GDS"], ["GDS"])
                GDK = [("GD", t) for t in range(17)]
                BTK = [("BT", t) for t in range(17)]
                NBK = [("NBm", t) for t in range(17)]
                G3 = T_("G3", [128, 512], F32)
                CB = T_("CB", [128, 512], F32)
                ECB = T_("ECB", [128, 512], F32)
                EI = T_("EI", [128, 512], BF16)
                ES = T_("ES", [128, 512], BF16)
                CC = T_("CC", [128, 16, 1], F32)
                DEX = T_("DEX", [128, 16, 1], F32)
                GCc = T_("GCc", [128, 16, 1], F32)
                BCc = T_("BCc", [128, 16, 1], F32)
                NBc = T_("NBc", [128, 16, 1], F32)
                MA = [T_("MA%d" % i, [128, 512], F32) for i in range(2)]
                MTt = [T_("MT%d" % i, [128, 512], F32) for i in range(2)]
                X32 = T_("X32", [128, 512], F32)
                XB = T_("XB", [128, 512], BF16)
                QKD = T_("QKD", [128, 512], BF16)
                KCT = T_("KCT", [128, 512], BF16)
                QDT = T_("QDT", [128, 512], BF16)
                KTM = T_("KTM", [128, 8, 128], BF16)
                VTM = T_("VTM", [128, 8, 128], BF16)
                KEND = T_("KEND", [128, 8, 128], BF16)
                RT = T_("RT", [128, 128], BF16)
                VN = T_("VN", [128, 128], BF16)
                S32 = T_("S32", [128, 128], F32)
                Sb_ = T_("Sb_", [128, 128], BF16)
                S32A = T_("S32A", [128, 16, 128], F32)
                RAW = [T_("RAW%d" % i, [128, 515], F32) for i in range(3)]
                RAWS = [T_("RAWS%d" % i, [128, 16, 7], F32) for i in range(3)]
                ACC = T_("ACC", [128, 512], F32)
                CQ = T_("CQ", [128, 512], F32)
                QTg = T_("QTg", [128, 512], BF16)
                KTg = T_("KTg", [128, 512], BF16)
                CVb = T_("CVb", [128, 512], BF16)
                ZS = T_("ZS", [128, 512], BF16)
                gco = T_("gco", [128, 128], F32)
                t48 = T_("t48", [128, 48], F32)

                def gdn_group(C, G, own, h, gsrc, bsrc, nbsrc, gkeys, mi, ms, S_of, ctc0, c0=0, chain=True):
                    T = C * G
                    nsq = {128: 6, 4: 1}[C]

                    def v3(t):
                        return t[0:C, 0:T].rearrange("p (g c) -> p g c", g=G)

                    def p3(bank):
                        return PS[bank][0:C, 0:T].rearrange("p (g c) -> p g c", g=G)

                    cp("pool", GCc[0:C, 0:G, :], gsrc, gkeys, ["GCc"])
                    cp("pool", BCc[0:C, 0:G, :], bsrc, gkeys, ["BCc"])
                    cp("pool", NBc[0:C, 0:G, :], nbsrc, gkeys, ["NBc"])
                    tt("pool", v3(G3), GCc[0:C, 0:G, :].broadcast_to([C, G, C]), tri_le3[0:C, :, 0:C].broadcast_to([C, G, C]), ALU.mult,
                       ["GCc", "tri_le3"], ["G3"])
                    mm(PS[2][:, 0:T], ones_f[0:C, :], G3[0:C, 0:T], True, True, ["G3", "ones_f"], ["ps2"])
                    mm(PS[7][0:C, 0:G], tri_le[0:C, 0:C], GCc[0:C, 0:G, 0], True, True, ["GCc", "tri_le"], ["ps7"])
                    cp("dve", CC[0:C, 0:G, 0], PS[7][0:C, 0:G], ["ps7"], ["CC"])
                    cp("dve", CB[:, 0:T], PS[2][:, 0:T], ["ps2"], ["CB"])
                    act(ECB[:, 0:T], CB[:, 0:T], AF.Exp, ["CB"], ["ECB"])
                    cb3 = CB[0:C, 0:T].rearrange("p (g c) -> p g c", g=G)
                    tt("dve", DEX[0:C, 0:G, 0], cb3[:, :, C - 1], CC[0:C, 0:G, 0], ALU.subtract, ["CB", "CC"], ["DEX"])
                    act(DEX[0:C, 0:G, :], DEX[0:C, 0:G, :], AF.Exp, ["DEX"], ["DEX"])
                    tt("dve", v3(CB), v3(CB), CC[0:C, 0:G, :].broadcast_to([C, G, C]), ALU.subtract, ["CB", "CC"], ["CB"])
                    ts("dve", v3(CB), v3(CB), 0.0, None, ALU.min, None, ["CB"], ["CB"])
                    act(v3(CB), v3(CB), AF.Exp, ["CB"], ["CB"])
                    tt("pool", v3(EI), v3(CB), mi, ALU.mult, ["CB", "MI", "MI4"], ["EI"])
                    tt("pool", v3(ES), v3(CB), ms, ALU.mult, ["CB", "MS", "MS4"], ["ES"])
                    for g in range(G):
                        mm(PS[3][0:C, g * C:(g + 1) * C], KTg[:, c0 + g * C:c0 + (g + 1) * C], KTg[:, c0 + g * C:c0 + (g + 1) * C], True, True, ["KTg"], ["ps3"])
                    tt("dve", v3(G3), p3(3), v3(ES), ALU.mult, ["ps3", "ES"], ["G3"])
                    tt("dve", v3(MTt[0]).bitcast(F32R), v3(G3), NBc[0:C, 0:G, :].broadcast_to([C, G, C]), ALU.mult,
                       ["G3", "NBc"], ["MT0"])
                    for g in range(G):
                        mm(PS[4][0:C, g * C:(g + 1) * C], KTg[:, c0 + g * C:c0 + (g + 1) * C], QTg[:, c0 + g * C:c0 + (g + 1) * C], True, True, ["KTg", "QTg"], ["ps4"])
                    tt("dve", v3(QKD), p3(4), v3(EI), ALU.mult, ["ps4", "EI"], ["QKD"])
                    tt("pool", KCT[:, 0:T], KTg[:, c0:c0 + T], ECB[:, 0:T], ALU.mult, ["KTg", "ECB"], ["KCT"])
                    tt("pool", QDT[:, 0:T], QTg[:, c0:c0 + T], ECB[:, 0:T], ALU.mult, ["QTg", "ECB"], ["QDT"])
                    for g0 in range(0, G, 8):
                        ng = min(8, G - g0)
                        for g in range(g0, g0 + ng):
                            tr(psb(6)[0:C, (g - g0) * 128:(g - g0 + 1) * 128], KTg[:, c0 + g * C:c0 + (g + 1) * C], ident_b[:, :], ["KTg", "ident_b"], ["ps6"])
                        cp("act", KTM[0:C, g0:g0 + ng, :], psb(6)[0:C, 0:ng * 128].rearrange("p (g d) -> p g d", g=ng), ["ps6"], ["KTM"])
                        for g in range(g0, g0 + ng):
                            tr(psb(7)[0:C, (g - g0) * 128:(g - g0 + 1) * 128], CVb[:, c0 + g * C:c0 + (g + 1) * C], ident_b[:, :], ["CVb", "ident_b"], ["ps7"])
                        cp("dve", VTM[0:C, g0:g0 + ng, :], psb(7)[0:C, 0:ng * 128].rearrange("p (g d) -> p g d", g=ng), ["ps7"], ["VTM"])
                    tt("pool", KEND[0:C, 0:G, :], KTM[0:C, 0:G, :], DEX[0:C, 0:G, :].broadcast_to([C, G, 128]), ALU.mult, ["KTM", "DEX"], ["KEND"])
                    RD = F32R if C == 128 else F32

                    def rv(ap):
                        return ap.bitcast(RD) if RD is F32R else ap

                    def wv(ap):
                        return ap.bitcast(F32R)

                    for g in range(G):
                        tr(PS[3][0:C, g * C:(g + 1) * C], MTt[0][0:C, g * C:(g + 1) * C], ident_f[0:C, 0:C], ["MT0", "ident_f"], ["ps3"])
                    cp("act", wv(MA[0][0:C, 0:T]), PS[3][0:C, 0:T], ["ps3"], ["MA0"])
                    tt("dve", wv(v3(X32)), v3(MTt[0]), identf3[0:C, :, 0:C].broadcast_to([C, G, C]), ALU.add, ["MT0", "identf3"], ["X32"])
                    cur = 0
                    for r in range(nsq):
                        M, Mk = MA[cur], "MA%d" % cur
                        Mt, Mtk = MTt[cur], "MT%d" % cur
                        Mn, Mnk = MA[1 - cur], "MA%d" % (1 - cur)
                        Mtn, Mtnk = MTt[1 - cur], "MT%d" % (1 - cur)
                        for g in range(G):
                            sl = slice(g * C, (g + 1) * C)
                            mm(PS[3][0:C, sl], rv(Mt[0:C, sl]), rv(M[0:C, sl]), True, True, [Mk, Mtk], ["ps3"])
                        if r < nsq - 1:
                            for g in range(G):
                                sl = slice(g * C, (g + 1) * C)
                                mm(PS[4][0:C, sl], rv(M[0:C, sl]), rv(Mt[0:C, sl]), True, True, [Mk, Mtk], ["ps4"])
                        cp("act", wv(Mn[0:C, 0:T]), PS[3][0:C, 0:T], ["ps3"], [Mnk])
                        if r < nsq - 1:
                            cp("dve", wv(Mtn[0:C, 0:T]), PS[4][0:C, 0:T], ["ps4"], [Mtnk])
                        for g in range(G):
                            sl = slice(g * C, (g + 1) * C)
                            mm(PS[5][0:C, sl], rv(Mn[0:C, sl]), rv(X32[0:C, sl]), True, True, [Mnk, "X32"], ["ps5"])
                        tt("dve", wv(X32[0:C, 0:T]), X32[0:C, 0:T], PS[5][0:C, 0:T], ALU.add, ["X32", "ps5"], ["X32"])
                        cur = 1 - cur
                    cp("act", XB[0:C, 0:T], X32[0:C, 0:T], ["X32"], ["XB"])
                    for g in range(G):
                        S32g, Sbg, skey, sbkey = S_of(g)
                        sl = slice(g * C, (g + 1) * C)
                        if not chain:
                            cp("act", Sbg, S32g, [skey], [sbkey])
                        kb = pa_next()
                        mm(PS[kb][0:C, 0:128], KCT[:, sl], Sbg, True, True, ["KCT", sbkey], [PSK[kb]])
                        tt("dve", RT[0:C, :], VTM[0:C, g, :], PS[kb][0:C, 0:128], ALU.subtract, ["VTM", PSK[kb]], ["RT"])
                        kb2 = pa_next()
                        mm(PS[kb2][0:C, 0:128], XB[0:C, sl], RT[0:C, :], True, True, ["XB", "RT"], [PSK[kb2]])
                        ts("dve", VN[0:C, :], PS[kb2][0:C, 0:128], BCc[0:C, g, :], None, ALU.mult, None, [PSK[kb2], "BCc"], ["VN"])
                        if own:
                            mm(PS[6][:, sl], Sbg, QDT[:, sl], True, False, [sbkey, "QDT"], ["ps6"])
                            mm(PS[6][:, sl], VN[0:C, :], QKD[0:C, sl], False, True, ["VN", "QKD"], ["ps6"])
                        mm(PS[7][:, 0:128], KEND[0:C, g, :], VN[0:C, :], True, True, ["KEND", "VN"], ["ps7"])
                        stt(S32g, S32g, ECB[:, g * C + C - 1:g * C + C], PS[7][:, 0:128], ALU.mult, ALU.add, [skey, "ECB", "ps7"], [skey])
                        if chain:
                            cp("act", Sbg, S32g, [skey], [sbkey])
                    if own:
                        fm_norm("ps6", PS[6][:, 0:T], T, hg[:, 4:5], ACC[:, 0:T], ["ACC"], st2, ["hg"])
                        tt("pool", CT[:, 8 + h, ctc0:ctc0 + T], ACC[:, 0:T], ZS[:, c0:c0 + T], ALU.mult, ["ACC", "ZS"], [("CT", 8 + h, ctc0)])

                def conv_silu(flat_in, L, cidx, out_ap_fn, key_in):
                    ts("dve", ACC[:, 0:L], flat_in[:, 3:3 + L], CW[:, cidx, 3:4], None, ALU.mult, None, [key_in, "CW"], ["ACC"])
                    for j in (2, 1, 0):
                        stt(ACC[:, 0:L], flat_in[:, j:j + L], CW[:, cidx, j:j + 1], ACC[:, 0:L], ALU.mult, ALU.add, [key_in, "CW", "ACC"], ["ACC"])

                for h in range(n_gdn):
                    ws = []
                    for c0_ in (C_GQ, C_GK, C_GV, C_GZ):
                        w_, wk_ = wb_next()
                        load_w(w_, wk_, D["w_in"], 0, 16, c0_ + h * 128, 128, gn[:, 0])
                        ws.append((w_, wk_))
                    memset("pool", S32[:], 0.0, ["S"])
                    memset("pool", Sb_[:], 0.0, [("S", "b")])
                    for i in range(3):
                        memset("pool", RAW[i][:, 0:3], 0.0, ["RAW%d" % i])
                    for grp in range(4):
                        own = grp >= 2
                        xc0 = grp * 512
                        if grp == 2:
                            for i in range(3):
                                ts("dve", RAW[i][:, 0:3], RAW[i][:, 0:3], flg[:, 0:1], None, ALU.mult, None, ["RAW%d" % i, "flg"], ["RAW%d" % i])
                            ts("dve", S32[:], S32[:], flg[:, 0:1], None, ALU.mult, None, ["S", "flg"], ["S"])
                            cp("act", Sb_[:], S32[:], ["S"], [("S", "b")])
                        for comp in range(3):
                            w_, wk_ = ws[comp]
                            rk = "RAW%d" % comp
                            bk = pa_next()
                            proj_fm(bk, w_, wk_, xc0, 512)
                            cp("act", RAW[comp][:, 3:515], PS[bk][:, 0:512], [PSK[bk]], [rk])
                            cidx = comp * 8 + h
                            if grp == 3:
                                tr(PS[7][0:3, 0:128], RAW[comp][:, 512:515], ident_f[:], [rk, "ident_f"], ["ps7"])
                                cp("dve", gco[0:3, :], PS[7][0:3, 0:128], ["ps7"], ["gco"])
                                dma("sp", D["gcp"][:, cidx * 128:(cidx + 1) * 128], gco[0:3, :], ["gco"], [])
                            conv_silu(RAW[comp], 512, cidx, None, rk)
                            cp("pool", RAW[comp][:, 0:3], RAW[comp][:, 512:515], [rk], [rk])
                            if comp == 0:
                                act(CQ[:, :], ACC[:, :], AF.Silu, ["ACC"], ["CQ"])
                                fm_norm("CQ", CQ[:, :], 512, 128.0 ** -0.5, QTg[:, :], ["QTg"], st2, mean=False)
                            elif comp == 1:
                                act(CQ[:, :], ACC[:, :], AF.Silu, ["ACC"], ["CQ"])
                                fm_norm("CQ", CQ[:, :], 512, 1.0, KTg[:, :], ["KTg"], st2, mean=False)
                            else:
                                act(CVb[:, :], ACC[:, :], AF.Silu, ["ACC"], ["CVb"])
                        if own:
                            bk = pa_next()
                            proj_fm(bk, ws[3][0], ws[3][1], xc0, 512)
                            act(ZS[:, :], PS[bk][:, 0:512], AF.Silu, [PSK[bk]], ["ZS"])
                        t0 = grp * 4
                        gdn_group(128, 4, own, h, GD[:, t0:t0 + 4, h:h + 1], BT[:, t0:t0 + 4, h:h + 1], NBm[:, t0:t0 + 4, h:h + 1],
                                  GDK + BTK + NBK, MI[:, :, :], MS[:, :, :], lambda g: (S32[:], Sb_[:], "S", ("S", "b")), (grp - 2) * 512)
                    dma("sp", D["gsp"][h], S32[:], ["S"], [])
                    dma("sp", S32A[:], D["sg"][:, h].rearrange("i d e -> d i e"), [], [("SA", g) for g in range(16)])
                    for comp in range(3):
                        w_, wk_ = ws[comp]
                        rk = "RAWS%d" % comp
                        cidx = comp * 8 + h
                        bk = pa_next()
                        proj_fm(bk, w_, wk_, SMP0, 64)
                        cp("act", RAWS[comp][:, :, 3:7], PS[bk][:, 0:64].rearrange("p (i t) -> p i t", t=4), [PSK[bk]], [rk])
                        cp("dve", RAWS[comp][:, :, 0:3], SCT[:, cidx, :].rearrange("p (i r) -> p i r", r=3), ["SCT"], [rk])
                        cp("pool", t48[:, :].rearrange("p (i r) -> p i r", r=3), RAWS[comp][:, :, 4:7], [rk], ["t48"])
                        tr(PS[7][0:48, 0:128], t48[:, :], ident_f[:], ["t48", "ident_f"], ["ps7"])
                        cp("dve", gco[0:48, :], PS[7][0:48, 0:128], ["ps7"], ["gco"])
                        dma("sp", D["gcs"][:, cidx * 128:(cidx + 1) * 128], gco[0:48, :], ["gco"], [])
                        flat = RAWS[comp][:, :, :].rearrange("p i s -> p (i s)")
                        conv_silu(flat, 109, cidx, None, rk)
                        accv = ACC[:, 0:112].rearrange("p (i s) -> p i s", s=7)[:, :, 0:4]
                        if comp == 0:
                            act(CQ[:, 0:64].rearrange("p (i t) -> p i t", t=4), accv, AF.Silu, ["ACC"], ["CQ"])
                            fm_norm("CQ", CQ[:, 0:64], 64, 128.0 ** -0.5, QTg[:, 0:64], ["QTg"], st2, mean=False)
                        elif comp == 1:
                            act(CQ[:, 0:64].rearrange("p (i t) -> p i t", t=4), accv, AF.Silu, ["ACC"], ["CQ"])
                            fm_norm("CQ", CQ[:, 0:64], 64, 1.0, KTg[:, 0:64], ["KTg"], st2, mean=False)
                        else:
                            act(CVb[:, 0:64].rearrange("p (i t) -> p i t", t=4), accv, AF.Silu, ["ACC"], ["CVb"])
                    bk = pa_next()
                    proj_fm(bk, ws[3][0], ws[3][1], SMP0, 64)
                    act(ZS[:, 0:64], PS[bk][:, 0:64], AF.Silu, [PSK[bk]], ["ZS"])
                    for i0 in (0, 8):
                        gdn_group(4, 8, True, h, GDS[0:4, i0:i0 + 8, h:h + 1], GDS[0:4, i0:i0 + 8, 8 + h:9 + h],
                                  GDS[0:4, i0:i0 + 8, 16 + h:17 + h], ["GDS"], MI4[0:4, 0:8, :], MS4[0:4, 0:8, :],
                                  (lambda i0_: (lambda g: (S32A[:, i0_ + g, :], Sb_[:], ("SA", i0_ + g), ("S", "b"))))(i0), 1024 + i0 * 4,
                                  c0=i0 * 4, chain=False)
                    dma("sp", D["gss"][:, h].rearrange("i d e -> d i e"), S32A[:], [("SA", g) for g in range(16)], [])
                P.emit()

        with contextlib.ExitStack() as ph2c:
            n_ssb = 16 if stage >= 4 else 0

            def U_(name, shape, dt):
                return sb(name, shape, dt, ph2c)

            ptb = U_("ptb", [128, 256], I32)
            IDX = U_("IDX", [128, 256], I32)
            pid = U_("pid", [128, 1], F32)
            BH512 = U_("BH512", [128, 16, 8, 4], F32)
            MN = U_("MN", [128, 8, 4], F32)
            KPb = [U_("KPb%d" % i, [128, 1024], BF16) for i in range(2)]
            KTall = U_("KTall", [128, 16, 8, 128], BF16)
            VBall = U_("VBall", [128, 16, 1024], BF16)
            VNb = U_("VNb", [128, 1024], BF16)
            ZP = U_("ZP", [128, 512], F32)
            SPt = U_("SPt", [128, 512], F32)
            TOa = U_("TOa", [128, 17, 32], F32)
            TOb = U_("TOb", [128, 17, 32], F32)
            Wsb = U_("Wsb", [128, 512], BF16)
            OACC = U_("OACC", [128, 32], F32)
            if n_ssb:
                dma("sp", ptb[:], D["pt"][0:1, :].partition_broadcast(128), [], ["ptb"])
                P.op("pool", lambda e: e.iota(pid[:], pattern=[[0, 1]], base=0, channel_multiplier=1, allow_small_or_imprecise_dtypes=True), [], ["pid"])
                ptf = SPt[:, 0:256]
                cp("dve", ptf, ptb[:], ["ptb"], ["SPt"])
                ts("dve", ptf, ptf, 128.0, pid[:, 0:1], ALU.mult, ALU.add, ["SPt", "pid"], ["SPt"])
                cp("dve", IDX[:], ptf, ["SPt"], ["IDX"])
                for j in range(16):
                    cp("dve", BH512[:, j, :, :], sbb[:, :].rearrange("p (h o) -> p h o", o=1).broadcast_to([128, 8, 4]), ["sbb"], ["BH512"])
                memset("pool", MN[:], 1.0, ["MN"])
                asel(MN[:], MN[:], [[0, 8], [1, 4]], ALU.is_ge, -1, -1, ["MN"], ["MN"])
                memset("pool", TOa[:], 0.0, ["TOa"])
                memset("pool", TOb[:], 0.0, ["TOb"])
            BHf = BH512[:].rearrange("p j h t -> p (j h t)")
            MN2 = MN[:].rearrange("p h t -> p (h t)")

            for i in range(n_ssb):
                for j in range(16):
                    b = j % 2
                    col = i * 16 + j
                    P.op("pool", (lambda o, ix: (lambda e: e.indirect_dma_start(out=o, out_offset=None, in_=D["ck"],
                         in_offset=bass.IndirectOffsetOnAxis(ap=ix, axis=0))))(KPb[b][:, :], IDX[:, col:col + 1]), ["IDX"], ["KPb%d" % b], dma=True)
                    P.op("pool", (lambda o, ix: (lambda e: e.indirect_dma_start(out=o, out_offset=None, in_=D["cv"],
                         in_offset=bass.IndirectOffsetOnAxis(ap=ix, axis=0))))(VBall[:, j, :], IDX[:, col:col + 1]), ["IDX"], [("VB", j)], dma=True)
                    bank = j % 2
                    pv = psb(bank)
                    for h in range(8):
                        tr(pv[:, h * 128:(h + 1) * 128], KPb[b][:, h * 128:(h + 1) * 128], ident_b[:], ["KPb%d" % b, "ident_b"], [PSK[bank]])
                    cp("act" if j % 2 == 0 else "dve", KTall[:, j, :, :], pv[:, 0:1024].rearrange("p (h s) -> p h s", h=8), [PSK[bank]], [("KT", j)])
                dma("pool", VNb[0:4, :], D["svs"][4 * i:4 * i + 4, :], [], ["VNb"])
                for h in range(8):
                    mm(PS[2][0:4, h * 4:(h + 1) * 4], KTS[:, h, 4 * i:4 * i + 4], QTS[:, h, 4 * i:4 * i + 4], True, True, [], ["ps2"])
                stt(ZP[0:4, 0:32], PS[2][0:4, 0:32], SCALE, BHf[0:4, 0:32], ALU.mult, ALU.add, ["ps2", "BH512"], ["ZP"])
                act(SPt[0:4, 0:32], ZP[0:4, 0:32], AF.Exp, ["ZP"], ["SPt"])
                act(SPt[0:4, 0:32], SPt[0:4, 0:32], AF.Ln, ["SPt"], ["SPt"], bias=1.0)
                tt("dve", SPt[0:4, 0:32], SPt[0:4, 0:32], MN2[0:4, :], ALU.mult, ["SPt", "MN"], ["SPt"])
                mm(PS[3][0:4, 0:32], tri_gt[0:4, 0:4], SPt[0:4, 0:32], True, True, ["SPt", "tri_gt"], ["ps3"])
                mm(PS[4][:, 0:32], ones_f[0:4, :], SPt[0:4, 0:32], True, True, ["SPt", "ones_f"], ["ps4"])
                tt("dve", ZP[0:4, 0:32], ZP[0:4, 0:32], SPt[0:4, 0:32], ALU.subtract, ["ZP", "SPt"], ["ZP"])
                tt("dve", ZP[0:4, 0:32], ZP[0:4, 0:32], PS[3][0:4, 0:32], ALU.subtract, ["ZP", "ps3"], ["ZP"])
                cp("dve", TOa[:, 16, :], PS[4][:, 0:32], ["ps4"], ["TOa"])
                act(Wsb[0:4, 0:32], ZP[0:4, 0:32], AF.Exp, ["ZP"], ["Wsb"])
                tt("dve", Wsb[0:4, 0:32], Wsb[0:4, 0:32], MN2[0:4, :], ALU.mult, ["Wsb", "MN"], ["Wsb"])
                for h in range(8):
                    mm(PS[5][:, h * 4:(h + 1) * 4], VNb[0:4, h * 128:(h + 1) * 128], Wsb[0:4, h * 4:(h + 1) * 4], True, True, ["Wsb", "VNb"], ["ps5"])
                cp("dve", OACC[:], PS[5][:, 0:32], ["ps5"], ["OACC"])
                KTK_ = [("KT", j) for j in range(16)]
                for j in range(16):
                    for h in range(8):
                        c = (j * 8 + h) * 4
                        mm(PS[2][:, c:c + 4], KTall[:, j, h, :], QTS[:, h, 4 * i:4 * i + 4], True, True, KTK_, ["ps2"])
                stt(ZP[:, :], PS[2][:, :], SCALE, BHf[:, :], ALU.mult, ALU.add, ["ps2", "BH512"], ["ZP"])
                act(SPt[:, :], ZP[:, :], AF.Exp, ["ZP"], ["SPt"])
                act(SPt[:, :], SPt[:, :], AF.Ln, ["SPt"], ["SPt"], bias=1.0)
                mm(PS[3][:, :], tri_gt[:], SPt[:, :], True, True, ["SPt", "tri_gt"], ["ps3"])
                mm(PS[4][:, :], ones_f[:], SPt[:, :], True, True, ["SPt", "ones_f"], ["ps4"])
                tt("dve", ZP[:, :], ZP[:, :], SPt[:, :], ALU.subtract, ["ZP", "SPt"], ["ZP"])
                tt("dve", ZP[:, :], ZP[:, :], PS[3][:, :], ALU.subtract, ["ZP", "ps3"], ["ZP"])
                cp("dve", TOa[:, 0:16, :], PS[4][:, :].rearrange("p (j c) -> p j c", j=16), ["ps4"], ["TOa"])
                src_, sk_, dst_, dk_ = TOa, "TOa", TOb, "TOb"
                for sh in (1, 2, 4, 8, 16):
                    n_ = 17 - sh
                    tt("dve", dst_[:, 0:n_, :], src_[:, 0:n_, :], src_[:, sh:17, :], ALU.add, [sk_], [dk_])
                    cp("dve", dst_[:, n_:17, :], src_[:, n_:17, :], [sk_], [dk_])
                    src_, sk_, dst_, dk_ = dst_, dk_, src_, sk_
                tt("dve", ZP[:, :].rearrange("p (j c) -> p j c", j=16), ZP[:, :].rearrange("p (j c) -> p j c", j=16), src_[:, 1:17, :],
                   ALU.subtract, ["ZP", sk_], ["ZP"])
                act(Wsb[:, :], ZP[:, :], AF.Exp, ["ZP"], ["Wsb"])
                VBK_ = [("VB", j) for j in range(16)]
                for h in range(8):
                    for j in range(16):
                        c = (j * 8 + h) * 4
                        mm(PS[5][:, h * 4:(h + 1) * 4], VBall[:, j, h * 128:(h + 1) * 128], Wsb[:, c:c + 4], j == 0, j == 15, ["Wsb"] + VBK_, ["ps5"])
                tt("dve", OACC[:], OACC[:], PS[5][:, 0:32], ALU.add, ["OACC", "ps5"], ["OACC"])
                cp("act", CT[:, 0:8, 1024 + 4 * i:1024 + 4 * i + 4], OACC[:].rearrange("p (h t) -> p h t", t=4), ["OACC"], [("CTs", i)])
            P.emit()

        if stage >= 5:
            X1 = BIG.bitcast(F32).reshape([128, 8, 2112])
            with contextlib.ExitStack() as ph3:
                X1s = sb("X1s", [128, 2048], F32, ph3)

                def x1_tile(t):
                    if t < 8:
                        return X1[:, t, 0:2048], 128, ("X1", t)
                    return X1s[0:64, :], 64, ("X1", 8)

                def ct_cols(t):
                    return (t * 128, 128) if t < 8 else (1024, 64)

                with contextlib.ExitStack() as pa3:
                    WO = [sb("WO%d" % i, [128, 16, 512], BF16, pa3) for i in range(2)]
                    xres = [sb("xres%d" % i, [128, 512], F32, pa3) for i in range(2)]
                    cnt = 0
                    for n in range(4):
                        wo, wok = WO[n % 2], "WO%d" % (n % 2)
                        load_w(wo, wok, D["w_out"], 0, 16, n * 512, 512, None)
                        for t in range(9):
                            xa, nn, xk = x1_tile(t)
                            c0, _ = ct_cols(t)
                            xr, xrk = xres[cnt % 2], "xres%d" % (cnt % 2)
                            bk = cnt % 2
                            cnt += 1
                            src = D["xp"][1024 + t * 128:1024 + (t + 1) * 128, n * 512:(n + 1) * 512] if t < 8 else D["xs"][:, n * 512:(n + 1) * 512]
                            dma("sp", xr[0:nn, :], src, [], [xrk])
                            for k in range(16):
                                mm(PS[bk][0:nn, :], CT[:, k, c0:c0 + nn], wo[:, k, :], k == 0, k == 15, [wok], [PSK[bk]])
                            tt("dve", xa[:, n * 512:(n + 1) * 512], PS[bk][0:nn, :], xr[0:nn, :], ALU.add, [PSK[bk], xrk], [xk + (n,)])
                    P.emit()

                def norm_to_ct(st):
                    for t in range(9):
                        xa, nn, xk = x1_tile(t)
                        c0, _ = ct_cols(t)
                        junk, ss, xb = st["junk"], st["ss"], st["xb"]
                        act(xb[0:nn, :], xa, AF.Square, [xk], ["xb", "ss"], accum=ss[0:nn, :])
                        act(ss[0:nn, :], ss[0:nn, :], AF.Ln, ["ss"], ["ss"], bias=EPS, scale=1.0 / 2048)
                        act(ss[0:nn, :], ss[0:nn, :], AF.Exp, ["ss"], ["ss"], scale=-0.5)
                        ts("dve", xb[0:nn, :], xa, ss[0:nn, 0:1], None, ALU.mult, None, [xk, "ss"], ["xb"])
                        for half in range(2):
                            bank = 6 + half
                            pv = psb(bank)
                            for k in range(8):
                                kk = half * 8 + k
                                tr(pv[:, k * 128:k * 128 + nn], xb[0:nn, kk * 128:(kk + 1) * 128], ident_b[0:nn, 0:nn], ["xb", "ident_b"], [PSK[bank]])
                            srcv = pv[:, 0:1024].rearrange("p (a b) -> p a b", a=8)[:, :, 0:nn]
                            cp("act" if half == 0 else "dve", CT[:, half * 8:half * 8 + 8, c0:c0 + nn], srcv, [PSK[bank]], [("CT", t, half)])

                CTK = [("CT", t, hf_) for t in range(9) for hf_ in range(2)]
                with contextlib.ExitStack() as pb3:
                    st3 = {"junk": None, "ss": sb("ss3", [128, 1], F32, pb3), "xb": sb("xb3", [128, 2048], BF16, pb3),
                           "sq": sb("sq3", [128, 512], BF16, pb3), "rr": sb("rr3", [128, 512], F32, pb3)}
                    XQ = sb("XQ", [128, 4, NTO], BF16, pb3)
                    XO = sb("XO", [128, 4, NTO], BF16, pb3)
                    wxq = [sb("wxq%d" % i, [128, 16, 128], BF16, pb3) for i in range(2)]
                    WXO = sb("WXO", [128, 4, 2048], BF16, pb3)
                    EM = [sb("EM%d" % i, [128, 512], BF16, pb3) for i in range(2)]
                    rden = sb("rden", [128, 512], F32, pb3)
                    CK = [sb("CK%d" % i, [128, 2, 512], F32, pb3) for i in range(2)]
                    CV = [sb("CVm%d" % i, [128, 2, 512], F32, pb3) for i in range(2)]
                    KTm = sb("KTm", [128, 4, 256], BF16, pb3)
                    Vbm = sb("Vbm", [128, 2, 512], BF16, pb3)
                    Es = sb("Es", [128, 32], BF16, pb3)
                    norm_to_ct(st3)
                    for n_ in range(4):
                        load_w(WXO, "WXO", D["w_xo"], 0, 4, n_ * 512, 512, None, dcol0=n_ * 512)
                    for h in range(4):
                        w_, wk_ = wxq[h % 2], "wxq%d" % (h % 2)
                        load_w(w_, wk_, D["w_xq"], 0, 16, h * 128, 128, gn[:, 1])
                        for c0, T in ((0, 512), (512, 512), (1024, 64)):
                            bk = pa_next()
                            for k in range(16):
                                mm(PS[bk][:, 0:T], w_[:, k, :], CT[:, k, c0:c0 + T], k == 0, k == 15, [wk_] + CTK, [PSK[bk]])
                            fm_norm(PSK[bk], PS[bk][:, 0:T], T, hg[:, 2:3], XQ[:, h, c0:c0 + T], [("XQ", h, c0)], st3, ["hg"])
                        for g in range(2):
                            c0 = g * 512
                            for mt in range(2):
                                mm(PS[2 + mt][:, :], MKT[:, h, mt * 128:(mt + 1) * 128], XQ[:, h, c0:c0 + 512], True, True, [("XQ", h, c0)], [PSK[2 + mt]])
                                act(EM[mt][:, :], PS[2 + mt][:, :], AF.Exp, [PSK[2 + mt]], ["EM%d" % mt], scale=SCALE)
                            for mt in range(2):
                                mm(PS[4][:, :], ones_b[:], EM[mt][:, :], mt == 0, mt == 1, ["EM%d" % mt, "ones_b"], ["ps4"])
                            for mt in range(2):
                                mm(PS[5][:, :], MV[:, mt, h * 128:(h + 1) * 128], EM[mt][:, :], mt == 0, mt == 1, ["EM%d" % mt], ["ps5"])
                            P.op("dve", (lambda o, i_: (lambda e: e.reciprocal(out=o, in_=i_)))(rden[:, :], PS[4][:, :]), ["ps4"], ["rden"])
                            tt("dve", XO[:, h, c0:c0 + 512], PS[5][:, :], rden[:, :], ALU.mult, ["ps5", "rden"], [("XO", h, c0)])
                    XQK = [("XQ", h, 1024) for h in range(4)]
                    for i in range(16):
                        ck_, ckk = CK[i % 2], "CK%d" % (i % 2)
                        cv_, cvk = CV[i % 2], "CVm%d" % (i % 2)
                        dma("sp", ck_[:], D["cmk"][i].rearrange("(t p) c -> p t c", p=128), [], [ckk])
                        dma("sp", cv_[:], D["cmv"][i].rearrange("(t p) c -> p t c", p=128), [], [cvk])
                        for mt in range(2):
                            for h in range(4):
                                tr(PS[mt][:, h * 128:(h + 1) * 128], ck_[:, mt, h * 128:(h + 1) * 128], ident_f[:], [ckk, "ident_f"], [PSK[mt]])
                            cp("act" if mt == 0 else "dve", KTm[:, :, mt * 128:(mt + 1) * 128], PS[mt][:, :].rearrange("p (h m) -> p h m", h=4),
                               [PSK[mt]], [("KTm", mt)])
                        cp("pool", Vbm[:], cv_[:], [cvk], ["Vbm"])
                        for mt in range(2):
                            for h in range(4):
                                mm(PS[2][:, mt * 16 + h * 4:mt * 16 + h * 4 + 4], KTm[:, h, mt * 128:(mt + 1) * 128], XQ[:, h, 1024 + 4 * i:1024 + 4 * i + 4],
                                   True, True, [("KTm", 0), ("KTm", 1)] + XQK, ["ps2"])
                        act(Es[:, :], PS[2][:, 0:32], AF.Exp, ["ps2"], ["Es"], scale=SCALE)
                        for mt in range(2):
                            mm(PS[4][:, 0:16], ones_b[:], Es[:, mt * 16:(mt + 1) * 16], mt == 0, mt == 1, ["Es", "ones_b"], ["ps4"])
                        for h in range(4):
                            for mt in range(2):
                                mm(PS[5][:, h * 4:(h + 1) * 4], Vbm[:, mt, h * 128:(h + 1) * 128], Es[:, mt * 16 + h * 4:mt * 16 + h * 4 + 4],
                                   mt == 0, mt == 1, ["Es", "Vbm"], ["ps5"])
                        P.op("dve", (lambda o, i_: (lambda e: e.reciprocal(out=o, in_=i_)))(rden[:, 0:16], PS[4][:, 0:16]), ["ps4"], ["rden"])
                        tt("dve", XO[:, :, 1024 + 4 * i:1024 + 4 * i + 4], PS[5][:, 0:16].rearrange("p (h t) -> p h t", t=4),
                           rden[:, 0:16].rearrange("p (h t) -> p h t", t=4), ALU.mult, ["ps5", "rden"], [("XOs", i)])
                    XOK = [("XO", h, c0) for h in range(4) for c0 in (0, 512)] + [("XOs", i) for i in range(16)]
                    for t in range(9):
                        xa, nn, xk = x1_tile(t)
                        c0, _ = ct_cols(t)
                        for n in range(4):
                            bk = pa_next()
                            for k in range(4):
                                mm(PS[bk][0:nn, :], XO[:, k, c0:c0 + nn], WXO[:, k, n * 512:(n + 1) * 512], k == 0, k == 3, ["WXO"] + XOK, [PSK[bk]])
                            tt("dve", xa[:, n * 512:(n + 1) * 512], xa[:, n * 512:(n + 1) * 512], PS[bk][0:nn, :], ALU.add, [PSK[bk], xk], [xk])
                    P.emit()

                with contextlib.ExitStack() as pc3:
                    st4 = {"junk": None, "ss": sb("ss4", [128, 1], F32, pc3), "xb": sb("xb4", [128, 2048], BF16, pc3)}
                    norm_to_ct(st4)
                    P.emit()
                with contextlib.ExitStack() as pc3:
                    HT = sb("HT", [128, 44, 576], BF16, pc3)
                    FH = 5632
                    for half in range(2):
                        segs = ((0, 512, 0), (1024, 64, 512)) if half == 0 else ((512, 512, 0),)
                        with contextlib.ExitStack() as pg3:
                            WG = [sb("WG%d_%d" % (i, half), [128, 16, 128], BF16, pg3) for i in range(4)]
                            SG = [sb("SG%d_%d" % (i, half), [128, 576], F32, pg3) for i in range(2)]
                            for hc in range(44):
                                wg, wgk = WG[(2 * hc) % 4], "WG%d" % ((2 * hc) % 4)
                                wu, wuk = WG[(2 * hc + 1) % 4], "WG%d" % ((2 * hc + 1) % 4)
                                load_w(wg, wgk, D["w_gate_up"], 0, 16, hc * 128, 128, gn[:, 2])
                                load_w(wu, wuk, D["w_gate_up"], 0, 16, FH + hc * 128, 128, gn[:, 2])
                                sg, sgk = SG[hc % 2], "SG%d" % (hc % 2)
                                pb_ = (hc % 2) * 4
                                for si_, (c0, T, o0) in enumerate(segs):
                                    bg, bu = pb_ + 2 * si_, pb_ + 2 * si_ + 1
                                    for k in range(16):
                                        mm(PS[bg][:, 0:T], wg[:, k, :], CT[:, k, c0:c0 + T], k == 0, k == 15, [wgk], [PSK[bg]])
                                    for k in range(16):
                                        mm(PS[bu][:, 0:T], wu[:, k, :], CT[:, k, c0:c0 + T], k == 0, k == 15, [wuk], [PSK[bu]])
                                    act(sg[:, o0:o0 + T], PS[bg][:, 0:T], AF.Silu, [PSK[bg]], [(sgk, o0)])
                                    tt("dve", HT[:, hc, o0:o0 + T], PS[bu][:, 0:T], sg[:, o0:o0 + T], ALU.mult, [PSK[bu], (sgk, o0)], [("HT", hc, o0)])
                            P.emit()
                        with contextlib.ExitStack() as pd3:
                            WD = [sb("WD%d_%d" % (i, half), [128, 44, 128], BF16, pd3) for i in range(2)]
                            YT = [sb("YT%d_%d" % (i, half), [128, 576], F32, pd3) for i in range(2)]
                            for ocb in range(16):
                                wd, wdk = WD[ocb % 2], "WD%d" % (ocb % 2)
                                load_w(wd, wdk, D["w_down"], 0, 44, ocb * 128, 128, None)
                                yt, ytk = YT[ocb % 2], "YT%d" % (ocb % 2)
                                b0 = (ocb % 2) * 2
                                for si_, (c0, T, o0) in enumerate(segs):
                                    bk = b0 + si_
                                    for k in range(44):
                                        mm(PS[bk][:, 0:T], wd[:, k, :], HT[:, k, o0:o0 + T], k == 0, k == 43, [wdk], [PSK[bk]])
                                    cp("act", yt[:, o0:o0 + T], PS[bk][:, 0:T], [PSK[bk]], [(ytk, o0)])
                                tb = 4 + (ocb % 2)
                                for j in range(4):
                                    tr(PS[tb][:, j * 128:(j + 1) * 128], yt[:, j * 128:(j + 1) * 128], ident_f[:], [(ytk, 0), "ident_f"], [PSK[tb]])
                                for j in range(4):
                                    t = half * 4 + j
                                    xs_ = X1[:, t, ocb * 128:(ocb + 1) * 128]
                                    tt("dve", xs_, xs_, PS[tb][:, j * 128:(j + 1) * 128], ALU.add, [PSK[tb], ("X1", t)], [("X1", t)])
                                if half == 0:
                                    tr(PS[6 + (ocb % 2)][0:64, 0:128], yt[:, 512:576], ident_f[:], [(ytk, 512), "ident_f"], [PSK[6 + (ocb % 2)]])
                                    xs_ = X1s[0:64, ocb * 128:(ocb + 1) * 128]
                                    tt("dve", xs_, xs_, PS[6 + (ocb % 2)][0:64, 0:128], ALU.add, [PSK[6 + (ocb % 2)], ("X1", 8)], [("X1", 8)])
                            for j in range(4):
                                t = half * 4 + j
                                dma("sp", D["yp"][t * 128:(t + 1) * 128, :], X1[:, t, 0:2048], [("X1", t)], [])
                            P.emit()
                    dma("sp", D["ys"][:, :], X1s[0:64, :], [("X1", 8)], [])
                    P.emit()

        P.emit(final=True)
    return nc


def _prep_inputs(inp):
    n_phys = inp["cache_sb_k"].shape[1]
    ck = np.ascontiguousarray(inp["cache_sb_k"][0]).reshape(n_phys * 128, 1024)
    cv = np.ascontiguousarray(inp["cache_sb_v"][0]).reshape(n_phys * 128, 1024)
    maps = []
    for c in range(8):
        b, hf = c // 2, c % 2
        xp = np.concatenate([inp["x_prompt"][b, 0:1024], inp["x_prompt"][b, hf * 1024:hf * 1024 + 1024]], axis=0)
        m = {
            "xp": np.ascontiguousarray(xp, dtype=np.float32),
            "xs": np.ascontiguousarray(inp["x_sample"][16 * c:16 * c + 16]).reshape(64, 2048),
            "mem": np.ascontiguousarray(inp["mem_prompt"][b]),
            "ck": ck, "cv": cv,
            "pt": np.ascontiguousarray(inp["page_table"][16 * c:16 * c + 16]).reshape(1, 256).astype(np.int32),
            "sg": np.ascontiguousarray(inp["state_gdn"][0, 16 * c:16 * c + 16]),
            "sc": np.ascontiguousarray(inp["state_gdn_conv"][0, 16 * c:16 * c + 16]).reshape(48, 3072),
            "cmk": np.ascontiguousarray(inp["cache_mem_k"][0, 16 * c:16 * c + 16]).reshape(16, 256, 512),
            "cmv": np.ascontiguousarray(inp["cache_mem_v"][0, 16 * c:16 * c + 16]).reshape(16, 256, 512),
            "flags": np.array([[float(hf), 0.0 if hf else NEG, 0.0, 0.0]], np.float32),
        }
        for n, s in W_NAMES:
            m[n] = np.ascontiguousarray(inp[n][0]).reshape(s)
        maps.append(m)
    return n_phys, maps


def _assemble(res):
    R = res.results
    f = np.float32
    yp = np.zeros((4, 2048, 2048), f); skp = np.zeros((1, 4, 2048, 8, 128), f); svp = np.zeros((1, 4, 2048, 8, 128), f)
    ys = np.zeros((128, 4, 2048), f); sks = np.zeros((1, 128, 4, 8, 128), f); svs = np.zeros((1, 128, 4, 8, 128), f)
    gsp = np.zeros((1, 4, 8, 128, 128), f); gcp = np.zeros((1, 4, 3, 3072), f)
    gss = np.zeros((1, 128, 8, 128, 128), f); gcs = np.zeros((1, 128, 3, 3072), f)
    mkp = np.zeros((1, 4, 256, 4, 128), f); mvp = np.zeros((1, 4, 256, 4, 128), f)
    for c in range(8):
        b, hf = c // 2, c % 2
        r = R[c]
        sl = slice(hf * 1024, hf * 1024 + 1024)
        yp[b, sl] = r["yp"]
        skp[0, b, sl] = r["skp"].reshape(1024, 8, 128)
        svp[0, b, sl] = r["svp"].reshape(1024, 8, 128)
        ss = slice(16 * c, 16 * c + 16)
        ys[ss] = r["ys"].reshape(16, 4, 2048)
        sks[0, ss] = r["sks"].reshape(16, 4, 8, 128)
        svs[0, ss] = r["svs"].reshape(16, 4, 8, 128)
        gss[0, ss] = r["gss"]
        gcs[0, ss] = r["gcs"].reshape(16, 3, 3072)
        if hf == 1:
            gsp[0, b] = r["gsp"]
            gcp[0, b] = r["gcp"]
        else:
            mkp[0, b] = r["mkp"].reshape(256, 4, 128)
            mvp[0, b] = r["mvp"].reshape(256, 4, 128)
    return (yp, ys, skp, svp, sks, svs, gsp, gcp, gss, gcs, mkp, mvp)


def kernel(**inputs):
    inp = {k: np.asarray(v) for k, v in inputs.items()}
    n_phys, maps = _prep_inputs(inp)
    nc = build(n_phys)
    res = run_bass_kernel_spmd(nc, maps, core_ids=list(range(8)))
    return _assemble(res)
```
